# Optimizing a Trainium2 kernel written in Bass

```python
import jax, jax.numpy as jnp
from jax import lax
import numpy as np

D_MODEL = 1024
BATCH = 32
SEQ = 2048
DEPTH = 1

CHUNK = 64
Q_BLOCK = 2 * CHUNK
HEAD_DIM = 64
MIX_WIDTH = D_MODEL
RWKV_WIDTH = MIX_WIDTH // 2
SB_WIDTH = MIX_WIDTH - RWKV_WIDTH
RWKV_HEADS = RWKV_WIDTH // HEAD_DIM
SB_HEADS = SB_WIDTH // HEAD_DIM
DECAY_LORA = 64
AAA_LORA = 64
GATE_LORA = 128
D_FF = 4 * D_MODEL
RMS_EPS = 1e-5
GN_EPS = 64e-5
RWKV_COLS = 3 * RWKV_WIDTH + DECAY_LORA + AAA_LORA + GATE_LORA
SB_COLS = 3 * SB_WIDTH
IN_COLS = RWKV_COLS + SB_COLS

kernel_name = "hybrid_rwkv7_stickbreak_block"


def _rmsnorm(x, g):
    xf = x.astype(jnp.float32)
    y = xf * lax.rsqrt(jnp.mean(xf * xf, axis=-1, keepdims=True) + RMS_EPS)
    return (y * g.astype(jnp.float32)).astype(x.dtype)


def _token_shift(p):
    return jnp.pad(p[:, :-1], ((0, 0), (1, 0), (0, 0)))


def _rwkv7_time_mix(p, mu, w0, w_decay_up, a0, w_aaa_up, w_gate_up, k_k, k_a, r_k, gn_w, gn_b):
    B, T, _ = p.shape
    H, dh = RWKV_HEADS, HEAD_DIM
    p = p + mu * (_token_shift(p) - p)
    W = RWKV_WIDTH
    splits = [W, 2 * W, 3 * W, 3 * W + DECAY_LORA, 3 * W + DECAY_LORA + AAA_LORA]
    r, k, v, xw, xa, xg = jnp.split(p, splits, axis=-1)
    w = -jax.nn.softplus(-(w0 + jnp.tanh(xw) @ w_decay_up)) - 0.5
    a = jax.nn.sigmoid(a0 + xa @ w_aaa_up)
    g = jax.nn.sigmoid(xg) @ w_gate_up

    def heads(t):
        return t.reshape(B, T, H, dh).astype(jnp.float32)

    r, w, k, v, a = heads(r), heads(w), heads(k), heads(v), heads(a)
    kk = k * k_k.reshape(H, dh).astype(jnp.float32)
    kk = kk / jnp.maximum(jnp.sqrt(jnp.sum(kk * kk, axis=-1, keepdims=True)), 1e-12)
    k = k * (1.0 + (a - 1.0) * k_a.reshape(H, dh).astype(jnp.float32))
    decay = jnp.exp(-jnp.exp(w))

    def step(S, inp):
        r_t, d_t, k_t, v_t, a_t, b_t = inp
        sa = jnp.einsum('bhvk,bhk->bhv', S, a_t)
        S = S * d_t[:, :, None, :] + sa[..., None] * b_t[:, :, None, :] + v_t[..., None] * k_t[:, :, None, :]
        return S, jnp.einsum('bhvk,bhk->bhv', S, r_t)

    def seq_first(t):
        return jnp.swapaxes(t, 0, 1)

    S0 = jnp.zeros((B, H, dh, dh), jnp.float32)
    xs = (seq_first(r), seq_first(decay), seq_first(k), seq_first(v), seq_first(-kk), seq_first(kk * a))
    _, y = lax.scan(step, S0, xs)
    y = jnp.swapaxes(y, 0, 1)
    mean = jnp.mean(y, axis=-1, keepdims=True)
    var = jnp.mean(jnp.square(y - mean), axis=-1, keepdims=True)
    y = (y - mean) * lax.rsqrt(var + GN_EPS) * gn_w.reshape(H, dh).astype(jnp.float32) + gn_b.reshape(H, dh).astype(jnp.float32)
    bonus = jnp.sum(r * k * r_k.astype(jnp.float32), axis=-1, keepdims=True) * v
    out = (y + bonus).reshape(B, T, RWKV_WIDTH) * g.astype(jnp.float32)
    return out.astype(p.dtype)


def _stick_breaking_attention(p, sb_gain):
    B, T, _ = p.shape
    H, dh = SB_HEADS, HEAD_DIM
    q, k, v = jnp.split(p, 3, axis=-1)

    def to_heads(t):
        return t.reshape(B, T, H, dh).transpose(0, 2, 1, 3)

    q, k, v = to_heads(q), to_heads(k), to_heads(v)
    scale = HEAD_DIM ** -0.5
    outs = []
    for blk in range(T // Q_BLOCK):
        start = blk * Q_BLOCK
        end = start + Q_BLOCK
        z = jnp.einsum('bhqd,bhkd->bhqk', q[:, :, start:end], k[:, :, :end]).astype(jnp.float32) * scale
        causal = jnp.arange(end)[None, :] < (start + jnp.arange(Q_BLOCK))[:, None]
        log_beta = jax.nn.log_sigmoid(z)
        log_keep = jnp.where(causal, log_beta - z, 0.0)
        log_a = log_beta + lax.cumsum(log_keep, axis=3, reverse=True) - log_keep
        att = jnp.where(causal, jnp.exp(log_a), 0.0)
        outs.append(jnp.einsum('bhqk,bhkd->bhqd', att.astype(v.dtype), v[:, :, :end]))
    o = jnp.concatenate(outs, axis=2).transpose(0, 2, 1, 3).astype(jnp.float32)
    o = o * lax.rsqrt(jnp.mean(o * o, axis=-1, keepdims=True) + RMS_EPS) * sb_gain.reshape(H, dh).astype(jnp.float32)
    return o.reshape(B, T, SB_WIDTH).astype(p.dtype)


def setup_inputs(seed: int = 0) -> dict:
    key = jax.random.key(seed)
    ks = jax.random.split(key, 21)
    n = jax.random.normal
    f32 = jnp.float32
    return {
        "x": n(ks[0], (BATCH, SEQ, D_MODEL), f32),
        "ln1_g": 1.0 + 0.02 * n(ks[1], (DEPTH, D_MODEL), f32),
        "w_in": n(ks[2], (DEPTH, D_MODEL, IN_COLS), f32) * D_MODEL ** -0.5,
        "tok_mu": jax.random.uniform(ks[3], (DEPTH, RWKV_COLS), f32),
        "w0": jax.random.uniform(ks[4], (DEPTH, RWKV_WIDTH), f32, -6.0, -1.0),
        "w_decay_up": n(ks[5], (DEPTH, DECAY_LORA, RWKV_WIDTH), f32) * 0.5 * DECAY_LORA ** -0.5,
        "a0": 0.1 * n(ks[6], (DEPTH, RWKV_WIDTH), f32),
        "w_aaa_up": n(ks[7], (DEPTH, AAA_LORA, RWKV_WIDTH), f32) * 0.5 * AAA_LORA ** -0.5,
        "w_gate_up": n(ks[8], (DEPTH, GATE_LORA, RWKV_WIDTH), f32) * GATE_LORA ** -0.5,
        "k_k": 0.85 + 0.05 * n(ks[9], (DEPTH, RWKV_WIDTH), f32),
        "k_a": 1.0 + 0.05 * n(ks[10], (DEPTH, RWKV_WIDTH), f32),
        "r_k": 0.1 * n(ks[11], (DEPTH, RWKV_HEADS, HEAD_DIM), f32),
        "gn_w": 1.0 + 0.02 * n(ks[12], (DEPTH, RWKV_WIDTH), f32),
        "gn_b": 0.02 * n(ks[13], (DEPTH, RWKV_WIDTH), f32),
        "sb_gain": 1.0 + 0.02 * n(ks[14], (DEPTH, SB_WIDTH), f32),
        "w_out": n(ks[15], (DEPTH, MIX_WIDTH, D_MODEL), f32) * MIX_WIDTH ** -0.5,
        "ln2_g": 1.0 + 0.02 * n(ks[16], (DEPTH, D_MODEL), f32),
        "w_up": n(ks[17], (DEPTH, D_MODEL, D_FF), f32) * D_MODEL ** -0.5,
        "w_down": n(ks[18], (DEPTH, D_FF, D_MODEL), f32) * 0.5 * D_FF ** -0.5,
        "lnf_g": 1.0 + 0.02 * n(ks[19], (D_MODEL,), f32),
    }


def reference(x, ln1_g, w_in, tok_mu, w0, w_decay_up, a0, w_aaa_up, w_gate_up, k_k, k_a, r_k,
              gn_w, gn_b, sb_gain, w_out, ln2_g, w_up, w_down, lnf_g):
    for l in range(DEPTH):
        h = _rmsnorm(x, ln1_g[l])
        p = h @ w_in[l]
        y_rwkv = _rwkv7_time_mix(p[..., :RWKV_COLS], tok_mu[l], w0[l], w_decay_up[l], a0[l],
                                 w_aaa_up[l], w_gate_up[l], k_k[l], k_a[l], r_k[l], gn_w[l], gn_b[l])
        y_sb = _stick_breaking_attention(p[..., RWKV_COLS:], sb_gain[l])
        x = x + jnp.concatenate([y_rwkv, y_sb], axis=-1) @ w_out[l]
        h = _rmsnorm(x, ln2_g[l])
        x = x + jnp.square(jax.nn.relu(h @ w_up[l])) @ w_down[l]
    return _rmsnorm(x, lnf_g)
```

```python
import numpy as np
from contextlib import ExitStack
import concourse.bass as bass
import concourse.mybir as mybir
from concourse.bass_utils import run_bass_kernel_spmd

F32 = mybir.dt.float32
BF16 = mybir.dt.bfloat16
AF = mybir.ActivationFunctionType
ALU = mybir.AluOpType

NCORES = 8
D = 1024
T = 2048
G = 512
NG = T // G
DFF = 4096
C0 = 0.6065306597126334
RMS_EPS = 1e-5
GN_EPS = 64e-5

PC_LN1, PC_LN2, PC_MU, PC_W0, PC_A0, PC_KK, PC_KA, PC_RK, PC_GNW, PC_GNB, PC_SBG = 0, 8, 16, 30, 34, 38, 42, 46, 50, 54, 58
NPC = 62
CS_ID, CS_TRI, CS_ONES, CS_MATT, CS_BLK, CS_UTS, CS_LTS, CS_UTI = range(8)
NCS = 8


class Atom:
    __slots__ = ("w", "r", "x")

    def __init__(self, x=False):
        self.w = {}
        self.r = {}
        self.x = x


class Tl:
    def __init__(self, ap, atoms=None):
        self.ap = ap
        self.atoms = atoms if atoms is not None else [Atom()]

    def __getitem__(self, k):
        return self.ap[k]


class Eng:
    def __init__(self, e, sem, is_pe=False):
        self.e = e
        self.sem = sem
        self.cnt = 0
        self.waited = {}
        self.is_pe = is_pe


def _atoms(lst):
    out = []
    for t in lst:
        if isinstance(t, Tl):
            out.extend(t.atoms)
        elif isinstance(t, Atom):
            out.append(t)
        else:
            out.extend(_atoms(t))
    return out


class KB:
    def __init__(self, nc, es):
        self.nc = nc
        self.es = es
        self.PE = Eng(nc.tensor, es.enter_context(nc.semaphore("s_pe")), True)
        self.ACT = Eng(nc.scalar, es.enter_context(nc.semaphore("s_act")))
        self.DVE = Eng(nc.vector, es.enter_context(nc.semaphore("s_dve")))
        self.POOL = Eng(nc.gpsimd, es.enter_context(nc.semaphore("s_pool")))
        self.SP = Eng(nc.sync, None)
        self.dkeys = []
        self.hist = {}
        self.ninstr = 0

    def _need(self, eng, reads, writes):
        need = {}
        for a in reads:
            for s, v in a.w.items():
                if need.get(s, 0) < v:
                    need[s] = v
        for a in writes:
            for s, v in a.w.items():
                if need.get(s, 0) < v:
                    need[s] = v
            for s, v in a.r.items():
                if need.get(s, 0) < v:
                    need[s] = v
        for s, v in sorted(need.items(), key=lambda kv: -kv[1]):
            if eng.is_pe and s is eng.sem:
                continue
            if eng.waited.get(s, 0) >= v:
                continue
            eng.e.wait_ge(s, v)
            eng.waited[s] = v
            self.ninstr += 1
            snap = self.hist.get((s, v))
            if snap:
                w = eng.waited
                for s2, v2 in snap.items():
                    if w.get(s2, 0) < v2:
                        w[s2] = v2

    def op(self, eng, fn, reads, writes, inc=True):
        reads = _atoms(reads)
        writes = _atoms(writes)
        xr = [a for a in reads if a.x]
        if xr:
            writes = writes + xr
        self._need(eng, reads, writes)
        ins = fn()
        self.ninstr += 1
        if inc:
            eng.cnt += 1
            ins.then_inc(eng.sem, 1)
            val = eng.cnt
            snap = dict(eng.waited)
            if not eng.is_pe:
                snap[eng.sem] = val - 1
            self.hist[(eng.sem, val)] = snap
        else:
            val = eng.cnt + 1
        s = eng.sem
        for a in reads:
            if a.r.get(s, 0) < val:
                a.r[s] = val
        for a in writes:
            if a.w.get(s, 0) < val:
                a.w[s] = val

    def dma(self, out, in_, reads, writes, key, q=None, **kw):
        q = q or self.SP
        reads = _atoms(reads)
        writes = _atoms(writes)
        if not hasattr(key, "dsem"):
            key.dsem = self.es.enter_context(self.nc.semaphore("s_dma%d" % len(self.dkeys)))
            key.dcnt = 0
            self.dkeys.append(key)
        s = key.dsem
        self._need(q, reads, writes)
        if q.waited.get(s, 0) < key.dcnt:
            q.e.wait_ge(s, key.dcnt)
            q.waited[s] = key.dcnt
        key.dcnt += 16
        val = key.dcnt
        q.e.dma_start(out=out, in_=in_, **kw).then_inc(s, 16)
        self.ninstr += 1
        self.hist[(s, val)] = dict(q.waited)
        for a in reads:
            if a.r.get(s, 0) < val:
                a.r[s] = val
        for a in writes:
            if a.w.get(s, 0) < val:
                a.w[s] = val

    def finish(self):
        for k in self.dkeys:
            if self.SP.waited.get(k.dsem, 0) < k.dcnt:
                self.nc.sync.wait_ge(k.dsem, k.dcnt)

    def mm(self, out, lhsT, rhs, reads, writes, start=True, stop=True, inc=True):
        nc = self.nc
        self.op(self.PE, lambda: nc.tensor.matmul(out, lhsT, rhs, start=start, stop=stop), reads, writes, inc)

    def tr(self, out, in_, ident, reads, writes, inc=True):
        nc = self.nc
        self.op(self.PE, lambda: nc.tensor.transpose(out, in_, ident), reads, writes, inc)

    def act(self, out, in_, func, reads, writes, **kw):
        nc = self.nc
        self.op(self.ACT, lambda: nc.scalar.activation(out=out, in_=in_, func=func, **kw), reads, writes)

    def tt(self, eng, out, in0, in1, op, reads, writes):
        self.op(eng, lambda: eng.e.tensor_tensor(out=out, in0=in0, in1=in1, op=op), reads, writes)

    def ts(self, eng, out, in0, s1, s2, op0, op1, reads, writes):
        if s2 is None:
            self.op(eng, lambda: eng.e.tensor_scalar(out=out, in0=in0, scalar1=s1, scalar2=None, op0=op0), reads, writes)
        else:
            self.op(eng, lambda: eng.e.tensor_scalar(out=out, in0=in0, scalar1=s1, scalar2=s2, op0=op0, op1=op1), reads, writes)

    def stt(self, eng, out, in0, scalar, in1, op0, op1, reads, writes):
        self.op(eng, lambda: eng.e.scalar_tensor_tensor(out=out, in0=in0, scalar=scalar, in1=in1, op0=op0, op1=op1), reads, writes)

    def cp(self, eng, out, in_, reads, writes):
        self.op(eng, lambda: eng.e.tensor_copy(out, in_), reads, writes)

    def memset(self, eng, ap, val, writes):
        self.op(eng, lambda: eng.e.memset(ap, val), [], writes)


def _win_chunks():
    ch = [[(1536, 256)]]
    for p in range(4):
        ch.append([(128 * p, 128), (512 + 128 * p, 128), (1024 + 128 * p, 128)])
    ch.append([(1792, 512)])
    ch.append([(2304, 512)])
    ch.append([(2816, 512)])
    return ch


class StopBuild(Exception):
    pass


def build(nseq=4, dbg=False, stop_at=None):
    nc = bass.Bass("TRN2", target_bir_lowering=False)
    NT = nseq * T
    x_d = nc.dram_tensor("x", [NT, D], F32, kind="ExternalInput").ap()
    win_d = nc.dram_tensor("w_in", [D, 3328], F32, kind="ExternalInput").ap()
    wout_d = nc.dram_tensor("w_out", [D, D], F32, kind="ExternalInput").ap()
    wup_d = nc.dram_tensor("w_up", [D, DFF], F32, kind="ExternalInput").ap()
    wdn_d = nc.dram_tensor("w_down", [DFF, D], F32, kind="ExternalInput").ap()
    wlo_d = nc.dram_tensor("wlo", [128, 512], F32, kind="ExternalInput").ap()
    wg_d = nc.dram_tensor("wg", [128, 512], F32, kind="ExternalInput").ap()
    pcol_d = nc.dram_tensor("pcol", [128, NPC], F32, kind="ExternalInput").ap()
    lnf_d = nc.dram_tensor("lnf", [1, D], F32, kind="ExternalInput").ap()
    cst_d = nc.dram_tensor("cst", [128, NCS * 128], F32, kind="ExternalInput").ap()
    out_d = nc.dram_tensor("out", [NT, D], F32, kind="ExternalOutput").ap()
    wsc_d = nc.dram_tensor("wscr", [24, 128, 4096], BF16).ap()
    if dbg:
        dbg_y = nc.dram_tensor("dbg_y", [D, NT], F32, kind="ExternalOutput").ap()
        dbg_x1 = nc.dram_tensor("dbg_x1", [NT, D], F32, kind="ExternalOutput").ap()

    es = ExitStack()
    with es:
        kb = KB(nc, es)
        PE, ACT, DVE, POOL = kb.PE, kb.ACT, kb.DVE, kb.POOL

        def ckpt(name):
            if stop_at is not None and name == stop_at:
                raise StopBuild()
        try:
            _body(nc, es, kb, nseq, dbg, ckpt, locals())
        except StopBuild:
            pass
        kb.finish()
        print("instructions emitted:", kb.ninstr, "pe", PE.cnt, "act", ACT.cnt, "dve", DVE.cnt, "pool", POOL.cnt)
    return nc


def _body(nc, es, kb, nseq, dbg, ckpt, env):
    globals_ = env
    x_d, win_d, wout_d, wup_d, wdn_d, wlo_d, wg_d, pcol_d, lnf_d, cst_d, out_d, wsc_d = (env[k] for k in (
        "x_d", "win_d", "wout_d", "wup_d", "wdn_d", "wlo_d", "wg_d", "pcol_d", "lnf_d", "cst_d", "out_d", "wsc_d"))
    dbg_y = env.get("dbg_y")
    dbg_x1 = env.get("dbg_x1")
    PE, ACT, DVE, POOL = kb.PE, kb.ACT, kb.DVE, kb.POOL
    if True:

        def sb(name, shape, dt, natoms=1):
            h = es.enter_context(nc.sbuf_tensor("sb_" + name, shape, dt))
            return h

        cst = Tl(sb("cstb", [128, NCS * 128], BF16)[:])
        pcol = Tl(sb("pcol", [128, NPC], F32)[:])
        lnf = Tl(sb("lnfb", [128, D], F32)[:])
        wlo = Tl(sb("wlo", [128, 512], BF16)[:])
        wgt = Tl(sb("wgt", [128, 512], BF16)[:])
        wout = Tl(sb("wout", [128, 8, D], BF16)[:])
        NWB = 2
        wbuf = [Tl(sb("wbuf%d" % i, [128, 4096], BF16)[:]) for i in range(NWB)]
        xb = [Tl(sb("xb%d" % i, [128, D], F32)[:]) for i in range(2)]
        hn = [Tl(sb("hn%d" % i, [128, D], BF16)[:]) for i in range(1)] * 2
        stat = [Tl(sb("stat%d" % i, [128, 4], F32)[:]) for i in range(2)]
        hT_h = sb("hT", [128, 8, G], BF16)
        hT = Tl(hT_h[:], [Atom() for _ in range(4)])
        x1 = [Tl(sb("x1_%d" % i, [128, D], F32)[:]) for i in range(4)]
        yT = [Tl(sb("yT%d" % k, [128, G], BF16)[:]) for k in range(8)]
        kT = [Tl(sb("kT%d" % p, [128, T], BF16)[:], [Atom() for _ in range(NG)]) for p in range(4)]
        vtok = [Tl(sb("vtok%d" % i, [128, 512], BF16)[:]) for i in range(16)]
        qs = [Tl(sb("qs%d" % p, [128, G], BF16)[:]) for p in range(4)]
        qn = [Tl(sb("qn%d" % p, [128, G], BF16)[:]) for p in range(4)]
        NWK = 18
        wk_h = [sb("wk%d" % i, [128, 512], F32) for i in range(NWK)]
        wk = [Tl(h[:]) for h in wk_h]
        wkb = [Tl(h[:].bitcast(BF16), wk[i].atoms) for i, h in enumerate(wk_h)]
        rkv = [[Tl(sb("rkv%d_%d" % (b, j), [128, 512], F32)[:]) for j in range(3)] for b in range(1)]
        pst = [Tl(sb("pst%d" % i, [128, 513], F32)[:]) for i in range(1)] * 2
        carry = Tl(sb("carry", [128, 14], F32)[:])
        txa = Tl(sb("txa", [128, 512], BF16)[:])
        sgx = Tl(sb("sgx", [128, 512], BF16)[:])
        pb = [[Tl(sb("pb%d_%d" % (b, j), [128, 512], BF16)[:]) for j in range(5)] for b in range(1)]
        tmpb = Tl(sb("tmpb", [128, 512], BF16)[:])
        zt = Tl(sb("zt", [128, 64], BF16)[:])
        NBD = 36
        bdt = [Tl(sb("bd%d" % i, [128, 128], BF16)[:]) for i in range(NBD)]
        bdl = [[Tl(sb("bdl%d_%d" % (j, i), [128, 128], BF16)[:]) for i in range(9)] for j in range(8)]
        PCS = [Tl(sb("pcs%d" % i, [128, 8], F32)[:]) for i in range(2)]
        H32 = [Tl(sb("H32_%d" % p, [128, 128], F32)[:]) for p in range(4)]
        HP = [Tl(sb("HP_%d" % p, [128, 128], F32)[:]) for p in range(2)]
        Hb = [Tl(sb("Hb_%d" % p, [128, 128], BF16)[:]) for p in range(4)]
        bank_h = [es.enter_context(nc.psum_tensor("bank%d" % i, [128, 512], F32)) for i in range(8)]
        bank_atoms = [[Atom(True)] for _ in range(8)]
        bank = [Tl(bank_h[i][:], bank_atoms[i]) for i in range(8)]
        small = []
        for q in range(4):
            for b in range(4, 8):
                t_ = Tl(bank_h[b][:, q * 128:(q + 1) * 128], [bank_atoms[b][0]])
                t_.bfv = bank_h[b][:].bitcast(BF16)[:, q * 256:q * 256 + 128]
                small.append(t_)
        st = {"bd": 0, "small": 0, "big": 0}

        def nbd():
            t = bdt[st["bd"] % NBD]
            st["bd"] += 1
            return t

        def nsmall():
            t = small[st["small"] % len(small)]
            st["small"] += 1
            return t

        def nbig():
            t = bank[st["big"] % 4]
            st["big"] += 1
            return t

        def C(i):
            return cst[:, i * 128:(i + 1) * 128]

        def pc(col):
            return pcol[:, col:col + 1]

        kb.dma(wk[14][:, :], cst_d[:, 0:512], [], [wk[14]], wk[14])
        kb.dma(wk[15][:, :], cst_d[:, 512:1024], [], [wk[15]], wk[15])
        kb.memset(POOL, zt[:, :], 0.0, [zt])
        kb.dma(pcol[:, :], pcol_d[:, :], [], [pcol], pcol)
        kb.dma(lnf[:, :], lnf_d[0:1, :].partition_broadcast(128), [], [lnf], lnf)
        kb.cp(DVE, cst[:, 0:512], wk[14][:, :], [wk[14]], [cst])
        kb.cp(DVE, cst[:, 512:1024], wk[15][:, :], [wk[15]], [cst])
        kb.dma(wk[0][:, :], wlo_d[:, :], [], [wk[0]], wk[0])
        kb.dma(wk[1][:, :], wg_d[:, :], [], [wk[1]], wk[1])
        kb.cp(DVE, wlo[:, :], wk[0][:, :], [wk[0]], [wlo])
        kb.cp(DVE, wgt[:, :], wk[1][:, :], [wk[1]], [wgt])

        ckpt('consts')
        scr = [Tl(wsc_d[i], [Atom()]) for i in range(24)]
        cast_rr = [0]
        cast_engs = [DVE, POOL, ACT]

        def cast(out_ap, in_tl, in_ap, scale_col, out_tl):
            e = cast_engs[cast_rr[0] % 3]
            cast_rr[0] += 1
            if scale_col is None:
                if e is ACT:
                    kb.act(out_ap, in_ap, AF.Copy, [in_tl], [out_tl])
                else:
                    kb.cp(e, out_ap, in_ap, [in_tl], [out_tl])
            else:
                if e is ACT:
                    kb.act(out_ap, in_ap, AF.Copy, [in_tl, pcol], [out_tl], scale=pc(scale_col))
                else:
                    kb.ts(e, out_ap, in_ap, pc(scale_col), None, ALU.mult, None, [in_tl, pcol], [out_tl])

        stg_rr = [0]

        def stage():
            t = wk[2 + (stg_rr[0] % 12)]
            t.q = kb.SP
            stg_rr[0] += 1
            return t

        wchunks = _win_chunks()
        wb_rr = [0]
        big_stages = x1 + xb
        bst = [0]

        def stage4():
            t = big_stages[bst[0] % len(big_stages)]
            bst[0] += 1
            return t

        for ci, ch in enumerate(wchunks):
            ncols = sum(n for _, n in ch)
            wb = wbuf[wb_rr[0] % NWB]
            wb_rr[0] += 1
            for k0 in range(0, 8, 2):
                s = stage4()
                sv = s[:, 0:2 * ncols].rearrange("p (k c) -> p k c", k=2)
                off = 0
                for (c0, n) in ch:
                    kb.dma(sv[:, :, off:off + n],
                           win_d[k0 * 128:(k0 + 2) * 128, c0:c0 + n].rearrange("(k p) c -> p k c", k=2), [], [s], s)
                    off += n
                for kk in range(2):
                    cast(wb[:, (k0 + kk) * ncols:(k0 + kk + 1) * ncols], s, s[:, kk * ncols:(kk + 1) * ncols],
                         PC_LN1 + k0 + kk, wb)
            kb.dma(wsc_d[ci][:, 0:8 * ncols], wb[:, 0:8 * ncols], [wb], [scr[ci]], wb)
        for c in range(8):
            wb = wbuf[wb_rr[0] % NWB]
            wb_rr[0] += 1
            for k0 in range(0, 8, 2):
                s = stage4()
                kb.dma(s[:, :].rearrange("p (k c) -> p k c", k=2),
                       wup_d[k0 * 128:(k0 + 2) * 128, c * 512:(c + 1) * 512].rearrange("(k p) c -> p k c", k=2), [], [s], s)
                for kk in range(2):
                    cast(wb[:, (k0 + kk) * 512:(k0 + kk + 1) * 512], s, s[:, kk * 512:(kk + 1) * 512], PC_LN2 + k0 + kk, wb)
            kb.dma(wsc_d[8 + c][:, :], wb[:, :], [wb], [scr[8 + c]], wb)
        for c in range(8):
            wb = wbuf[wb_rr[0] % NWB]
            wb_rr[0] += 1
            for fi in range(4):
                s = stage4()
                r0 = c * 512 + fi * 128
                kb.dma(s[:, :], wdn_d[r0:r0 + 128, :], [], [s], s)
                cast(wb[:, fi * 1024:(fi + 1) * 1024], s, s[:, :], None, wb)
            kb.dma(wsc_d[16 + c][:, :], wb[:, :], [wb], [scr[16 + c]], wb)
        for k in range(8):
            s = stage4()
            kb.dma(s[:, :], wout_d[k * 128:(k + 1) * 128, :], [], [s], s)
            cast(wout[:, k, :], s, s[:, :], None, wout)

        ckpt('casts')
        stream = []
        for _ in range(nseq * NG):
            for ci, ch in enumerate(wchunks):
                stream.append((ci, 8 * sum(n for _, n in ch)))
            for c in range(8):
                stream.append((8 + c, 4096))
            for c in range(8):
                stream.append((16 + c, 4096))
        sst = {"issued": 0, "used": 0}

        def wprefetch():
            while sst["issued"] < len(stream) and sst["issued"] < sst["used"] + NWB:
                i = sst["issued"]
                ci, n = stream[i]
                wb = wbuf[(wb_rr[0] + i) % NWB]
                kb.dma(wb[:, 0:n], wsc_d[ci][:, 0:n], [scr[ci]], [wb], wb)
                sst["issued"] += 1

        def wnext():
            wprefetch()
            i = sst["used"]
            wb = wbuf[(wb_rr[0] + i) % NWB]
            sst["used"] += 1
            return wb

        def wdone():
            wprefetch()

        def ln_T(src, i, sidx):
            h = hn[sidx % 2]
            s = stat[sidx % 2]
            kb.memset(POOL, s[:, 0:1], 0.0, [s])
            kb.act(h[:, :], src[:, :], AF.Square, [src, s], [h, s], accum_out=s[:, 0:1])
            kb.act(s[:, 1:2], s[:, 0:1], AF.Ln, [s], [s], scale=1.0 / D, bias=RMS_EPS)
            kb.act(s[:, 2:3], s[:, 1:2], AF.Exp, [s], [s], scale=-0.5)
            kb.ts(DVE, h[:, :], src[:, :], s[:, 2:3], None, ALU.mult, None, [src, s], [h])
            bk = nbig()
            bv = bk.ap.bitcast(BF16)
            for k in range(8):
                kb.tr(bv[:, k * 128:(k + 1) * 128], h[:, k * 128:(k + 1) * 128], C(CS_ID), [h, cst], [bk], inc=(k == 7))
            kb.cp(DVE, hT_h[:, :, i * 128:(i + 1) * 128],
                  bv[:, :].rearrange("p (k c) -> p k c", k=8), [bk], [hT.atoms[i]])
            return s

        def rstd_from_ss(ps_ap, ps_tl, out_tl, scale, eps):
            kb.act(out_tl[:, :], ps_ap, AF.Ln, [ps_tl], [out_tl], scale=scale, bias=eps)
            kb.act(out_tl[:, :], out_tl[:, :], AF.Exp, [out_tl], [out_tl], scale=-0.5)

        for seq in range(nseq):
            kb.memset(POOL, carry[:, :], 0.0, [carry])
            for p in range(4):
                kb.memset(POOL, H32[p][:, :], 0.0, [H32[p]])
                kb.memset(POOL, Hb[p][:, :], 0.0, [Hb[p]])
            for g in range(NG):
                tok0 = seq * T + g * G
                for i in range(4):
                    xt = xb[i % 2]
                    kb.dma(xt[:, :], x_d[tok0 + i * 128: tok0 + (i + 1) * 128, :], [], [xt], xt)
                    ln_T(xt, i, i)

                ckpt('ln1')
                def inproj_fm(wb, col0, ncols_chunk):
                    bk = nbig()
                    for k in range(8):
                        kb.mm(bk[:, :], wb[:, k * ncols_chunk + col0: k * ncols_chunk + col0 + 128], hT_h[:, k, :],
                              [wb, hT], [bk], start=(k == 0), stop=(k == 7), inc=(k == 7))
                    return bk

                def shiftmix(bk, ct, dst_ap, dst_tl):
                    P = pst[ct % 2]
                    kb.act(P[:, 1:513], bk[:, :], AF.Copy, [bk], [P])
                    kb.cp(POOL, P[:, 0:1], carry[:, ct:ct + 1], [carry], [P])
                    d = wk[17]
                    kb.tt(DVE, d[:, :], P[:, 0:512], P[:, 1:513], ALU.subtract, [P], [d])
                    kb.stt(DVE, dst_ap, d[:, :], pc(PC_MU + ct), P[:, 1:513], ALU.mult, ALU.add, [d, P, pcol], [dst_tl])
                    kb.cp(POOL, carry[:, ct:ct + 1], P[:, 512:513], [P], [carry])

                wb = wnext()
                bk = inproj_fm(wb, 0, 256)
                t12 = wk[16]
                shiftmix(bk, 12, t12[:, :], t12)
                kb.act(txa[0:64, :], t12[0:64, :], AF.Tanh, [t12], [txa])
                kb.cp(POOL, txa[64:128, :], t12[64:128, :], [t12], [txa])
                bk = inproj_fm(wb, 128, 256)
                wdone()
                t13 = wk[16]
                shiftmix(bk, 13, t13[:, :], t13)
                kb.act(sgx[:, :], t13[:, :], AF.Sigmoid, [t13], [sgx])

                ckpt('chunk0')
                PP = {}

                def prep_gen(p):
                        wb = wnext()
                        R_, K_, V_ = rkv[0]
                        for j, dst in enumerate((R_, K_, V_)):
                            bk = inproj_fm(wb, 128 * j, 384)
                            shiftmix(bk, 4 * j + p, dst[:, :], dst)
                            yield
                        wdone()
                        Ab, Rb, Bb, Kb_, Vb = pb[0]
                        cols = slice(128 * p, 128 * (p + 1))
                        b1 = nbig()
                        kb.mm(b1[:, :], wlo[0:64, cols], txa[0:64, :], [wlo, txa], [b1])
                        sgm = wk[0]
                        kb.act(sgm[:, :], b1[:, :], AF.Sigmoid, [b1, pcol], [sgm], bias=pc(PC_W0 + p))
                        b2 = nbig()
                        kb.mm(b2[:, :], wlo[64:128, cols], txa[64:128, :], [wlo, txa], [b2])
                        lr = wk[1]
                        kb.act(lr[:, :], b2[:, :], AF.Sigmoid, [b2, pcol], [lr], bias=pc(PC_A0 + p))
                        yield
                        cs = wk[2]
                        for j in range(8):
                            sl = slice(64 * j, 64 * j + 64)
                            kb.op(DVE, lambda sl=sl: nc.vector.tensor_tensor_scan(out=cs[:, sl], data0=sgm[:, sl], data1=sgm[:, sl],
                                                                                      initial=0.0, op0=ALU.add, op1=ALU.bypass),
                                  [sgm], [cs])
                        yield
                        kkr = wk[3]
                        kb.ts(POOL, kkr[:, :], K_[:, :], pc(PC_KK + p), None, ALU.mult, None, [K_, pcol], [kkr])
                        kb.act(tmpb[:, :], kkr[:, :], AF.Square, [kkr], [tmpb])
                        b4 = nbig()
                        kb.mm(b4[:, :], C(CS_BLK), tmpb[:, :], [cst, tmpb], [b4])
                        rn = wk[4]
                        kb.ts(DVE, rn[:, :], b4[:, :], 1e-24, None, ALU.max, None, [b4], [rn])
                        kb.act(rn[:, :], rn[:, :], AF.Ln, [rn], [rn])
                        kb.act(rn[:, :], rn[:, :], AF.Exp, [rn], [rn], scale=-0.5)
                        kb.tt(POOL, kkr[:, :], kkr[:, :], rn[:, :], ALU.mult, [kkr, rn], [kkr])
                        yield
                        Ep, Em, Epv = wk[5], wk[6], wk[7]
                        kb.act(Ep[:, :], cs[:, :], AF.Exp, [cs], [Ep], scale=-C0)
                        kb.act(Em[:, :], cs[:, :], AF.Exp, [cs], [Em], scale=C0)
                        kb.tt(POOL, Epv[:, :], cs[:, :], sgm[:, :], ALU.subtract, [cs, sgm], [Epv])
                        kb.act(Epv[:, :], Epv[:, :], AF.Exp, [Epv], [Epv], scale=-C0)
                        yield
                        kb.stt(DVE, Ab[:, :], kkr[:, :], -1.0, Epv[:, :], ALU.mult, ALU.mult, [kkr, Epv], [Ab])
                        kb.tt(POOL, Rb[:, :], R_[:, :], Ep[:, :], ALU.mult, [R_, Ep], [Rb])
                        bv_ = wk[8]
                        kb.tt(DVE, bv_[:, :], kkr[:, :], lr[:, :], ALU.mult, [kkr, lr], [bv_])
                        kb.tt(POOL, Bb[:, :], bv_[:, :], Em[:, :], ALU.mult, [bv_, Em], [Bb])
                        kb.ts(DVE, lr[:, :], lr[:, :], -1.0, pc(PC_KA + p), ALU.add, ALU.mult, [lr, pcol], [lr])
                        yield
                        km = wk[9]
                        kb.stt(DVE, km[:, :], lr[:, :], 1.0, K_[:, :], ALU.add, ALU.mult, [lr, K_], [km])
                        kb.tt(POOL, Kb_[:, :], km[:, :], Em[:, :], ALU.mult, [km, Em], [Kb_])
                        kb.act(Vb[:, :], V_[:, :], AF.Copy, [V_], [Vb])
                        kb.stt(DVE, tmpb[:, :], R_[:, :], pc(PC_RK + p), km[:, :], ALU.mult, ALU.mult, [R_, km, pcol], [tmpb])
                        b5 = nbig()
                        kb.mm(b5[:, :], C(CS_BLK), tmpb[:, :], [cst, tmpb], [b5])
                        bonus = wk[10] if p % 2 == 0 else wk[15]
                        kb.tt(DVE, bonus[:, :], V_[:, :], b5[:, :], ALU.mult, [V_, b5], [bonus])
                        pcs = PCS[p % 2]
                        kb.cp(POOL, pcs[:, :], Ep[:, 63:512:64], [Ep], [pcs])
                        PP[p] = dict(bonus=bonus, pcs=pcs)
                        yield

                for _ in prep_gen(0):
                    pass
                CTX = [[dict() for _ in range(8)] for _ in range(4)]

                def seqgen(*gens):
                    for g_ in gens:
                        yield from g_

                for p in range(4):
                    bdm = C(CS_BLK).rearrange("p (a b) -> p a b", a=2)
                    pcs = PP[p]["pcs"]
                    bonus = PP[p]["bonus"]
                    Ab, Rb, Bb, Kb_, Vb = pb[0]
                    yraw = wk[12]
                    cols = slice(128 * p, 128 * (p + 1))
                    ctxs = CTX[p]

                    def pre_gen(jl, pp):
                        ctxs = CTX[pp]
                        def mkbd(src, t, sl):
                            kb.tt(POOL, t[:, :].rearrange("p (a b) -> p a b", a=2),
                                  src[:, sl].unsqueeze(1).broadcast_to([128, 2, 64]), bdm, ALU.mult, [src, cst], [t])
                            return t
                        for j in jl:
                            c = ctxs[j]
                            sl = slice(64 * j, 64 * j + 64)
                            LL = bdl[j]
                            c["AT"] = mkbd(Ab, LL[0], sl)
                            c["RT"] = mkbd(Rb, LL[1], sl)
                            c["BT"] = mkbd(Bb, nbd(), sl)
                            c["KT"] = mkbd(Kb_, nbd(), sl)
                            c["VT"] = mkbd(Vb, nbd(), sl)
                        yield
                        for j in jl:
                            c = ctxs[j]
                            LL = bdl[j]
                            for nm, src, dst in (("Btok", c["BT"], LL[2]), ("Ktok", c["KT"], LL[3]), ("Vtok", c["VT"], LL[4])):
                                ps = nsmall()
                                kb.tr(ps.bfv, src[:, :], C(CS_ID), [src, cst], [ps])
                                kb.cp(DVE, dst[:, :], ps.bfv, [ps], [dst])
                                c[nm] = dst
                        yield

                        def gram(l, r, mask, t):
                            ps = nsmall()
                            kb.mm(ps[:, :], l[:, :], r[:, :], [l, r], [ps])
                            kb.tt(DVE, t[:, :], ps[:, :], C(mask), ALU.mult, [ps, cst], [t])
                            return t
                        for j in jl:
                            c = ctxs[j]
                            LL = bdl[j]
                            c["M"] = gram(c["BT"], c["AT"], CS_UTS, nbd())
                            c["N"] = gram(c["AT"], c["BT"], CS_LTS, nbd())
                            c["Aak"] = gram(c["KT"], c["AT"], CS_UTS, LL[5])
                        yield
                        for j in jl:
                            c = ctxs[j]
                            LL = bdl[j]
                            c["Arb"] = gram(c["BT"], c["RT"], CS_UTI, LL[6])
                            c["Ark"] = gram(c["KT"], c["RT"], CS_UTI, LL[7])
                            X = nbd()
                            kb.tt(POOL, X[:, :], c["M"][:, :], C(CS_ID), ALU.add, [c["M"], cst], [X])
                            c["X"] = X
                        yield
                        for lev in range(1, 6):
                            for j in jl:
                                c = ctxs[j]
                                psn = nsmall()
                                kb.mm(psn[:, :], c["M"][:, :], c["N"][:, :], [c["M"], c["N"]], [psn])
                                Nn = nbd()
                                kb.act(Nn[:, :], psn[:, :], AF.Copy, [psn], [Nn])
                                c["Nn"] = Nn
                            yield
                            if lev < 5:
                                for j in jl:
                                    c = ctxs[j]
                                    psm = nsmall()
                                    kb.mm(psm[:, :], c["N"][:, :], c["M"][:, :], [c["M"], c["N"]], [psm])
                                    Mn = nbd()
                                    kb.act(Mn[:, :], psm[:, :], AF.Copy, [psm], [Mn])
                                    c["Mn"] = Mn
                                yield
                            for j in jl:
                                c = ctxs[j]
                                psx = nsmall()
                                kb.mm(psx[:, :], c["Nn"][:, :], c["X"][:, :], [c["Nn"], c["X"]], [psx])
                                Xn = nbd() if lev < 5 else bdl[j][8]
                                kb.tt(DVE, Xn[:, :], psx[:, :], c["X"][:, :], ALU.add, [psx, c["X"]], [Xn])
                                c["X"] = Xn
                                c["N"] = c["Nn"]
                                c["M"] = c.get("Mn") if lev < 5 else None
                            yield

                    def chain_gen(jl):
                        for j in jl:
                            c = ctxs[j]
                            sl = slice(64 * j, 64 * j + 64)
                            hp = HP[j % 2]
                            kb.act(hp[:, :], H32[p][:, :], AF.Copy, [H32[p], pcs], [hp], scale=pcs[:, j:j + 1])
                            psw = nsmall()
                            kb.mm(psw[:, :], c["AT"][:, :], Hb[p][:, :], [c["AT"], Hb[p]], [psw], start=True, stop=False, inc=False)
                            kb.mm(psw[:, :], c["Aak"][:, :], c["Vtok"][:, :], [c["Aak"], c["Vtok"]], [psw], start=False, stop=True)
                            Wb = nbd()
                            kb.act(Wb[:, :], psw[:, :], AF.Copy, [psw], [Wb])
                            yield
                            psu = nsmall()
                            kb.mm(psu[:, :], c["X"][:, :], Wb[:, :], [c["X"], Wb], [psu])
                            Ub = nbd()
                            kb.cp(DVE, Ub[:, :], psu[:, :], [psu], [Ub])
                            yield
                            psh = nsmall()
                            kb.mm(psh[:, :], c["Btok"][:, :], Ub[:, :], [c["Btok"], Ub], [psh], start=True, stop=False, inc=False)
                            kb.mm(psh[:, :], c["Ktok"][:, :], c["Vtok"][:, :], [c["Ktok"], c["Vtok"]], [psh], start=False, stop=True)
                            psy = nsmall()
                            kb.mm(psy[:, :], Hb[p][:, :], c["RT"][:, :], [Hb[p], c["RT"]], [psy], start=True, stop=False, inc=False)
                            kb.mm(psy[:, :], Ub[:, :], c["Arb"][:, :], [Ub, c["Arb"]], [psy], start=False, stop=False, inc=False)
                            kb.mm(psy[:, :], c["Vtok"][:, :], c["Ark"][:, :], [c["Vtok"], c["Ark"]], [psy], start=False, stop=True)
                            kb.stt(DVE, Hb[p][:, :], psh[:, :], pcs[:, j:j + 1], hp[:, :], ALU.mult, ALU.add,
                                   [psh, pcs, hp], [Hb[p]])
                            kb.stt(DVE, H32[p][:, :], psh[:, :], pcs[:, j:j + 1], hp[:, :], ALU.mult, ALU.add,
                                   [psh, pcs, hp], [H32[p]])
                            yield
                            kb.act(yraw[0:64, sl], psy[0:64, 0:64], AF.Copy, [psy], [yraw])
                            kb.act(yraw[64:128, sl], psy[64:128, 64:128], AF.Copy, [psy], [yraw])
                            yield

                    def run_il(*gens):
                        gens = list(gens)
                        while gens:
                            for g_ in list(gens):
                                try:
                                    next(g_)
                                except StopIteration:
                                    gens.remove(g_)

                    if p == 0:
                        run_il(pre_gen([0, 1, 2, 3], 0))
                    run_il(chain_gen([0, 1, 2, 3]), pre_gen([4, 5, 6, 7], p))
                    if p < 3:
                        run_il(chain_gen([4, 5, 6, 7]), seqgen(prep_gen(p + 1), pre_gen([0, 1, 2, 3], p + 1)))
                    else:
                        run_il(chain_gen([4, 5, 6, 7]))
                    ckpt('chunks')
                    kb.act(tmpb[:, :], yraw[:, :], AF.Copy, [yraw], [tmpb])
                    b6 = nbig()
                    kb.mm(b6[:, :], C(CS_BLK), tmpb[:, :], [cst, tmpb], [b6])
                    yc = wk[13]
                    kb.stt(DVE, yc[:, :], b6[:, :], -1.0 / 64, yraw[:, :], ALU.mult, ALU.add, [b6, yraw], [yc])
                    kb.act(tmpb[:, :], yc[:, :], AF.Square, [yc], [tmpb])
                    b7 = nbig()
                    kb.mm(b7[:, :], C(CS_BLK), tmpb[:, :], [cst, tmpb], [b7])
                    rs = wk[14]
                    rstd_from_ss(b7[:, :], b7, rs, 1.0 / 64, GN_EPS)
                    kb.tt(DVE, yc[:, :], yc[:, :], rs[:, :], ALU.mult, [yc, rs], [yc])
                    kb.ts(POOL, yc[:, :], yc[:, :], pc(PC_GNW + p), pc(PC_GNB + p), ALU.mult, ALU.add, [yc, pcol], [yc])
                    kb.tt(DVE, yc[:, :], yc[:, :], bonus[:, :], ALU.add, [yc, bonus], [yc])
                    b3 = nbig()
                    kb.mm(b3[:, :], wgt[:, cols], sgx[:, :], [wgt, sgx], [b3])
                    kb.tt(DVE, yT[p][:, :], yc[:, :], b3[:, :], ALU.mult, [yc, b3], [yT[p]])

                ckpt('rwkv')
                wb = wnext()
                for p in range(4):
                    bk = inproj_fm(wb, 128 * p, 512)
                    kb.act(qs[p][:, :], bk[:, :], AF.Copy, [bk], [qs[p]], scale=0.125)
                wdone()
                wb = wnext()
                for p in range(4):
                    bk = inproj_fm(wb, 128 * p, 512)
                    kb.act(kT[p][:, g * G:(g + 1) * G], bk[:, :], AF.Copy, [bk], [kT[p].atoms[g]])
                wdone()
                wb = wnext()
                for i in range(4):
                    bk = nbig()
                    for k in range(8):
                        kb.mm(bk[:, :], hT_h[:, k, i * 128:(i + 1) * 128], wb[:, k * 512:(k + 1) * 512],
                              [wb, hT.atoms[i]], [bk], start=(k == 0), stop=(k == 7), inc=(k == 7))
                    kb.cp(DVE, vtok[4 * g + i][:, :], bk[:, :], [bk], [vtok[4 * g + i]])
                wdone()

                ckpt('qkv')
                def bv(i):
                    return Tl(wkb[i][:, 0:512], wk[i].atoms)
                E32 = [[wk[0], wk[14]], [wk[1], wk[15]]]
                XC = [wk[16], wk[17]]
                SPb = [[bv(2), bv(3)], [bv(4), bv(5)]]
                ATb = [[bv(6), bv(7)], [bv(8), bv(9)]]
                Ssum = [bv(10), bv(11)]
                osq = bv(12)
                rst = wk[13]
                nkb = 4 * g + 4
                kbs = list(range(nkb - 1, -1, -1))
                for p in range(4):
                    Bo = bank[0]
                    Bz = [[bank[1], bank[6]], [bank[2], bank[7]]]
                    Bc = [bank[3], bank[4]]

                    def geom(idx):
                        kbk = kbs[idx]
                        dq = kbk - 4 * g
                        q0 = 128 * max(dq, 0)
                        return kbk, dq, q0, slice(q0, 512), slice(kbk * 128, (kbk + 1) * 128), kT[p].atoms[kbk // 4]

                    def zmm(idx, hh):
                        kbk, dq, q0, cs_, kblk, katoms = geom(idx)
                        rows = slice(64 * hh, 64 * hh + 64)
                        bz = Bz[hh][idx % 2]
                        kb.mm(bz[:, cs_], kT[p][rows, kblk], qs[p][rows, cs_], [katoms, qs[p]], [bz])

                    def expA(idx, hh):
                        kbk, dq, q0, cs_, kblk, katoms = geom(idx)
                        bz = Bz[hh][idx % 2]
                        e = E32[hh][idx % 2]
                        kb.act(e[:, cs_], bz[:, cs_], AF.Exp, [bz], [e])

                    def lnA(idx, hh):
                        kbk, dq, q0, cs_, kblk, katoms = geom(idx)
                        e = E32[hh][idx % 2]
                        sp = SPb[hh][idx % 2]
                        kb.act(sp[:, cs_], e[:, cs_], AF.Ln, [e], [sp], bias=1.0)
                        if dq >= 0:
                            kb.tt(POOL, sp[:, q0:q0 + 128], sp[:, q0:q0 + 128], C(CS_MATT), ALU.mult, [sp, cst], [sp])

                    def actA(idx):
                        expA(idx, 0)
                        expA(idx, 1)
                        lnA(idx, 0)
                        lnA(idx, 1)

                    def stB1(idx, hh):
                        kbk, dq, q0, cs_, kblk, katoms = geom(idx)
                        rows = slice(64 * hh, 64 * hh + 64)
                        sp = SPb[hh][idx % 2]
                        kb.mm(Bc[hh][:, cs_], C(CS_TRI), sp[:, cs_], [cst, sp], [Bc[hh]], start=True, stop=(idx == 0), inc=(idx == 0))
                        if idx > 0:
                            kb.mm(Bc[hh][:, cs_], C(CS_ONES), Ssum[hh][:, cs_], [cst, Ssum[hh]], [Bc[hh]], start=False, stop=True)
                        if kbk > 0:
                            kb.tt(POOL, Ssum[hh][:, cs_], Ssum[hh][:, cs_], sp[:, cs_], ALU.add, [Ssum[hh], sp], [Ssum[hh]])

                    def stB2a(idx, hh):
                        kbk, dq, q0, cs_, kblk, katoms = geom(idx)
                        at = ATb[hh][idx % 2]
                        xc = XC[hh]
                        kb.act(xc[:, cs_], Bc[hh][:, cs_], AF.Exp, [Bc[hh]], [xc], scale=-1.0)
                        e = E32[hh][idx % 2]
                        kb.tt(DVE, at[:, cs_], e[:, cs_], xc[:, cs_], ALU.mult, [e, xc], [at])
                        if dq >= 0:
                            kb.tt(POOL, at[:, q0:q0 + 128], at[:, q0:q0 + 128], C(CS_MATT), ALU.mult, [at, cst], [at])

                    def stB2b(idx, hh):
                        kbk, dq, q0, cs_, kblk, katoms = geom(idx)
                        rows = slice(64 * hh, 64 * hh + 64)
                        at = ATb[hh][idx % 2]
                        kb.mm(Bo[rows, cs_], vtok[kbk][:, 128 * p + 64 * hh: 128 * p + 64 * hh + 64], at[:, cs_],
                              [vtok[kbk], at], [Bo], start=False, stop=(kbk == 0), inc=True)

                    for hh in range(2):
                        rows = slice(64 * hh, 64 * hh + 64)
                        kb.memset(POOL, Ssum[hh][:, :], 0.0, [Ssum[hh]])
                        kb.mm(Bo[rows, :], zt[:, :], qs[p][:, :], [zt, qs[p]], [Bo], start=True, stop=False, inc=False)
                    zmm(0, 0)
                    zmm(0, 1)
                    if nkb > 1:
                        zmm(1, 0)
                        zmm(1, 1)
                    actA(0)
                    for idx in range(nkb):
                        if idx + 2 < nkb:
                            zmm(idx + 2, 0)
                            zmm(idx + 2, 1)
                        if idx + 1 < nkb:
                            actA(idx + 1)
                        stB1(idx, 0)
                        stB1(idx, 1)
                        if idx > 0:
                            stB2b(idx - 1, 0)
                            stB2b(idx - 1, 1)
                        stB2a(idx, 0)
                        stB2a(idx, 1)
                    stB2b(nkb - 1, 0)
                    stB2b(nkb - 1, 1)
                    kb.act(osq[:, :], Bo[:, :], AF.Square, [Bo], [osq])
                    Bs = bank[5]
                    kb.mm(Bs[:, :], C(CS_BLK), osq[:, :], [cst, osq], [Bs])
                    rstd_from_ss(Bs[:, :], Bs, rst, 1.0 / 64, RMS_EPS)
                    kb.stt(DVE, yT[4 + p][:, :], Bo[:, :], pc(PC_SBG + p), rst[:, :], ALU.mult, ALU.mult, [Bo, rst, pcol], [yT[4 + p]])

                ckpt('attn')
                if dbg:
                    for k in range(8):
                        t = wk[14 + (k % 2)]
                        kb.cp(DVE, t[:, :], yT[k][:, :], [yT[k]], [t])
                        kb.dma(dbg_y[k * 128:(k + 1) * 128, tok0:tok0 + G], t[:, :], [t], [], t)

                ckpt('dbgy')
                for i in range(4):
                    xt = xb[i % 2]
                    kb.dma(xt[:, :], x_d[tok0 + i * 128: tok0 + (i + 1) * 128, :], [], [xt], xt)
                    for hf in range(2):
                        bk = nbig()
                        for k in range(8):
                            kb.mm(bk[:, :], yT[k][:, i * 128:(i + 1) * 128], wout[:, k, hf * 512:(hf + 1) * 512],
                                  [yT[k], wout], [bk], start=(k == 0), stop=(k == 7), inc=(k == 7))
                        kb.tt(DVE, x1[i][:, hf * 512:(hf + 1) * 512], bk[:, :], xt[:, hf * 512:(hf + 1) * 512], ALU.add,
                              [bk, xt], [x1[i]])
                    if dbg:
                        kb.dma(dbg_x1[tok0 + i * 128: tok0 + (i + 1) * 128, :], x1[i][:, :], [x1[i]], [], x1[i])

                ckpt('outproj')
                for i in range(4):
                    ln_T(x1[i], i, i)
                for c in range(8):
                    wb = wnext()
                    for fi in range(4):
                        f = 4 * c + fi
                        bk = nbig()
                        for k in range(8):
                            kb.mm(bk[:, :], wb[:, k * 512 + fi * 128: k * 512 + (fi + 1) * 128], hT_h[:, k, :],
                                  [wb, hT], [bk], start=(k == 0), stop=(k == 7), inc=(k == 7))
                        r32 = wk[16 + (f % 2)]
                        kb.act(r32[:, :], bk[:, :], AF.Relu, [bk], [r32])
                        ut = wkb[f // 2]
                        kb.tt(POOL if (f % 2 == 0) else DVE, ut[:, (f % 2) * 512:(f % 2 + 1) * 512], r32[:, :], r32[:, :], ALU.mult,
                              [r32], [ut])
                    wdone()
                for c in range(8):
                    wb = wnext()
                    for fi in range(4):
                        f = 4 * c + fi
                        ut = wkb[f // 2]
                        for i in range(4):
                            for hf in range(2):
                                bk = bank[2 * i + hf]
                                kb.mm(bk[:, :], ut[:, (f % 2) * 512 + i * 128:(f % 2) * 512 + (i + 1) * 128],
                                      wb[:, fi * 1024 + hf * 512: fi * 1024 + (hf + 1) * 512],
                                      [ut, wb], [bk], start=(f == 0), stop=(f == 31), inc=(f == 31 or (fi == 3 and i == 3 and hf == 1)))
                    wdone()
                for i in range(4):
                    for hf in range(2):
                        bk = bank[2 * i + hf]
                        kb.tt(DVE, x1[i][:, hf * 512:(hf + 1) * 512], bk[:, :], x1[i][:, hf * 512:(hf + 1) * 512], ALU.add,
                              [bk, x1[i]], [x1[i]])
                    s = stat[i % 2]
                    h = hn[i % 2]
                    kb.memset(POOL, s[:, 0:1], 0.0, [s])
                    kb.act(h[:, :], x1[i][:, :], AF.Square, [x1[i], s], [h, s], accum_out=s[:, 0:1])
                    kb.act(s[:, 1:2], s[:, 0:1], AF.Ln, [s], [s], scale=1.0 / D, bias=RMS_EPS)
                    kb.act(s[:, 2:3], s[:, 1:2], AF.Exp, [s], [s], scale=-0.5)
                    kb.stt(DVE, x1[i][:, :], x1[i][:, :], s[:, 2:3], lnf[:, :], ALU.mult, ALU.mult, [x1[i], s, lnf], [x1[i]])
                    kb.dma(out_d[tok0 + i * 128: tok0 + (i + 1) * 128, :], x1[i][:, :], [x1[i]], [], x1[i])


def _consts():
    c = np.zeros((NCS, 128, 128), np.float32)
    i = np.arange(128)
    c[CS_ID] = np.eye(128)
    c[CS_TRI] = (i[:, None] >= i[None, :])
    c[CS_ONES] = 1.0
    c[CS_MATT] = (i[:, None] < i[None, :])
    blk = (i[:, None] // 64) == (i[None, :] // 64)
    c[CS_BLK] = blk
    c[CS_UTS] = blk & (i[:, None] < i[None, :])
    c[CS_LTS] = blk & (i[:, None] > i[None, :])
    c[CS_UTI] = blk & (i[:, None] <= i[None, :])
    return np.ascontiguousarray(c.transpose(1, 0, 2).reshape(128, NCS * 128))


def _host_inputs(inp):
    f = lambda a: np.asarray(a, np.float32)
    cols = []
    cols.append(f(inp["ln1_g"])[0].reshape(8, 128))
    cols.append(f(inp["ln2_g"])[0].reshape(8, 128))
    cols.append(f(inp["tok_mu"])[0].reshape(14, 128))
    for k in ("w0", "a0", "k_k", "k_a"):
        cols.append(f(inp[k])[0].reshape(4, 128))
    cols.append(f(inp["r_k"])[0].reshape(4, 128))
    for k in ("gn_w", "gn_b", "sb_gain"):
        cols.append(f(inp[k])[0].reshape(4, 128))
    pcol = np.ascontiguousarray(np.concatenate(cols, axis=0).T)
    assert pcol.shape == (128, NPC)
    wlo = np.ascontiguousarray(np.concatenate([f(inp["w_decay_up"])[0], f(inp["w_aaa_up"])[0]], axis=0))
    shared = {
        "w_in": np.ascontiguousarray(f(inp["w_in"])[0]),
        "w_out": np.ascontiguousarray(f(inp["w_out"])[0]),
        "w_up": np.ascontiguousarray(f(inp["w_up"])[0]),
        "w_down": np.ascontiguousarray(f(inp["w_down"])[0]),
        "wlo": wlo,
        "wg": np.ascontiguousarray(f(inp["w_gate_up"])[0]),
        "pcol": pcol,
        "lnf": np.ascontiguousarray(f(inp["lnf_g"]).reshape(1, D)),
        "cst": _consts(),
    }
    return shared


def kernel(**inputs):
    x = np.asarray(inputs["x"], np.float32)
    B = x.shape[0]
    nseq = B // NCORES
    shared = _host_inputs(inputs)
    nc = build(nseq=nseq)
    in_maps = []
    for c in range(NCORES):
        m = dict(shared)
        m["x"] = np.ascontiguousarray(x[c * nseq:(c + 1) * nseq].reshape(nseq * T, D))
        in_maps.append(m)
    res = run_bass_kernel_spmd(nc, in_maps, core_ids=list(range(NCORES)))
    outs = [np.asarray(r["out"]).reshape(nseq, T, D) for r in res.results]
    return np.concatenate(outs, axis=0).astype(np.float32)
```

```python
import numpy as np
from contextlib import ExitStack
import concourse.bass as bass
import concourse.mybir as mybir
from concourse.bass_utils import run_bass_kernel_spmd

F32 = mybir.dt.float32
BF16 = mybir.dt.bfloat16
AF = mybir.ActivationFunctionType
ALU = mybir.AluOpType

NCORES = 8
D = 1024
T = 2048
G = 512
NG = T // G
DFF = 4096
C0 = 0.6065306597126334
RMS_EPS = 1e-5
GN_EPS = 64e-5

PC_LN1, PC_LN2, PC_MU, PC_W0, PC_A0, PC_KK, PC_KA, PC_RK, PC_GNW, PC_GNB, PC_SBG = 0, 8, 16, 30, 34, 38, 42, 46, 50, 54, 58
NPC = 62
CS_ID, CS_TRI, CS_ONES, CS_MATT, CS_BLK, CS_UTS, CS_LTS, CS_UTI = range(8)
NCS = 8


class Atom:
    __slots__ = ("w", "r", "x")

    def __init__(self, x=False):
        self.w = {}
        self.r = {}
        self.x = x


class Tl:
    def __init__(self, ap, atoms=None):
        self.ap = ap
        self.atoms = atoms if atoms is not None else [Atom()]

    def __getitem__(self, k):
        return self.ap[k]


class Eng:
    def __init__(self, e, sem, is_pe=False):
        self.e = e
        self.sem = sem
        self.cnt = 0
        self.waited = {}
        self.is_pe = is_pe


def _atoms(lst):
    out = []
    for t in lst:
        if isinstance(t, Tl):
            out.extend(t.atoms)
        elif isinstance(t, Atom):
            out.append(t)
        else:
            out.extend(_atoms(t))
    return out


class KB:
    def __init__(self, nc, es):
        self.nc = nc
        self.es = es
        self.PE = Eng(nc.tensor, es.enter_context(nc.semaphore("s_pe")), True)
        self.ACT = Eng(nc.scalar, es.enter_context(nc.semaphore("s_act")))
        self.DVE = Eng(nc.vector, es.enter_context(nc.semaphore("s_dve")))
        self.POOL = Eng(nc.gpsimd, es.enter_context(nc.semaphore("s_pool")))
        self.SP = Eng(nc.sync, None)
        self.dkeys = []
        self.hist = {}
        self.ninstr = 0

    def _need(self, eng, reads, writes):
        need = {}
        for a in reads:
            for s, v in a.w.items():
                if need.get(s, 0) < v:
                    need[s] = v
        for a in writes:
            for s, v in a.w.items():
                if need.get(s, 0) < v:
                    need[s] = v
            for s, v in a.r.items():
                if need.get(s, 0) < v:
                    need[s] = v
        for s, v in sorted(need.items(), key=lambda kv: -kv[1]):
            if eng.is_pe and s is eng.sem:
                continue
            if eng.waited.get(s, 0) >= v:
                continue
            eng.e.wait_ge(s, v)
            eng.waited[s] = v
            self.ninstr += 1
            snap = self.hist.get((s, v))
            if snap:
                w = eng.waited
                for s2, v2 in snap.items():
                    if w.get(s2, 0) < v2:
                        w[s2] = v2

    def op(self, eng, fn, reads, writes, inc=True):
        reads = _atoms(reads)
        writes = _atoms(writes)
        xr = [a for a in reads if a.x]
        if xr:
            writes = writes + xr
        self._need(eng, reads, writes)
        ins = fn()
        self.ninstr += 1
        if inc:
            eng.cnt += 1
            ins.then_inc(eng.sem, 1)
            val = eng.cnt
            snap = dict(eng.waited)
            if not eng.is_pe:
                snap[eng.sem] = val - 1
            self.hist[(eng.sem, val)] = snap
        else:
            val = eng.cnt + 1
        s = eng.sem
        for a in reads:
            if a.r.get(s, 0) < val:
                a.r[s] = val
        for a in writes:
            if a.w.get(s, 0) < val:
                a.w[s] = val

    def dma(self, out, in_, reads, writes, key, q=None, **kw):
        q = q or self.SP
        reads = _atoms(reads)
        writes = _atoms(writes)
        if not hasattr(key, "dsem"):
            key.dsem = self.es.enter_context(self.nc.semaphore("s_dma%d" % len(self.dkeys)))
            key.dcnt = 0
            self.dkeys.append(key)
        s = key.dsem
        self._need(q, reads, writes)
        if q.waited.get(s, 0) < key.dcnt:
            q.e.wait_ge(s, key.dcnt)
            q.waited[s] = key.dcnt
        key.dcnt += 16
        val = key.dcnt
        q.e.dma_start(out=out, in_=in_, **kw).then_inc(s, 16)
        self.ninstr += 1
        self.hist[(s, val)] = dict(q.waited)
        for a in reads:
            if a.r.get(s, 0) < val:
                a.r[s] = val
        for a in writes:
            if a.w.get(s, 0) < val:
                a.w[s] = val

    def finish(self):
        for k in self.dkeys:
            if self.SP.waited.get(k.dsem, 0) < k.dcnt:
                self.nc.sync.wait_ge(k.dsem, k.dcnt)

    def mm(self, out, lhsT, rhs, reads, writes, start=True, stop=True, inc=True):
        nc = self.nc
        self.op(self.PE, lambda: nc.tensor.matmul(out, lhsT, rhs, start=start, stop=stop), reads, writes, inc)

    def tr(self, out, in_, ident, reads, writes, inc=True):
        nc = self.nc
        self.op(self.PE, lambda: nc.tensor.transpose(out, in_, ident), reads, writes, inc)

    def act(self, out, in_, func, reads, writes, **kw):
        nc = self.nc
        self.op(self.ACT, lambda: nc.scalar.activation(out=out, in_=in_, func=func, **kw), reads, writes)

    def tt(self, eng, out, in0, in1, op, reads, writes):
        self.op(eng, lambda: eng.e.tensor_tensor(out=out, in0=in0, in1=in1, op=op), reads, writes)

    def ts(self, eng, out, in0, s1, s2, op0, op1, reads, writes):
        if s2 is None:
            self.op(eng, lambda: eng.e.tensor_scalar(out=out, in0=in0, scalar1=s1, scalar2=None, op0=op0), reads, writes)
        else:
            self.op(eng, lambda: eng.e.tensor_scalar(out=out, in0=in0, scalar1=s1, scalar2=s2, op0=op0, op1=op1), reads, writes)

    def stt(self, eng, out, in0, scalar, in1, op0, op1, reads, writes):
        self.op(eng, lambda: eng.e.scalar_tensor_tensor(out=out, in0=in0, scalar=scalar, in1=in1, op0=op0, op1=op1), reads, writes)

    def cp(self, eng, out, in_, reads, writes):
        self.op(eng, lambda: eng.e.tensor_copy(out, in_), reads, writes)

    def memset(self, eng, ap, val, writes):
        self.op(eng, lambda: eng.e.memset(ap, val), [], writes)


def _win_chunks():
    ch = [[(1536, 256)]]
    for p in range(4):
        ch.append([(128 * p, 128), (512 + 128 * p, 128), (1024 + 128 * p, 128)])
    ch.append([(1792, 512)])
    ch.append([(2304, 512)])
    ch.append([(2816, 512)])
    return ch


class StopBuild(Exception):
    pass


def build(nseq=4, dbg=False, stop_at=None):
    nc = bass.Bass("TRN2", target_bir_lowering=False)
    NT = nseq * T
    x_d = nc.dram_tensor("x", [NT, D], F32, kind="ExternalInput").ap()
    win_d = nc.dram_tensor("w_in", [D, 3328], F32, kind="ExternalInput").ap()
    wout_d = nc.dram_tensor("w_out", [D, D], F32, kind="ExternalInput").ap()
    wup_d = nc.dram_tensor("w_up", [D, DFF], F32, kind="ExternalInput").ap()
    wdn_d = nc.dram_tensor("w_down", [DFF, D], F32, kind="ExternalInput").ap()
    wlo_d = nc.dram_tensor("wlo", [128, 512], F32, kind="ExternalInput").ap()
    wg_d = nc.dram_tensor("wg", [128, 512], F32, kind="ExternalInput").ap()
    pcol_d = nc.dram_tensor("pcol", [128, NPC], F32, kind="ExternalInput").ap()
    lnf_d = nc.dram_tensor("lnf", [1, D], F32, kind="ExternalInput").ap()
    cst_d = nc.dram_tensor("cst", [128, NCS * 128], F32, kind="ExternalInput").ap()
    out_d = nc.dram_tensor("out", [NT, D], F32, kind="ExternalOutput").ap()
    wsc_d = nc.dram_tensor("wscr", [24, 128, 4096], BF16).ap()
    if dbg:
        dbg_y = nc.dram_tensor("dbg_y", [D, NT], F32, kind="ExternalOutput").ap()
        dbg_x1 = nc.dram_tensor("dbg_x1", [NT, D], F32, kind="ExternalOutput").ap()

    es = ExitStack()
    with es:
        kb = KB(nc, es)
        PE, ACT, DVE, POOL = kb.PE, kb.ACT, kb.DVE, kb.POOL

        def ckpt(name):
            if stop_at is not None and name == stop_at:
                raise StopBuild()
        try:
            _body(nc, es, kb, nseq, dbg, ckpt, locals())
        except StopBuild:
            pass
        kb.finish()
        print("instructions emitted:", kb.ninstr, "pe", PE.cnt, "act", ACT.cnt, "dve", DVE.cnt, "pool", POOL.cnt)
    return nc


def _body(nc, es, kb, nseq, dbg, ckpt, env):
    globals_ = env
    x_d, win_d, wout_d, wup_d, wdn_d, wlo_d, wg_d, pcol_d, lnf_d, cst_d, out_d, wsc_d = (env[k] for k in (
        "x_d", "win_d", "wout_d", "wup_d", "wdn_d", "wlo_d", "wg_d", "pcol_d", "lnf_d", "cst_d", "out_d", "wsc_d"))
    dbg_y = env.get("dbg_y")
    dbg_x1 = env.get("dbg_x1")
    PE, ACT, DVE, POOL = kb.PE, kb.ACT, kb.DVE, kb.POOL
    if True:

        def sb(name, shape, dt, natoms=1):
            h = es.enter_context(nc.sbuf_tensor("sb_" + name, shape, dt))
            return h

        cst = Tl(sb("cstb", [128, NCS * 128], BF16)[:])
        pcol = Tl(sb("pcol", [128, NPC], F32)[:])
        lnf = Tl(sb("lnfb", [128, D], F32)[:])
        wlo = Tl(sb("wlo", [128, 512], BF16)[:])
        wgt = Tl(sb("wgt", [128, 512], BF16)[:])
        wout = Tl(sb("wout", [128, 8, D], BF16)[:])
        NWB = 2
        wbuf = [Tl(sb("wbuf%d" % i, [128, 4096], BF16)[:]) for i in range(NWB)]
        xb = [Tl(sb("xb%d" % i, [128, D], F32)[:]) for i in range(2)]
        hn = [Tl(sb("hn%d" % i, [128, D], BF16)[:]) for i in range(1)] * 2
        stat = [Tl(sb("stat%d" % i, [128, 4], F32)[:]) for i in range(2)]
        hT_h = sb("hT", [128, 8, G], BF16)
        hT = Tl(hT_h[:], [Atom() for _ in range(4)])
        x1 = [Tl(sb("x1_%d" % i, [128, D], F32)[:]) for i in range(4)]
        yT = [Tl(sb("yT%d" % k, [128, G], BF16)[:]) for k in range(8)]
        kT = [Tl(sb("kT%d" % p, [128, T], BF16)[:], [Atom() for _ in range(NG)]) for p in range(4)]
        vtok = [Tl(sb("vtok%d" % i, [128, 512], BF16)[:]) for i in range(16)]
        qs = [Tl(sb("qs%d" % p, [128, G], BF16)[:]) for p in range(4)]
        qn = [Tl(sb("qn%d" % p, [128, G], BF16)[:]) for p in range(4)]
        NWK = 18
        wk_h = [sb("wk%d" % i, [128, 512], F32) for i in range(NWK)]
        wk = [Tl(h[:]) for h in wk_h]
        wkb = [Tl(h[:].bitcast(BF16), wk[i].atoms) for i, h in enumerate(wk_h)]
        rkv = [[Tl(sb("rkv%d_%d" % (b, j), [128, 512], F32)[:]) for j in range(3)] for b in range(1)]
        pst = [Tl(sb("pst%d" % i, [128, 513], F32)[:]) for i in range(1)] * 2
        carry = Tl(sb("carry", [128, 14], F32)[:])
        txa = Tl(sb("txa", [128, 512], BF16)[:])
        sgx = Tl(sb("sgx", [128, 512], BF16)[:])
        pb = [[Tl(sb("pb%d_%d" % (b, j), [128, 512], BF16)[:]) for j in range(5)] for b in range(1)]
        tmpb = Tl(sb("tmpb", [128, 512], BF16)[:])
        zt = Tl(sb("zt", [128, 64], BF16)[:])
        NBD = 36
        bdt = [Tl(sb("bd%d" % i, [128, 128], BF16)[:]) for i in range(NBD)]
        bdl = [[Tl(sb("bdl%d_%d" % (j, i), [128, 128], BF16)[:]) for i in range(9)] for j in range(8)]
        PCS = [Tl(sb("pcs%d" % i, [128, 8], F32)[:]) for i in range(2)]
        H32 = [Tl(sb("H32_%d" % p, [128, 128], F32)[:]) for p in range(4)]
        HP = [Tl(sb("HP_%d" % p, [128, 128], F32)[:]) for p in range(2)]
        Hb = [Tl(sb("Hb_%d" % p, [128, 128], BF16)[:]) for p in range(4)]
        bank_h = [es.enter_context(nc.psum_tensor("bank%d" % i, [128, 512], F32)) for i in range(8)]
        bank_atoms = [[Atom(True)] for _ in range(8)]
        bank = [Tl(bank_h[i][:], bank_atoms[i]) for i in range(8)]
        small = []
        for q in range(4):
            for b in range(4, 8):
                t_ = Tl(bank_h[b][:, q * 128:(q + 1) * 128], [bank_atoms[b][0]])
                t_.bfv = bank_h[b][:].bitcast(BF16)[:, q * 256:q * 256 + 128]
                small.append(t_)
        st = {"bd": 0, "small": 0, "big": 0}

        def nbd():
            t = bdt[st["bd"] % NBD]
            st["bd"] += 1
            return t

        def nsmall():
            t = small[st["small"] % len(small)]
            st["small"] += 1
            return t

        def nbig():
            t = bank[st["big"] % 4]
            st["big"] += 1
            return t

        def C(i):
            return cst[:, i * 128:(i + 1) * 128]

        def pc(col):
            return pcol[:, col:col + 1]

        kb.dma(wk[14][:, :], cst_d[:, 0:512], [], [wk[14]], wk[14])
        kb.dma(wk[15][:, :], cst_d[:, 512:1024], [], [wk[15]], wk[15])
        kb.memset(POOL, zt[:, :], 0.0, [zt])
        kb.dma(pcol[:, :], pcol_d[:, :], [], [pcol], pcol)
        kb.dma(lnf[:, :], lnf_d[0:1, :].partition_broadcast(128), [], [lnf], lnf)
        kb.cp(DVE, cst[:, 0:512], wk[14][:, :], [wk[14]], [cst])
        kb.cp(DVE, cst[:, 512:1024], wk[15][:, :], [wk[15]], [cst])
        kb.dma(wk[0][:, :], wlo_d[:, :], [], [wk[0]], wk[0])
        kb.dma(wk[1][:, :], wg_d[:, :], [], [wk[1]], wk[1])
        kb.cp(DVE, wlo[:, :], wk[0][:, :], [wk[0]], [wlo])
        kb.cp(DVE, wgt[:, :], wk[1][:, :], [wk[1]], [wgt])

        ckpt('consts')
        scr = [Tl(wsc_d[i], [Atom()]) for i in range(24)]
        cast_rr = [0]
        cast_engs = [DVE, POOL, ACT]

        def cast(out_ap, in_tl, in_ap, scale_col, out_tl):
            e = cast_engs[cast_rr[0] % 3]
            cast_rr[0] += 1
            if scale_col is None:
                if e is ACT:
                    kb.act(out_ap, in_ap, AF.Copy, [in_tl], [out_tl])
                else:
                    kb.cp(e, out_ap, in_ap, [in_tl], [out_tl])
            else:
                if e is ACT:
                    kb.act(out_ap, in_ap, AF.Copy, [in_tl, pcol], [out_tl], scale=pc(scale_col))
                else:
                    kb.ts(e, out_ap, in_ap, pc(scale_col), None, ALU.mult, None, [in_tl, pcol], [out_tl])

        stg_rr = [0]

        def stage():
            t = wk[2 + (stg_rr[0] % 12)]
            t.q = kb.SP
            stg_rr[0] += 1
            return t

        wchunks = _win_chunks()
        wb_rr = [0]
        big_stages = x1 + xb
        bst = [0]

        def stage4():
            t = big_stages[bst[0] % len(big_stages)]
            bst[0] += 1
            return t

        for ci, ch in enumerate(wchunks):
            ncols = sum(n for _, n in ch)
            wb = wbuf[wb_rr[0] % NWB]
            wb_rr[0] += 1
            for k0 in range(0, 8, 2):
                s = stage4()
                sv = s[:, 0:2 * ncols].rearrange("p (k c) -> p k c", k=2)
                off = 0
                for (c0, n) in ch:
                    kb.dma(sv[:, :, off:off + n],
                           win_d[k0 * 128:(k0 + 2) * 128, c0:c0 + n].rearrange("(k p) c -> p k c", k=2), [], [s], s)
                    off += n
                for kk in range(2):
                    cast(wb[:, (k0 + kk) * ncols:(k0 + kk + 1) * ncols], s, s[:, kk * ncols:(kk + 1) * ncols],
                         PC_LN1 + k0 + kk, wb)
            kb.dma(wsc_d[ci][:, 0:8 * ncols], wb[:, 0:8 * ncols], [wb], [scr[ci]], wb)
        for c in range(8):
            wb = wbuf[wb_rr[0] % NWB]
            wb_rr[0] += 1
            for k0 in range(0, 8, 2):
                s = stage4()
                kb.dma(s[:, :].rearrange("p (k c) -> p k c", k=2),
                       wup_d[k0 * 128:(k0 + 2) * 128, c * 512:(c + 1) * 512].rearrange("(k p) c -> p k c", k=2), [], [s], s)
                for kk in range(2):
                    cast(wb[:, (k0 + kk) * 512:(k0 + kk + 1) * 512], s, s[:, kk * 512:(kk + 1) * 512], PC_LN2 + k0 + kk, wb)
            kb.dma(wsc_d[8 + c][:, :], wb[:, :], [wb], [scr[8 + c]], wb)
        for c in range(8):
            wb = wbuf[wb_rr[0] % NWB]
            wb_rr[0] += 1
            for fi in range(4):
                s = stage4()
                r0 = c * 512 + fi * 128
                kb.dma(s[:, :], wdn_d[r0:r0 + 128, :], [], [s], s)
                cast(wb[:, fi * 1024:(fi + 1) * 1024], s, s[:, :], None, wb)
            kb.dma(wsc_d[16 + c][:, :], wb[:, :], [wb], [scr[16 + c]], wb)
        for k in range(8):
            s = stage4()
            kb.dma(s[:, :], wout_d[k * 128:(k + 1) * 128, :], [], [s], s)
            cast(wout[:, k, :], s, s[:, :], None, wout)

        ckpt('casts')
        stream = []
        for _ in range(nseq * NG):
            for ci, ch in enumerate(wchunks):
                stream.append((ci, 8 * sum(n for _, n in ch)))
            for c in range(8):
                stream.append((8 + c, 4096))
            for c in range(8):
                stream.append((16 + c, 4096))
        sst = {"issued": 0, "used": 0}

        def wprefetch():
            while sst["issued"] < len(stream) and sst["issued"] < sst["used"] + NWB:
                i = sst["issued"]
                ci, n = stream[i]
                wb = wbuf[(wb_rr[0] + i) % NWB]
                kb.dma(wb[:, 0:n], wsc_d[ci][:, 0:n], [scr[ci]], [wb], wb)
                sst["issued"] += 1

        def wnext():
            wprefetch()
            i = sst["used"]
            wb = wbuf[(wb_rr[0] + i) % NWB]
            sst["used"] += 1
            return wb

        def wdone():
            wprefetch()

        def ln_T(src, i, sidx):
            h = hn[sidx % 2]
            s = stat[sidx % 2]
            kb.memset(POOL, s[:, 0:1], 0.0, [s])
            kb.act(h[:, :], src[:, :], AF.Square, [src, s], [h, s], accum_out=s[:, 0:1])
            kb.act(s[:, 1:2], s[:, 0:1], AF.Ln, [s], [s], scale=1.0 / D, bias=RMS_EPS)
            kb.act(s[:, 2:3], s[:, 1:2], AF.Exp, [s], [s], scale=-0.5)
            kb.ts(DVE, h[:, :], src[:, :], s[:, 2:3], None, ALU.mult, None, [src, s], [h])
            bk = nbig()
            bv = bk.ap.bitcast(BF16)
            for k in range(8):
                kb.tr(bv[:, k * 128:(k + 1) * 128], h[:, k * 128:(k + 1) * 128], C(CS_ID), [h, cst], [bk], inc=(k == 7))
            kb.cp(DVE, hT_h[:, :, i * 128:(i + 1) * 128],
                  bv[:, :].rearrange("p (k c) -> p k c", k=8), [bk], [hT.atoms[i]])
            return s

        def rstd_from_ss(ps_ap, ps_tl, out_tl, scale, eps):
            kb.act(out_tl[:, :], ps_ap, AF.Ln, [ps_tl], [out_tl], scale=scale, bias=eps)
            kb.act(out_tl[:, :], out_tl[:, :], AF.Exp, [out_tl], [out_tl], scale=-0.5)

        for seq in range(nseq):
            kb.memset(POOL, carry[:, :], 0.0, [carry])
            for p in range(4):
                kb.memset(POOL, H32[p][:, :], 0.0, [H32[p]])
                kb.memset(POOL, Hb[p][:, :], 0.0, [Hb[p]])
            for g in range(NG):
                tok0 = seq * T + g * G
                for i in range(4):
                    xt = xb[i % 2]
                    kb.dma(xt[:, :], x_d[tok0 + i * 128: tok0 + (i + 1) * 128, :], [], [xt], xt)
                    ln_T(xt, i, i)

                ckpt('ln1')
                def inproj_fm(wb, col0, ncols_chunk):
                    bk = nbig()
                    for k in range(8):
                        kb.mm(bk[:, :], wb[:, k * ncols_chunk + col0: k * ncols_chunk + col0 + 128], hT_h[:, k, :],
                              [wb, hT], [bk], start=(k == 0), stop=(k == 7), inc=(k == 7))
                    return bk

                def shiftmix(bk, ct, dst_ap, dst_tl):
                    P = pst[ct % 2]
                    kb.act(P[:, 1:513], bk[:, :], AF.Copy, [bk], [P])
                    kb.cp(POOL, P[:, 0:1], carry[:, ct:ct + 1], [carry], [P])
                    d = wk[17]
                    kb.tt(DVE, d[:, :], P[:, 0:512], P[:, 1:513], ALU.subtract, [P], [d])
                    kb.stt(DVE, dst_ap, d[:, :], pc(PC_MU + ct), P[:, 1:513], ALU.mult, ALU.add, [d, P, pcol], [dst_tl])
                    kb.cp(POOL, carry[:, ct:ct + 1], P[:, 512:513], [P], [carry])

                wb = wnext()
                bk = inproj_fm(wb, 0, 256)
                t12 = wk[16]
                shiftmix(bk, 12, t12[:, :], t12)
                kb.act(txa[0:64, :], t12[0:64, :], AF.Tanh, [t12], [txa])
                kb.cp(POOL, txa[64:128, :], t12[64:128, :], [t12], [txa])
                bk = inproj_fm(wb, 128, 256)
                wdone()
                t13 = wk[16]
                shiftmix(bk, 13, t13[:, :], t13)
                kb.act(sgx[:, :], t13[:, :], AF.Sigmoid, [t13], [sgx])

                ckpt('chunk0')
                PP = {}

                def prep_gen(p):
                        wb = wnext()
                        R_, K_, V_ = rkv[0]
                        for j, dst in enumerate((R_, K_, V_)):
                            bk = inproj_fm(wb, 128 * j, 384)
                            shiftmix(bk, 4 * j + p, dst[:, :], dst)
                            yield
                        wdone()
                        Ab, Rb, Bb, Kb_, Vb = pb[0]
                        cols = slice(128 * p, 128 * (p + 1))
                        b1 = nbig()
                        kb.mm(b1[:, :], wlo[0:64, cols], txa[0:64, :], [wlo, txa], [b1])
                        sgm = wk[0]
                        kb.act(sgm[:, :], b1[:, :], AF.Sigmoid, [b1, pcol], [sgm], bias=pc(PC_W0 + p))
                        b2 = nbig()
                        kb.mm(b2[:, :], wlo[64:128, cols], txa[64:128, :], [wlo, txa], [b2])
                        lr = wk[1]
                        kb.act(lr[:, :], b2[:, :], AF.Sigmoid, [b2, pcol], [lr], bias=pc(PC_A0 + p))
                        yield
                        cs = wk[2]
                        for j in range(8):
                            sl = slice(64 * j, 64 * j + 64)
                            kb.op(DVE, lambda sl=sl: nc.vector.tensor_tensor_scan(out=cs[:, sl], data0=sgm[:, sl], data1=sgm[:, sl],
                                                                                      initial=0.0, op0=ALU.add, op1=ALU.bypass),
                                  [sgm], [cs])
                        yield
                        kkr = wk[3]
                        kb.ts(POOL, kkr[:, :], K_[:, :], pc(PC_KK + p), None, ALU.mult, None, [K_, pcol], [kkr])
                        kb.act(tmpb[:, :], kkr[:, :], AF.Square, [kkr], [tmpb])
                        b4 = nbig()
                        kb.mm(b4[:, :], C(CS_BLK), tmpb[:, :], [cst, tmpb], [b4])
                        rn = wk[4]
                        kb.ts(DVE, rn[:, :], b4[:, :], 1e-24, None, ALU.max, None, [b4], [rn])
                        kb.act(rn[:, :], rn[:, :], AF.Ln, [rn], [rn])
                        kb.act(rn[:, :], rn[:, :], AF.Exp, [rn], [rn], scale=-0.5)
                        kb.tt(POOL, kkr[:, :], kkr[:, :], rn[:, :], ALU.mult, [kkr, rn], [kkr])
                        yield
                        Ep, Em, Epv = wk[5], wk[6], wk[7]
                        kb.act(Ep[:, :], cs[:, :], AF.Exp, [cs], [Ep], scale=-C0)
                        kb.act(Em[:, :], cs[:, :], AF.Exp, [cs], [Em], scale=C0)
                        kb.tt(POOL, Epv[:, :], cs[:, :], sgm[:, :], ALU.subtract, [cs, sgm], [Epv])
                        kb.act(Epv[:, :], Epv[:, :], AF.Exp, [Epv], [Epv], scale=-C0)
                        yield
                        kb.stt(DVE, Ab[:, :], kkr[:, :], -1.0, Epv[:, :], ALU.mult, ALU.mult, [kkr, Epv], [Ab])
                        kb.tt(POOL, Rb[:, :], R_[:, :], Ep[:, :], ALU.mult, [R_, Ep], [Rb])
                        bv_ = wk[8]
                        kb.tt(DVE, bv_[:, :], kkr[:, :], lr[:, :], ALU.mult, [kkr, lr], [bv_])
                        kb.tt(POOL, Bb[:, :], bv_[:, :], Em[:, :], ALU.mult, [bv_, Em], [Bb])
                        kb.ts(DVE, lr[:, :], lr[:, :], -1.0, pc(PC_KA + p), ALU.add, ALU.mult, [lr, pcol], [lr])
                        yield
                        km = wk[9]
                        kb.stt(DVE, km[:, :], lr[:, :], 1.0, K_[:, :], ALU.add, ALU.mult, [lr, K_], [km])
                        kb.tt(POOL, Kb_[:, :], km[:, :], Em[:, :], ALU.mult, [km, Em], [Kb_])
                        kb.act(Vb[:, :], V_[:, :], AF.Copy, [V_], [Vb])
                        kb.stt(DVE, tmpb[:, :], R_[:, :], pc(PC_RK + p), km[:, :], ALU.mult, ALU.mult, [R_, km, pcol], [tmpb])
                        b5 = nbig()
                        kb.mm(b5[:, :], C(CS_BLK), tmpb[:, :], [cst, tmpb], [b5])
                        bonus = wk[10] if p % 2 == 0 else wk[15]
                        kb.tt(DVE, bonus[:, :], V_[:, :], b5[:, :], ALU.mult, [V_, b5], [bonus])
                        pcs = PCS[p % 2]
                        kb.cp(POOL, pcs[:, :], Ep[:, 63:512:64], [Ep], [pcs])
                        PP[p] = dict(bonus=bonus, pcs=pcs)
                        yield

                for _ in prep_gen(0):
                    pass
                CTX = [[dict() for _ in range(8)] for _ in range(4)]

                def seqgen(*gens):
                    for g_ in gens:
                        yield from g_

                for p in range(4):
                    bdm = C(CS_BLK).rearrange("p (a b) -> p a b", a=2)
                    pcs = PP[p]["pcs"]
                    bonus = PP[p]["bonus"]
                    Ab, Rb, Bb, Kb_, Vb = pb[0]
                    yraw = wk[12]
                    cols = slice(128 * p, 128 * (p + 1))
                    ctxs = CTX[p]

                    def pre_gen(jl, pp):
                        ctxs = CTX[pp]
                        def mkbd(src, t, sl):
                            kb.tt(POOL, t[:, :].rearrange("p (a b) -> p a b", a=2),
                                  src[:, sl].unsqueeze(1).broadcast_to([128, 2, 64]), bdm, ALU.mult, [src, cst], [t])
                            return t
                        for j in jl:
                            c = ctxs[j]
                            sl = slice(64 * j, 64 * j + 64)
                            LL = bdl[j]
                            c["AT"] = mkbd(Ab, LL[0], sl)
                            c["RT"] = mkbd(Rb, LL[1], sl)
                            c["BT"] = mkbd(Bb, nbd(), sl)
                            c["KT"] = mkbd(Kb_, nbd(), sl)
                            c["VT"] = mkbd(Vb, nbd(), sl)
                        yield
                        for j in jl:
                            c = ctxs[j]
                            LL = bdl[j]
                            for nm, src, dst in (("Btok", c["BT"], LL[2]), ("Ktok", c["KT"], LL[3]), ("Vtok", c["VT"], LL[4])):
                                ps = nsmall()
                                kb.tr(ps.bfv, src[:, :], C(CS_ID), [src, cst], [ps])
                                kb.cp(DVE, dst[:, :], ps.bfv, [ps], [dst])
                                c[nm] = dst
                        yield

                        def gram(l, r, mask, t):
                            ps = nsmall()
                            kb.mm(ps[:, :], l[:, :], r[:, :], [l, r], [ps])
                            kb.tt(DVE, t[:, :], ps[:, :], C(mask), ALU.mult, [ps, cst], [t])
                            return t
                        for j in jl:
                            c = ctxs[j]
                            LL = bdl[j]
                            c["M"] = gram(c["BT"], c["AT"], CS_UTS, nbd())
                            c["N"] = gram(c["AT"], c["BT"], CS_LTS, nbd())
                            c["Aak"] = gram(c["KT"], c["AT"], CS_UTS, LL[5])
                        yield
                        for j in jl:
                            c = ctxs[j]
                            LL = bdl[j]
                            c["Arb"] = gram(c["BT"], c["RT"], CS_UTI, LL[6])
                            c["Ark"] = gram(c["KT"], c["RT"], CS_UTI, LL[7])
                            X = nbd()
                            kb.tt(POOL, X[:, :], c["M"][:, :], C(CS_ID), ALU.add, [c["M"], cst], [X])
                            c["X"] = X
                        yield
                        for lev in range(1, 6):
                            for j in jl:
                                c = ctxs[j]
                                psn = nsmall()
                                kb.mm(psn[:, :], c["M"][:, :], c["N"][:, :], [c["M"], c["N"]], [psn])
                                Nn = nbd()
                                kb.act(Nn[:, :], psn[:, :], AF.Copy, [psn], [Nn])
                                c["Nn"] = Nn
                            yield
                            if lev < 5:
                                for j in jl:
                                    c = ctxs[j]
                                    psm = nsmall()
                                    kb.mm(psm[:, :], c["N"][:, :], c["M"][:, :], [c["M"], c["N"]], [psm])
                                    Mn = nbd()
                                    kb.act(Mn[:, :], psm[:, :], AF.Copy, [psm], [Mn])
                                    c["Mn"] = Mn
                                yield
                            for j in jl:
                                c = ctxs[j]
                                psx = nsmall()
                                kb.mm(psx[:, :], c["Nn"][:, :], c["X"][:, :], [c["Nn"], c["X"]], [psx])
                                Xn = nbd() if lev < 5 else bdl[j][8]
                                kb.tt(DVE, Xn[:, :], psx[:, :], c["X"][:, :], ALU.add, [psx, c["X"]], [Xn])
                                c["X"] = Xn
                                c["N"] = c["Nn"]
                                c["M"] = c.get("Mn") if lev < 5 else None
                            yield

                    def chain_gen(jl):
                        for j in jl:
                            c = ctxs[j]
                            sl = slice(64 * j, 64 * j + 64)
                            hp = HP[j % 2]
                            kb.act(hp[:, :], H32[p][:, :], AF.Copy, [H32[p], pcs], [hp], scale=pcs[:, j:j + 1])
                            psw = nsmall()
                            kb.mm(psw[:, :], c["AT"][:, :], Hb[p][:, :], [c["AT"], Hb[p]], [psw], start=True, stop=False, inc=False)
                            kb.mm(psw[:, :], c["Aak"][:, :], c["Vtok"][:, :], [c["Aak"], c["Vtok"]], [psw], start=False, stop=True)
                            Wb = nbd()
                            kb.act(Wb[:, :], psw[:, :], AF.Copy, [psw], [Wb])
                            yield
                            psu = nsmall()
                            kb.mm(psu[:, :], c["X"][:, :], Wb[:, :], [c["X"], Wb], [psu])
                            Ub = nbd()
                            kb.cp(DVE, Ub[:, :], psu[:, :], [psu], [Ub])
                            yield
                            psh = nsmall()
                            kb.mm(psh[:, :], c["Btok"][:, :], Ub[:, :], [c["Btok"], Ub], [psh], start=True, stop=False, inc=False)
                            kb.mm(psh[:, :], c["Ktok"][:, :], c["Vtok"][:, :], [c["Ktok"], c["Vtok"]], [psh], start=False, stop=True)
                            psy = nsmall()
                            kb.mm(psy[:, :], Hb[p][:, :], c["RT"][:, :], [Hb[p], c["RT"]], [psy], start=True, stop=False, inc=False)
                            kb.mm(psy[:, :], Ub[:, :], c["Arb"][:, :], [Ub, c["Arb"]], [psy], start=False, stop=False, inc=False)
                            kb.mm(psy[:, :], c["Vtok"][:, :], c["Ark"][:, :], [c["Vtok"], c["Ark"]], [psy], start=False, stop=True)
                            kb.stt(DVE, H32[p][:, :], psh[:, :], pcs[:, j:j + 1], hp[:, :], ALU.mult, ALU.add,
                                   [psh, pcs, hp], [H32[p]])
                            kb.cp(POOL, Hb[p][:, :], H32[p][:, :], [H32[p]], [Hb[p]])
                            yield
                            kb.act(yraw[0:64, sl], psy[0:64, 0:64], AF.Copy, [psy], [yraw])
                            kb.act(yraw[64:128, sl], psy[64:128, 64:128], AF.Copy, [psy], [yraw])
                            yield

                    def run_il(*gens):
                        gens = list(gens)
                        while gens:
                            for g_ in list(gens):
                                try:
                                    next(g_)
                                except StopIteration:
                                    gens.remove(g_)

                    if p == 0:
                        run_il(pre_gen([0, 1, 2, 3], 0))
                    run_il(chain_gen([0, 1, 2, 3]), pre_gen([4, 5, 6, 7], p))
                    if p < 3:
                        run_il(chain_gen([4, 5, 6, 7]), seqgen(prep_gen(p + 1), pre_gen([0, 1, 2, 3], p + 1)))
                    else:
                        run_il(chain_gen([4, 5, 6, 7]))
                    ckpt('chunks')
                    kb.act(tmpb[:, :], yraw[:, :], AF.Copy, [yraw], [tmpb])
                    b6 = nbig()
                    kb.mm(b6[:, :], C(CS_BLK), tmpb[:, :], [cst, tmpb], [b6])
                    yc = wk[13]
                    kb.stt(DVE, yc[:, :], b6[:, :], -1.0 / 64, yraw[:, :], ALU.mult, ALU.add, [b6, yraw], [yc])
                    kb.act(tmpb[:, :], yc[:, :], AF.Square, [yc], [tmpb])
                    b7 = nbig()
                    kb.mm(b7[:, :], C(CS_BLK), tmpb[:, :], [cst, tmpb], [b7])
                    rs = wk[14]
                    rstd_from_ss(b7[:, :], b7, rs, 1.0 / 64, GN_EPS)
                    kb.tt(DVE, yc[:, :], yc[:, :], rs[:, :], ALU.mult, [yc, rs], [yc])
                    kb.ts(POOL, yc[:, :], yc[:, :], pc(PC_GNW + p), pc(PC_GNB + p), ALU.mult, ALU.add, [yc, pcol], [yc])
                    kb.tt(DVE, yc[:, :], yc[:, :], bonus[:, :], ALU.add, [yc, bonus], [yc])
                    b3 = nbig()
                    kb.mm(b3[:, :], wgt[:, cols], sgx[:, :], [wgt, sgx], [b3])
                    kb.tt(DVE, yT[p][:, :], yc[:, :], b3[:, :], ALU.mult, [yc, b3], [yT[p]])

                ckpt('rwkv')
                wb = wnext()
                for p in range(4):
                    bk = inproj_fm(wb, 128 * p, 512)
                    kb.act(qs[p][:, :], bk[:, :], AF.Copy, [bk], [qs[p]], scale=0.125)
                wdone()
                wb = wnext()
                for p in range(4):
                    bk = inproj_fm(wb, 128 * p, 512)
                    kb.act(kT[p][:, g * G:(g + 1) * G], bk[:, :], AF.Copy, [bk], [kT[p].atoms[g]])
                wdone()
                wb = wnext()
                for i in range(4):
                    bk = nbig()
                    for k in range(8):
                        kb.mm(bk[:, :], hT_h[:, k, i * 128:(i + 1) * 128], wb[:, k * 512:(k + 1) * 512],
                              [wb, hT.atoms[i]], [bk], start=(k == 0), stop=(k == 7), inc=(k == 7))
                    kb.cp(DVE, vtok[4 * g + i][:, :], bk[:, :], [bk], [vtok[4 * g + i]])
                wdone()

                ckpt('qkv')
                def bv(i):
                    return Tl(wkb[i][:, 0:512], wk[i].atoms)
                E32 = [[wk[0], wk[14]], [wk[1], wk[15]]]
                XC = [wk[16], wk[17]]
                SPb = [[bv(2), bv(3)], [bv(4), bv(5)]]
                ATb = [[bv(6), bv(7)], [bv(8), bv(9)]]
                Ssum = [bv(10), bv(11)]
                osq = bv(12)
                rst = wk[13]
                nkb = 4 * g + 4
                kbs = list(range(nkb - 1, -1, -1))
                for p in range(4):
                    Bo = bank[0]
                    Bz = [[bank[1], bank[6]], [bank[2], bank[7]]]
                    Bc = [bank[3], bank[4]]

                    def geom(idx):
                        kbk = kbs[idx]
                        dq = kbk - 4 * g
                        q0 = 128 * max(dq, 0)
                        return kbk, dq, q0, slice(q0, 512), slice(kbk * 128, (kbk + 1) * 128), kT[p].atoms[kbk // 4]

                    def zmm(idx, hh):
                        kbk, dq, q0, cs_, kblk, katoms = geom(idx)
                        rows = slice(64 * hh, 64 * hh + 64)
                        bz = Bz[hh][idx % 2]
                        kb.mm(bz[:, cs_], kT[p][rows, kblk], qs[p][rows, cs_], [katoms, qs[p]], [bz])

                    def expA(idx, hh):
                        kbk, dq, q0, cs_, kblk, katoms = geom(idx)
                        bz = Bz[hh][idx % 2]
                        e = E32[hh][idx % 2]
                        kb.act(e[:, cs_], bz[:, cs_], AF.Exp, [bz], [e])
                        if dq >= 0:
                            kb.tt(DVE, e[:, q0:q0 + 128], e[:, q0:q0 + 128], C(CS_MATT), ALU.mult, [e, cst], [e])

                    def lnA(idx, hh):
                        kbk, dq, q0, cs_, kblk, katoms = geom(idx)
                        e = E32[hh][idx % 2]
                        sp = SPb[hh][idx % 2]
                        kb.act(sp[:, cs_], e[:, cs_], AF.Ln, [e], [sp], bias=1.0)

                    def actA(idx):
                        expA(idx, 0)
                        expA(idx, 1)
                        lnA(idx, 0)
                        lnA(idx, 1)

                    def stB1(idx, hh):
                        kbk, dq, q0, cs_, kblk, katoms = geom(idx)
                        rows = slice(64 * hh, 64 * hh + 64)
                        sp = SPb[hh][idx % 2]
                        kb.mm(Bc[hh][:, cs_], C(CS_TRI), sp[:, cs_], [cst, sp], [Bc[hh]], start=True, stop=(idx == 0), inc=(idx == 0))
                        if idx > 0:
                            kb.mm(Bc[hh][:, cs_], C(CS_ONES), Ssum[hh][:, cs_], [cst, Ssum[hh]], [Bc[hh]], start=False, stop=True)
                        if kbk > 0:
                            kb.tt(DVE, Ssum[hh][:, cs_], Ssum[hh][:, cs_], sp[:, cs_], ALU.add, [Ssum[hh], sp], [Ssum[hh]])

                    def stB2a(idx, hh):
                        kbk, dq, q0, cs_, kblk, katoms = geom(idx)
                        at = ATb[hh][idx % 2]
                        xc = XC[hh]
                        kb.act(xc[:, cs_], Bc[hh][:, cs_], AF.Exp, [Bc[hh]], [xc], scale=-1.0)
                        e = E32[hh][idx % 2]
                        kb.tt(DVE, at[:, cs_], e[:, cs_], xc[:, cs_], ALU.mult, [e, xc], [at])

                    def stB2b(idx, hh):
                        kbk, dq, q0, cs_, kblk, katoms = geom(idx)
                        rows = slice(64 * hh, 64 * hh + 64)
                        at = ATb[hh][idx % 2]
                        kb.mm(Bo[rows, cs_], vtok[kbk][:, 128 * p + 64 * hh: 128 * p + 64 * hh + 64], at[:, cs_],
                              [vtok[kbk], at], [Bo], start=False, stop=(kbk == 0), inc=True)

                    for hh in range(2):
                        rows = slice(64 * hh, 64 * hh + 64)
                        kb.memset(POOL, Ssum[hh][:, :], 0.0, [Ssum[hh]])
                        kb.mm(Bo[rows, :], zt[:, :], qs[p][:, :], [zt, qs[p]], [Bo], start=True, stop=False, inc=False)
                    zmm(0, 0)
                    zmm(0, 1)
                    if nkb > 1:
                        zmm(1, 0)
                        zmm(1, 1)
                    actA(0)
                    for idx in range(nkb):
                        if idx + 2 < nkb:
                            zmm(idx + 2, 0)
                            zmm(idx + 2, 1)
                        if idx + 1 < nkb:
                            actA(idx + 1)
                        stB1(idx, 0)
                        stB1(idx, 1)
                        if idx > 0:
                            stB2b(idx - 1, 0)
                            stB2b(idx - 1, 1)
                        stB2a(idx, 0)
                        stB2a(idx, 1)
                    stB2b(nkb - 1, 0)
                    stB2b(nkb - 1, 1)
                    kb.act(osq[:, :], Bo[:, :], AF.Square, [Bo], [osq])
                    Bs = bank[5]
                    kb.mm(Bs[:, :], C(CS_BLK), osq[:, :], [cst, osq], [Bs])
                    rstd_from_ss(Bs[:, :], Bs, rst, 1.0 / 64, RMS_EPS)
                    kb.stt(DVE, yT[4 + p][:, :], Bo[:, :], pc(PC_SBG + p), rst[:, :], ALU.mult, ALU.mult, [Bo, rst, pcol], [yT[4 + p]])

                ckpt('attn')
                if dbg:
                    for k in range(8):
                        t = wk[14 + (k % 2)]
                        kb.cp(DVE, t[:, :], yT[k][:, :], [yT[k]], [t])
                        kb.dma(dbg_y[k * 128:(k + 1) * 128, tok0:tok0 + G], t[:, :], [t], [], t)

                ckpt('dbgy')
                for i in range(4):
                    xt = xb[i % 2]
                    kb.dma(xt[:, :], x_d[tok0 + i * 128: tok0 + (i + 1) * 128, :], [], [xt], xt)
                    for hf in range(2):
                        bk = nbig()
                        for k in range(8):
                            kb.mm(bk[:, :], yT[k][:, i * 128:(i + 1) * 128], wout[:, k, hf * 512:(hf + 1) * 512],
                                  [yT[k], wout], [bk], start=(k == 0), stop=(k == 7), inc=(k == 7))
                        kb.tt(DVE, x1[i][:, hf * 512:(hf + 1) * 512], bk[:, :], xt[:, hf * 512:(hf + 1) * 512], ALU.add,
                              [bk, xt], [x1[i]])
                    if dbg:
                        kb.dma(dbg_x1[tok0 + i * 128: tok0 + (i + 1) * 128, :], x1[i][:, :], [x1[i]], [], x1[i])

                ckpt('outproj')
                for i in range(4):
                    ln_T(x1[i], i, i)
                for c in range(8):
                    wb = wnext()
                    for fi in range(4):
                        f = 4 * c + fi
                        bk = nbig()
                        for k in range(8):
                            kb.mm(bk[:, :], wb[:, k * 512 + fi * 128: k * 512 + (fi + 1) * 128], hT_h[:, k, :],
                                  [wb, hT], [bk], start=(k == 0), stop=(k == 7), inc=(k == 7))
                        r32 = wk[16 + (f % 2)]
                        kb.act(r32[:, :], bk[:, :], AF.Relu, [bk], [r32])
                        ut = wkb[f // 2]
                        kb.tt(POOL if (f % 2 == 0) else DVE, ut[:, (f % 2) * 512:(f % 2 + 1) * 512], r32[:, :], r32[:, :], ALU.mult,
                              [r32], [ut])
                    wdone()
                for c in range(8):
                    wb = wnext()
                    for fi in range(4):
                        f = 4 * c + fi
                        ut = wkb[f // 2]
                        for i in range(4):
                            for hf in range(2):
                                bk = bank[2 * i + hf]
                                kb.mm(bk[:, :], ut[:, (f % 2) * 512 + i * 128:(f % 2) * 512 + (i + 1) * 128],
                                      wb[:, fi * 1024 + hf * 512: fi * 1024 + (hf + 1) * 512],
                                      [ut, wb], [bk], start=(f == 0), stop=(f == 31), inc=(f == 31 or (fi == 3 and i == 3 and hf == 1)))
                    wdone()
                for i in range(4):
                    for hf in range(2):
                        bk = bank[2 * i + hf]
                        kb.tt(DVE, x1[i][:, hf * 512:(hf + 1) * 512], bk[:, :], x1[i][:, hf * 512:(hf + 1) * 512], ALU.add,
                              [bk, x1[i]], [x1[i]])
                    s = stat[i % 2]
                    h = hn[i % 2]
                    kb.memset(POOL, s[:, 0:1], 0.0, [s])
                    kb.act(h[:, :], x1[i][:, :], AF.Square, [x1[i], s], [h, s], accum_out=s[:, 0:1])
                    kb.act(s[:, 1:2], s[:, 0:1], AF.Ln, [s], [s], scale=1.0 / D, bias=RMS_EPS)
                    kb.act(s[:, 2:3], s[:, 1:2], AF.Exp, [s], [s], scale=-0.5)
                    kb.stt(DVE, x1[i][:, :], x1[i][:, :], s[:, 2:3], lnf[:, :], ALU.mult, ALU.mult, [x1[i], s, lnf], [x1[i]])
                    kb.dma(out_d[tok0 + i * 128: tok0 + (i + 1) * 128, :], x1[i][:, :], [x1[i]], [], x1[i])


def _consts():
    c = np.zeros((NCS, 128, 128), np.float32)
    i = np.arange(128)
    c[CS_ID] = np.eye(128)
    c[CS_TRI] = (i[:, None] >= i[None, :])
    c[CS_ONES] = 1.0
    c[CS_MATT] = (i[:, None] < i[None, :])
    blk = (i[:, None] // 64) == (i[None, :] // 64)
    c[CS_BLK] = blk
    c[CS_UTS] = blk & (i[:, None] < i[None, :])
    c[CS_LTS] = blk & (i[:, None] > i[None, :])
    c[CS_UTI] = blk & (i[:, None] <= i[None, :])
    return np.ascontiguousarray(c.transpose(1, 0, 2).reshape(128, NCS * 128))


def _host_inputs(inp):
    f = lambda a: np.asarray(a, np.float32)
    cols = []
    cols.append(f(inp["ln1_g"])[0].reshape(8, 128))
    cols.append(f(inp["ln2_g"])[0].reshape(8, 128))
    cols.append(f(inp["tok_mu"])[0].reshape(14, 128))
    for k in ("w0", "a0", "k_k", "k_a"):
        cols.append(f(inp[k])[0].reshape(4, 128))
    cols.append(f(inp["r_k"])[0].reshape(4, 128))
    for k in ("gn_w", "gn_b", "sb_gain"):
        cols.append(f(inp[k])[0].reshape(4, 128))
    pcol = np.ascontiguousarray(np.concatenate(cols, axis=0).T)
    assert pcol.shape == (128, NPC)
    wlo = np.ascontiguousarray(np.concatenate([f(inp["w_decay_up"])[0], f(inp["w_aaa_up"])[0]], axis=0))
    shared = {
        "w_in": np.ascontiguousarray(f(inp["w_in"])[0]),
        "w_out": np.ascontiguousarray(f(inp["w_out"])[0]),
        "w_up": np.ascontiguousarray(f(inp["w_up"])[0]),
        "w_down": np.ascontiguousarray(f(inp["w_down"])[0]),
        "wlo": wlo,
        "wg": np.ascontiguousarray(f(inp["w_gate_up"])[0]),
        "pcol": pcol,
        "lnf": np.ascontiguousarray(f(inp["lnf_g"]).reshape(1, D)),
        "cst": _consts(),
    }
    return shared


def kernel(**inputs):
    x = np.asarray(inputs["x"], np.float32)
    B = x.shape[0]
    nseq = B // NCORES
    shared = _host_inputs(inputs)
    nc = build(nseq=nseq)
    in_maps = []
    for c in range(NCORES):
        m = dict(shared)
        m["x"] = np.ascontiguousarray(x[c * nseq:(c + 1) * nseq].reshape(nseq * T, D))
        in_maps.append(m)
    res = run_bass_kernel_spmd(nc, in_maps, core_ids=list(range(NCORES)))
    outs = [np.asarray(r["out"]).reshape(nseq, T, D) for r in res.results]
    return np.concatenate(outs, axis=0).astype(np.float32)
```

```python
import numpy as np
from contextlib import ExitStack
import concourse.bass as bass
import concourse.mybir as mybir
from concourse.bass_utils import run_bass_kernel_spmd

F32 = mybir.dt.float32
BF16 = mybir.dt.bfloat16
AF = mybir.ActivationFunctionType
ALU = mybir.AluOpType

NCORES = 8
D = 1024
T = 2048
G = 512
NG = T // G
DFF = 4096
C0 = 0.6065306597126334
RMS_EPS = 1e-5
GN_EPS = 64e-5

PC_LN1, PC_LN2, PC_MU, PC_W0, PC_A0, PC_KK, PC_KA, PC_RK, PC_GNW, PC_GNB, PC_SBG = 0, 8, 16, 30, 34, 38, 42, 46, 50, 54, 58
NPC = 62
CS_ID, CS_TRI, CS_ONES, CS_MATT, CS_BLK, CS_UTS, CS_LTS, CS_UTI = range(8)
NCS = 8


class Atom:
    __slots__ = ("w", "r", "x")

    def __init__(self, x=False):
        self.w = {}
        self.r = {}
        self.x = x


class Tl:
    def __init__(self, ap, atoms=None):
        self.ap = ap
        self.atoms = atoms if atoms is not None else [Atom()]

    def __getitem__(self, k):
        return self.ap[k]


class Eng:
    def __init__(self, e, sem, is_pe=False):
        self.e = e
        self.sem = sem
        self.cnt = 0
        self.waited = {}
        self.is_pe = is_pe


def _atoms(lst):
    out = []
    for t in lst:
        if isinstance(t, Tl):
            out.extend(t.atoms)
        elif isinstance(t, Atom):
            out.append(t)
        else:
            out.extend(_atoms(t))
    return out


class KB:
    def __init__(self, nc, es):
        self.nc = nc
        self.es = es
        self.PE = Eng(nc.tensor, es.enter_context(nc.semaphore("s_pe")), True)
        self.ACT = Eng(nc.scalar, es.enter_context(nc.semaphore("s_act")))
        self.DVE = Eng(nc.vector, es.enter_context(nc.semaphore("s_dve")))
        self.POOL = Eng(nc.gpsimd, es.enter_context(nc.semaphore("s_pool")))
        self.SP = Eng(nc.sync, None)
        self.dkeys = []
        self.hist = {}
        self.ninstr = 0

    def _need(self, eng, reads, writes):
        need = {}
        for a in reads:
            for s, v in a.w.items():
                if need.get(s, 0) < v:
                    need[s] = v
        for a in writes:
            for s, v in a.w.items():
                if need.get(s, 0) < v:
                    need[s] = v
            for s, v in a.r.items():
                if need.get(s, 0) < v:
                    need[s] = v
        for s, v in sorted(need.items(), key=lambda kv: -kv[1]):
            if eng.is_pe and s is eng.sem:
                continue
            if eng.waited.get(s, 0) >= v:
                continue
            eng.e.wait_ge(s, v)
            eng.waited[s] = v
            self.ninstr += 1
            snap = self.hist.get((s, v))
            if snap:
                w = eng.waited
                for s2, v2 in snap.items():
                    if w.get(s2, 0) < v2:
                        w[s2] = v2

    def op(self, eng, fn, reads, writes, inc=True):
        reads = _atoms(reads)
        writes = _atoms(writes)
        xr = [a for a in reads if a.x]
        if xr:
            writes = writes + xr
        self._need(eng, reads, writes)
        ins = fn()
        self.ninstr += 1
        if inc:
            eng.cnt += 1
            ins.then_inc(eng.sem, 1)
            val = eng.cnt
            snap = dict(eng.waited)
            if not eng.is_pe:
                snap[eng.sem] = val - 1
            self.hist[(eng.sem, val)] = snap
        else:
            val = eng.cnt + 1
        s = eng.sem
        for a in reads:
            if a.r.get(s, 0) < val:
                a.r[s] = val
        for a in writes:
            if a.w.get(s, 0) < val:
                a.w[s] = val

    def dma(self, out, in_, reads, writes, key, q=None, **kw):
        q = q or self.SP
        reads = _atoms(reads)
        writes = _atoms(writes)
        if not hasattr(key, "dsem"):
            key.dsem = self.es.enter_context(self.nc.semaphore("s_dma%d" % len(self.dkeys)))
            key.dcnt = 0
            self.dkeys.append(key)
        s = key.dsem
        self._need(q, reads, writes)
        if q.waited.get(s, 0) < key.dcnt:
            q.e.wait_ge(s, key.dcnt)
            q.waited[s] = key.dcnt
        key.dcnt += 16
        val = key.dcnt
        q.e.dma_start(out=out, in_=in_, **kw).then_inc(s, 16)
        self.ninstr += 1
        self.hist[(s, val)] = dict(q.waited)
        for a in reads:
            if a.r.get(s, 0) < val:
                a.r[s] = val
        for a in writes:
            if a.w.get(s, 0) < val:
                a.w[s] = val

    def finish(self):
        for k in self.dkeys:
            if self.SP.waited.get(k.dsem, 0) < k.dcnt:
                self.nc.sync.wait_ge(k.dsem, k.dcnt)

    def mm(self, out, lhsT, rhs, reads, writes, start=True, stop=True, inc=True):
        nc = self.nc
        self.op(self.PE, lambda: nc.tensor.matmul(out, lhsT, rhs, start=start, stop=stop), reads, writes, inc)

    def tr(self, out, in_, ident, reads, writes, inc=True):
        nc = self.nc
        self.op(self.PE, lambda: nc.tensor.transpose(out, in_, ident), reads, writes, inc)

    def act(self, out, in_, func, reads, writes, **kw):
        nc = self.nc
        self.op(self.ACT, lambda: nc.scalar.activation(out=out, in_=in_, func=func, **kw), reads, writes)

    def tt(self, eng, out, in0, in1, op, reads, writes):
        self.op(eng, lambda: eng.e.tensor_tensor(out=out, in0=in0, in1=in1, op=op), reads, writes)

    def ts(self, eng, out, in0, s1, s2, op0, op1, reads, writes):
        if s2 is None:
            self.op(eng, lambda: eng.e.tensor_scalar(out=out, in0=in0, scalar1=s1, scalar2=None, op0=op0), reads, writes)
        else:
            self.op(eng, lambda: eng.e.tensor_scalar(out=out, in0=in0, scalar1=s1, scalar2=s2, op0=op0, op1=op1), reads, writes)

    def stt(self, eng, out, in0, scalar, in1, op0, op1, reads, writes):
        self.op(eng, lambda: eng.e.scalar_tensor_tensor(out=out, in0=in0, scalar=scalar, in1=in1, op0=op0, op1=op1), reads, writes)

    def cp(self, eng, out, in_, reads, writes):
        self.op(eng, lambda: eng.e.tensor_copy(out, in_), reads, writes)

    def memset(self, eng, ap, val, writes):
        self.op(eng, lambda: eng.e.memset(ap, val), [], writes)


def _win_chunks():
    ch = [[(1536, 256)]]
    for p in range(4):
        ch.append([(128 * p, 128), (512 + 128 * p, 128), (1024 + 128 * p, 128)])
    ch.append([(1792, 512)])
    ch.append([(2304, 512)])
    ch.append([(2816, 512)])
    return ch


class StopBuild(Exception):
    pass


def build(nseq=4, dbg=False, stop_at=None):
    nc = bass.Bass("TRN2", target_bir_lowering=False)
    NT = nseq * T
    x_d = nc.dram_tensor("x", [NT, D], F32, kind="ExternalInput").ap()
    win_d = nc.dram_tensor("w_in", [D, 3328], F32, kind="ExternalInput").ap()
    wout_d = nc.dram_tensor("w_out", [D, D], F32, kind="ExternalInput").ap()
    wup_d = nc.dram_tensor("w_up", [D, DFF], F32, kind="ExternalInput").ap()
    wdn_d = nc.dram_tensor("w_down", [DFF, D], F32, kind="ExternalInput").ap()
    wlo_d = nc.dram_tensor("wlo", [128, 512], F32, kind="ExternalInput").ap()
    wg_d = nc.dram_tensor("wg", [128, 512], F32, kind="ExternalInput").ap()
    pcol_d = nc.dram_tensor("pcol", [128, NPC], F32, kind="ExternalInput").ap()
    lnf_d = nc.dram_tensor("lnf", [1, D], F32, kind="ExternalInput").ap()
    cst_d = nc.dram_tensor("cst", [128, NCS * 128], F32, kind="ExternalInput").ap()
    out_d = nc.dram_tensor("out", [NT, D], F32, kind="ExternalOutput").ap()
    wsc_d = nc.dram_tensor("wscr", [24, 128, 4096], BF16).ap()
    if dbg:
        dbg_y = nc.dram_tensor("dbg_y", [D, NT], F32, kind="ExternalOutput").ap()
        dbg_x1 = nc.dram_tensor("dbg_x1", [NT, D], F32, kind="ExternalOutput").ap()

    es = ExitStack()
    with es:
        kb = KB(nc, es)
        PE, ACT, DVE, POOL = kb.PE, kb.ACT, kb.DVE, kb.POOL

        def ckpt(name):
            if stop_at is not None and name == stop_at:
                raise StopBuild()
        try:
            _body(nc, es, kb, nseq, dbg, ckpt, locals())
        except StopBuild:
            pass
        kb.finish()
        print("instructions emitted:", kb.ninstr, "pe", PE.cnt, "act", ACT.cnt, "dve", DVE.cnt, "pool", POOL.cnt)
    return nc


def _body(nc, es, kb, nseq, dbg, ckpt, env):
    globals_ = env
    x_d, win_d, wout_d, wup_d, wdn_d, wlo_d, wg_d, pcol_d, lnf_d, cst_d, out_d, wsc_d = (env[k] for k in (
        "x_d", "win_d", "wout_d", "wup_d", "wdn_d", "wlo_d", "wg_d", "pcol_d", "lnf_d", "cst_d", "out_d", "wsc_d"))
    dbg_y = env.get("dbg_y")
    dbg_x1 = env.get("dbg_x1")
    PE, ACT, DVE, POOL = kb.PE, kb.ACT, kb.DVE, kb.POOL
    if True:

        def sb(name, shape, dt, natoms=1):
            h = es.enter_context(nc.sbuf_tensor("sb_" + name, shape, dt))
            return h

        cst = Tl(sb("cstb", [128, NCS * 128], BF16)[:])
        pcol = Tl(sb("pcol", [128, NPC], F32)[:])
        lnf = Tl(sb("lnfb", [128, D], F32)[:])
        wlo = Tl(sb("wlo", [128, 512], BF16)[:])
        wgt = Tl(sb("wgt", [128, 512], BF16)[:])
        wout = Tl(sb("wout", [128, 8, D], BF16)[:])
        NWB = 2
        wbuf = [Tl(sb("wbuf%d" % i, [128, 4096], BF16)[:]) for i in range(NWB)]
        xb = [Tl(sb("xb%d" % i, [128, D], F32)[:]) for i in range(2)]
        hn = [Tl(sb("hn%d" % i, [128, D], BF16)[:]) for i in range(1)] * 2
        stat = [Tl(sb("stat%d" % i, [128, 4], F32)[:]) for i in range(2)]
        hT_h = sb("hT", [128, 8, G], BF16)
        hT = Tl(hT_h[:], [Atom() for _ in range(4)])
        x1 = [Tl(sb("x1_%d" % i, [128, D], F32)[:]) for i in range(4)]
        yT = [Tl(sb("yT%d" % k, [128, G], BF16)[:]) for k in range(8)]
        kT = [Tl(sb("kT%d" % p, [128, T], BF16)[:], [Atom() for _ in range(NG)]) for p in range(4)]
        vtok = [Tl(sb("vtok%d" % i, [128, 512], BF16)[:]) for i in range(16)]
        qs = [Tl(sb("qs%d" % p, [128, G], BF16)[:]) for p in range(4)]
        qn = [Tl(sb("qn%d" % p, [128, G], BF16)[:]) for p in range(4)]
        NWK = 18
        wk_h = [sb("wk%d" % i, [128, 512], F32) for i in range(NWK)]
        wk = [Tl(h[:]) for h in wk_h]
        wkb = [Tl(h[:].bitcast(BF16), wk[i].atoms) for i, h in enumerate(wk_h)]
        rkv = [[Tl(sb("rkv%d_%d" % (b, j), [128, 512], F32)[:]) for j in range(3)] for b in range(1)]
        pst = [Tl(sb("pst%d" % i, [128, 513], F32)[:]) for i in range(1)] * 2
        carry = Tl(sb("carry", [128, 14], F32)[:])
        txa = Tl(sb("txa", [128, 512], BF16)[:])
        sgx = Tl(sb("sgx", [128, 512], BF16)[:])
        pb = [[Tl(sb("pb%d_%d" % (b, j), [128, 512], BF16)[:]) for j in range(5)] for b in range(1)]
        tmpb = Tl(sb("tmpb", [128, 512], BF16)[:])
        zt = Tl(sb("zt", [128, 64], BF16)[:])
        NBD = 36
        bdt = [Tl(sb("bd%d" % i, [128, 128], BF16)[:]) for i in range(NBD)]
        bdl = [[Tl(sb("bdl%d_%d" % (j, i), [128, 128], BF16)[:]) for i in range(9)] for j in range(8)]
        PCS = [Tl(sb("pcs%d" % i, [128, 8], F32)[:]) for i in range(2)]
        H32 = [Tl(sb("H32_%d" % p, [128, 128], F32)[:]) for p in range(4)]
        HP = [Tl(sb("HP_%d" % p, [128, 128], F32)[:]) for p in range(2)]
        Hb = [Tl(sb("Hb_%d" % p, [128, 128], BF16)[:]) for p in range(4)]
        bank_h = [es.enter_context(nc.psum_tensor("bank%d" % i, [128, 512], F32)) for i in range(8)]
        bank_atoms = [[Atom(True)] for _ in range(8)]
        bank = [Tl(bank_h[i][:], bank_atoms[i]) for i in range(8)]
        small = []
        for q in range(4):
            for b in range(4, 8):
                t_ = Tl(bank_h[b][:, q * 128:(q + 1) * 128], [bank_atoms[b][0]])
                t_.bfv = bank_h[b][:].bitcast(BF16)[:, q * 256:q * 256 + 128]
                small.append(t_)
        st = {"bd": 0, "small": 0, "big": 0}

        def nbd():
            t = bdt[st["bd"] % NBD]
            st["bd"] += 1
            return t

        def nsmall():
            t = small[st["small"] % len(small)]
            st["small"] += 1
            return t

        def nbig():
            t = bank[st["big"] % 4]
            st["big"] += 1
            return t

        def C(i):
            return cst[:, i * 128:(i + 1) * 128]

        def pc(col):
            return pcol[:, col:col + 1]

        kb.dma(wk[14][:, :], cst_d[:, 0:512], [], [wk[14]], wk[14])
        kb.dma(wk[15][:, :], cst_d[:, 512:1024], [], [wk[15]], wk[15])
        kb.memset(POOL, zt[:, :], 0.0, [zt])
        kb.dma(pcol[:, :], pcol_d[:, :], [], [pcol], pcol)
        kb.dma(lnf[:, :], lnf_d[0:1, :].partition_broadcast(128), [], [lnf], lnf)
        kb.cp(DVE, cst[:, 0:512], wk[14][:, :], [wk[14]], [cst])
        kb.cp(DVE, cst[:, 512:1024], wk[15][:, :], [wk[15]], [cst])
        kb.dma(wk[0][:, :], wlo_d[:, :], [], [wk[0]], wk[0])
        kb.dma(wk[1][:, :], wg_d[:, :], [], [wk[1]], wk[1])
        kb.cp(DVE, wlo[:, :], wk[0][:, :], [wk[0]], [wlo])
        kb.cp(DVE, wgt[:, :], wk[1][:, :], [wk[1]], [wgt])

        ckpt('consts')
        scr = [Tl(wsc_d[i], [Atom()]) for i in range(24)]
        cast_rr = [0]
        cast_engs = [DVE, POOL, ACT]

        def cast(out_ap, in_tl, in_ap, scale_col, out_tl):
            e = cast_engs[cast_rr[0] % 3]
            cast_rr[0] += 1
            if scale_col is None:
                if e is ACT:
                    kb.act(out_ap, in_ap, AF.Copy, [in_tl], [out_tl])
                else:
                    kb.cp(e, out_ap, in_ap, [in_tl], [out_tl])
            else:
                if e is ACT:
                    kb.act(out_ap, in_ap, AF.Copy, [in_tl, pcol], [out_tl], scale=pc(scale_col))
                else:
                    kb.ts(e, out_ap, in_ap, pc(scale_col), None, ALU.mult, None, [in_tl, pcol], [out_tl])

        stg_rr = [0]

        def stage():
            t = wk[2 + (stg_rr[0] % 12)]
            t.q = kb.SP
            stg_rr[0] += 1
            return t

        wchunks = _win_chunks()
        wb_rr = [0]
        big_stages = x1 + xb
        bst = [0]

        def stage4():
            t = big_stages[bst[0] % len(big_stages)]
            bst[0] += 1
            return t

        for ci, ch in enumerate(wchunks):
            ncols = sum(n for _, n in ch)
            wb = wbuf[wb_rr[0] % NWB]
            wb_rr[0] += 1
            for k0 in range(0, 8, 2):
                s = stage4()
                sv = s[:, 0:2 * ncols].rearrange("p (k c) -> p k c", k=2)
                off = 0
                for (c0, n) in ch:
                    kb.dma(sv[:, :, off:off + n],
                           win_d[k0 * 128:(k0 + 2) * 128, c0:c0 + n].rearrange("(k p) c -> p k c", k=2), [], [s], s)
                    off += n
                for kk in range(2):
                    cast(wb[:, (k0 + kk) * ncols:(k0 + kk + 1) * ncols], s, s[:, kk * ncols:(kk + 1) * ncols],
                         PC_LN1 + k0 + kk, wb)
            kb.dma(wsc_d[ci][:, 0:8 * ncols], wb[:, 0:8 * ncols], [wb], [scr[ci]], wb)
        for c in range(8):
            wb = wbuf[wb_rr[0] % NWB]
            wb_rr[0] += 1
            for k0 in range(0, 8, 2):
                s = stage4()
                kb.dma(s[:, :].rearrange("p (k c) -> p k c", k=2),
                       wup_d[k0 * 128:(k0 + 2) * 128, c * 512:(c + 1) * 512].rearrange("(k p) c -> p k c", k=2), [], [s], s)
                for kk in range(2):
                    cast(wb[:, (k0 + kk) * 512:(k0 + kk + 1) * 512], s, s[:, kk * 512:(kk + 1) * 512], PC_LN2 + k0 + kk, wb)
            kb.dma(wsc_d[8 + c][:, :], wb[:, :], [wb], [scr[8 + c]], wb)
        for c in range(8):
            wb = wbuf[wb_rr[0] % NWB]
            wb_rr[0] += 1
            for fi in range(4):
                s = stage4()
                r0 = c * 512 + fi * 128
                kb.dma(s[:, :], wdn_d[r0:r0 + 128, :], [], [s], s)
                cast(wb[:, fi * 1024:(fi + 1) * 1024], s, s[:, :], None, wb)
            kb.dma(wsc_d[16 + c][:, :], wb[:, :], [wb], [scr[16 + c]], wb)
        for k in range(8):
            s = stage4()
            kb.dma(s[:, :], wout_d[k * 128:(k + 1) * 128, :], [], [s], s)
            cast(wout[:, k, :], s, s[:, :], None, wout)

        ckpt('casts')
        stream = []
        for _ in range(nseq * NG):
            for ci, ch in enumerate(wchunks):
                stream.append((ci, 8 * sum(n for _, n in ch)))
            for c in range(8):
                stream.append((8 + c, 4096))
            for c in range(8):
                stream.append((16 + c, 4096))
        sst = {"issued": 0, "used": 0}

        def wprefetch():
            while sst["issued"] < len(stream) and sst["issued"] < sst["used"] + NWB:
                i = sst["issued"]
                ci, n = stream[i]
                wb = wbuf[(wb_rr[0] + i) % NWB]
                kb.dma(wb[:, 0:n], wsc_d[ci][:, 0:n], [scr[ci]], [wb], wb)
                sst["issued"] += 1

        def wnext():
            wprefetch()
            i = sst["used"]
            wb = wbuf[(wb_rr[0] + i) % NWB]
            sst["used"] += 1
            return wb

        def wdone():
            wprefetch()

        def ln_T(src, i, sidx):
            h = hn[sidx % 2]
            s = stat[sidx % 2]
            kb.memset(POOL, s[:, 0:1], 0.0, [s])
            kb.act(h[:, :], src[:, :], AF.Square, [src, s], [h, s], accum_out=s[:, 0:1])
            kb.act(s[:, 1:2], s[:, 0:1], AF.Ln, [s], [s], scale=1.0 / D, bias=RMS_EPS)
            kb.act(s[:, 2:3], s[:, 1:2], AF.Exp, [s], [s], scale=-0.5)
            kb.ts(DVE, h[:, :], src[:, :], s[:, 2:3], None, ALU.mult, None, [src, s], [h])
            bk = nbig()
            bv = bk.ap.bitcast(BF16)
            for k in range(8):
                kb.tr(bv[:, k * 128:(k + 1) * 128], h[:, k * 128:(k + 1) * 128], C(CS_ID), [h, cst], [bk], inc=(k == 7))
            kb.cp(DVE, hT_h[:, :, i * 128:(i + 1) * 128],
                  bv[:, :].rearrange("p (k c) -> p k c", k=8), [bk], [hT.atoms[i]])
            return s

        def rstd_from_ss(ps_ap, ps_tl, out_tl, scale, eps):
            kb.act(out_tl[:, :], ps_ap, AF.Ln, [ps_tl], [out_tl], scale=scale, bias=eps)
            kb.act(out_tl[:, :], out_tl[:, :], AF.Exp, [out_tl], [out_tl], scale=-0.5)

        for seq in range(nseq):
            kb.memset(POOL, carry[:, :], 0.0, [carry])
            for p in range(4):
                kb.memset(POOL, H32[p][:, :], 0.0, [H32[p]])
                kb.memset(POOL, Hb[p][:, :], 0.0, [Hb[p]])
            for g in range(NG):
                tok0 = seq * T + g * G
                for i in range(4):
                    xt = xb[i % 2]
                    kb.dma(xt[:, :], x_d[tok0 + i * 128: tok0 + (i + 1) * 128, :], [], [xt], xt)
                    ln_T(xt, i, i)

                ckpt('ln1')
                def inproj_fm(wb, col0, ncols_chunk):
                    bk = nbig()
                    for k in range(8):
                        kb.mm(bk[:, :], wb[:, k * ncols_chunk + col0: k * ncols_chunk + col0 + 128], hT_h[:, k, :],
                              [wb, hT], [bk], start=(k == 0), stop=(k == 7), inc=(k == 7))
                    return bk

                def shiftmix(bk, ct, dst_ap, dst_tl):
                    P = pst[ct % 2]
                    kb.act(P[:, 1:513], bk[:, :], AF.Copy, [bk], [P])
                    kb.cp(POOL, P[:, 0:1], carry[:, ct:ct + 1], [carry], [P])
                    d = wk[17]
                    kb.tt(DVE, d[:, :], P[:, 0:512], P[:, 1:513], ALU.subtract, [P], [d])
                    kb.stt(DVE, dst_ap, d[:, :], pc(PC_MU + ct), P[:, 1:513], ALU.mult, ALU.add, [d, P, pcol], [dst_tl])
                    kb.cp(POOL, carry[:, ct:ct + 1], P[:, 512:513], [P], [carry])

                wb = wnext()
                bk = inproj_fm(wb, 0, 256)
                t12 = wk[16]
                shiftmix(bk, 12, t12[:, :], t12)
                kb.act(txa[0:64, :], t12[0:64, :], AF.Tanh, [t12], [txa])
                kb.cp(POOL, txa[64:128, :], t12[64:128, :], [t12], [txa])
                bk = inproj_fm(wb, 128, 256)
                wdone()
                t13 = wk[16]
                shiftmix(bk, 13, t13[:, :], t13)
                kb.act(sgx[:, :], t13[:, :], AF.Sigmoid, [t13], [sgx])

                ckpt('chunk0')
                PP = {}

                def prep_gen(p):
                        wb = wnext()
                        R_, K_, V_ = rkv[0]
                        for j, dst in enumerate((R_, K_, V_)):
                            bk = inproj_fm(wb, 128 * j, 384)
                            shiftmix(bk, 4 * j + p, dst[:, :], dst)
                            yield
                        wdone()
                        Ab, Rb, Bb, Kb_, Vb = pb[0]
                        cols = slice(128 * p, 128 * (p + 1))
                        b1 = nbig()
                        kb.mm(b1[:, :], wlo[0:64, cols], txa[0:64, :], [wlo, txa], [b1])
                        sgm = wk[0]
                        kb.act(sgm[:, :], b1[:, :], AF.Sigmoid, [b1, pcol], [sgm], bias=pc(PC_W0 + p))
                        b2 = nbig()
                        kb.mm(b2[:, :], wlo[64:128, cols], txa[64:128, :], [wlo, txa], [b2])
                        lr = wk[1]
                        kb.act(lr[:, :], b2[:, :], AF.Sigmoid, [b2, pcol], [lr], bias=pc(PC_A0 + p))
                        yield
                        cs = wk[2]
                        for j in range(8):
                            sl = slice(64 * j, 64 * j + 64)
                            kb.op(DVE, lambda sl=sl: nc.vector.tensor_tensor_scan(out=cs[:, sl], data0=sgm[:, sl], data1=sgm[:, sl],
                                                                                      initial=0.0, op0=ALU.add, op1=ALU.bypass),
                                  [sgm], [cs])
                        yield
                        kkr = wk[3]
                        kb.act(kkr[:, :], K_[:, :], AF.Copy, [K_, pcol], [kkr], scale=pc(PC_KK + p))
                        kb.act(tmpb[:, :], kkr[:, :], AF.Square, [kkr], [tmpb])
                        b4 = nbig()
                        kb.mm(b4[:, :], C(CS_BLK), tmpb[:, :], [cst, tmpb], [b4])
                        rn = wk[4]
                        kb.ts(DVE, rn[:, :], b4[:, :], 1e-24, None, ALU.max, None, [b4], [rn])
                        kb.act(rn[:, :], rn[:, :], AF.Ln, [rn], [rn])
                        kb.act(rn[:, :], rn[:, :], AF.Exp, [rn], [rn], scale=-0.5)
                        kb.tt(POOL, kkr[:, :], kkr[:, :], rn[:, :], ALU.mult, [kkr, rn], [kkr])
                        yield
                        Ep, Em, Epv = wk[5], wk[6], wk[7]
                        kb.act(Ep[:, :], cs[:, :], AF.Exp, [cs], [Ep], scale=-C0)
                        kb.act(Em[:, :], cs[:, :], AF.Exp, [cs], [Em], scale=C0)
                        kb.tt(POOL, Epv[:, :], cs[:, :], sgm[:, :], ALU.subtract, [cs, sgm], [Epv])
                        kb.act(Epv[:, :], Epv[:, :], AF.Exp, [Epv], [Epv], scale=-C0)
                        yield
                        kb.stt(DVE, Ab[:, :], kkr[:, :], -1.0, Epv[:, :], ALU.mult, ALU.mult, [kkr, Epv], [Ab])
                        kb.tt(POOL, Rb[:, :], R_[:, :], Ep[:, :], ALU.mult, [R_, Ep], [Rb])
                        bv_ = wk[8]
                        kb.tt(DVE, bv_[:, :], kkr[:, :], lr[:, :], ALU.mult, [kkr, lr], [bv_])
                        kb.tt(POOL, Bb[:, :], bv_[:, :], Em[:, :], ALU.mult, [bv_, Em], [Bb])
                        kb.ts(DVE, lr[:, :], lr[:, :], -1.0, pc(PC_KA + p), ALU.add, ALU.mult, [lr, pcol], [lr])
                        yield
                        km = wk[9]
                        kb.stt(DVE, km[:, :], lr[:, :], 1.0, K_[:, :], ALU.add, ALU.mult, [lr, K_], [km])
                        kb.tt(POOL, Kb_[:, :], km[:, :], Em[:, :], ALU.mult, [km, Em], [Kb_])
                        kb.act(Vb[:, :], V_[:, :], AF.Copy, [V_], [Vb])
                        kb.stt(DVE, tmpb[:, :], R_[:, :], pc(PC_RK + p), km[:, :], ALU.mult, ALU.mult, [R_, km, pcol], [tmpb])
                        b5 = nbig()
                        kb.mm(b5[:, :], C(CS_BLK), tmpb[:, :], [cst, tmpb], [b5])
                        bonus = wk[10] if p % 2 == 0 else wk[15]
                        kb.tt(DVE, bonus[:, :], V_[:, :], b5[:, :], ALU.mult, [V_, b5], [bonus])
                        pcs = PCS[p % 2]
                        kb.cp(POOL, pcs[:, :], Ep[:, 63:512:64], [Ep], [pcs])
                        PP[p] = dict(bonus=bonus, pcs=pcs)
                        yield

                for _ in prep_gen(0):
                    pass
                CTX = [[dict() for _ in range(8)] for _ in range(4)]

                def seqgen(*gens):
                    for g_ in gens:
                        yield from g_

                for p in range(4):
                    bdm = C(CS_BLK).rearrange("p (a b) -> p a b", a=2)
                    pcs = PP[p]["pcs"]
                    bonus = PP[p]["bonus"]
                    Ab, Rb, Bb, Kb_, Vb = pb[0]
                    yraw = wk[12]
                    cols = slice(128 * p, 128 * (p + 1))
                    ctxs = CTX[p]

                    def pre_gen(jl, pp):
                        ctxs = CTX[pp]
                        def mkbd(src, t, sl):
                            kb.tt(POOL, t[:, :].rearrange("p (a b) -> p a b", a=2),
                                  src[:, sl].unsqueeze(1).broadcast_to([128, 2, 64]), bdm, ALU.mult, [src, cst], [t])
                            return t
                        for j in jl:
                            c = ctxs[j]
                            sl = slice(64 * j, 64 * j + 64)
                            LL = bdl[j]
                            c["AT"] = mkbd(Ab, LL[0], sl)
                            c["RT"] = mkbd(Rb, LL[1], sl)
                            c["BT"] = mkbd(Bb, nbd(), sl)
                            c["KT"] = mkbd(Kb_, nbd(), sl)
                            c["VT"] = mkbd(Vb, nbd(), sl)
                        yield
                        for j in jl:
                            c = ctxs[j]
                            LL = bdl[j]
                            for nm, src, dst in (("Btok", c["BT"], LL[2]), ("Ktok", c["KT"], LL[3]), ("Vtok", c["VT"], LL[4])):
                                ps = nsmall()
                                kb.tr(ps.bfv, src[:, :], C(CS_ID), [src, cst], [ps])
                                kb.cp(DVE, dst[:, :], ps.bfv, [ps], [dst])
                                c[nm] = dst
                        yield

                        def gram(l, r, mask, t):
                            ps = nsmall()
                            kb.mm(ps[:, :], l[:, :], r[:, :], [l, r], [ps])
                            kb.tt(DVE, t[:, :], ps[:, :], C(mask), ALU.mult, [ps, cst], [t])
                            return t
                        for j in jl:
                            c = ctxs[j]
                            LL = bdl[j]
                            c["M"] = gram(c["BT"], c["AT"], CS_UTS, nbd())
                            c["N"] = gram(c["AT"], c["BT"], CS_LTS, nbd())
                            c["Aak"] = gram(c["KT"], c["AT"], CS_UTS, LL[5])
                        yield
                        for j in jl:
                            c = ctxs[j]
                            LL = bdl[j]
                            c["Arb"] = gram(c["BT"], c["RT"], CS_UTI, LL[6])
                            c["Ark"] = gram(c["KT"], c["RT"], CS_UTI, LL[7])
                            X = nbd()
                            kb.tt(POOL, X[:, :], c["M"][:, :], C(CS_ID), ALU.add, [c["M"], cst], [X])
                            c["X"] = X
                        yield
                        for lev in range(1, 6):
                            for j in jl:
                                c = ctxs[j]
                                psn = nsmall()
                                kb.mm(psn[:, :], c["M"][:, :], c["N"][:, :], [c["M"], c["N"]], [psn])
                                Nn = nbd()
                                kb.act(Nn[:, :], psn[:, :], AF.Copy, [psn], [Nn])
                                c["Nn"] = Nn
                            yield
                            if lev < 5:
                                for j in jl:
                                    c = ctxs[j]
                                    psm = nsmall()
                                    kb.mm(psm[:, :], c["N"][:, :], c["M"][:, :], [c["M"], c["N"]], [psm])
                                    Mn = nbd()
                                    kb.act(Mn[:, :], psm[:, :], AF.Copy, [psm], [Mn])
                                    c["Mn"] = Mn
                                yield
                            for j in jl:
                                c = ctxs[j]
                                psx = nsmall()
                                kb.mm(psx[:, :], c["Nn"][:, :], c["X"][:, :], [c["Nn"], c["X"]], [psx])
                                Xn = nbd() if lev < 5 else bdl[j][8]
                                kb.tt(DVE, Xn[:, :], psx[:, :], c["X"][:, :], ALU.add, [psx, c["X"]], [Xn])
                                c["X"] = Xn
                                c["N"] = c["Nn"]
                                c["M"] = c.get("Mn") if lev < 5 else None
                            yield

                    def chain_gen(jl):
                        for j in jl:
                            c = ctxs[j]
                            sl = slice(64 * j, 64 * j + 64)
                            hp = HP[j % 2]
                            kb.act(hp[:, :], H32[p][:, :], AF.Copy, [H32[p], pcs], [hp], scale=pcs[:, j:j + 1])
                            psw = nsmall()
                            kb.mm(psw[:, :], c["AT"][:, :], Hb[p][:, :], [c["AT"], Hb[p]], [psw], start=True, stop=False, inc=False)
                            kb.mm(psw[:, :], c["Aak"][:, :], c["Vtok"][:, :], [c["Aak"], c["Vtok"]], [psw], start=False, stop=True)
                            Wb = nbd()
                            kb.act(Wb[:, :], psw[:, :], AF.Copy, [psw], [Wb])
                            yield
                            psu = nsmall()
                            kb.mm(psu[:, :], c["X"][:, :], Wb[:, :], [c["X"], Wb], [psu])
                            Ub = nbd()
                            kb.cp(DVE, Ub[:, :], psu[:, :], [psu], [Ub])
                            yield
                            psh = nsmall()
                            kb.mm(psh[:, :], c["Btok"][:, :], Ub[:, :], [c["Btok"], Ub], [psh], start=True, stop=False, inc=False)
                            kb.mm(psh[:, :], c["Ktok"][:, :], c["Vtok"][:, :], [c["Ktok"], c["Vtok"]], [psh], start=False, stop=True)
                            psy = nsmall()
                            kb.mm(psy[:, :], Hb[p][:, :], c["RT"][:, :], [Hb[p], c["RT"]], [psy], start=True, stop=False, inc=False)
                            kb.mm(psy[:, :], Ub[:, :], c["Arb"][:, :], [Ub, c["Arb"]], [psy], start=False, stop=False, inc=False)
                            kb.mm(psy[:, :], c["Vtok"][:, :], c["Ark"][:, :], [c["Vtok"], c["Ark"]], [psy], start=False, stop=True)
                            kb.stt(DVE, H32[p][:, :], psh[:, :], pcs[:, j:j + 1], hp[:, :], ALU.mult, ALU.add,
                                   [psh, pcs, hp], [H32[p]])
                            kb.act(Hb[p][:, :], H32[p][:, :], AF.Copy, [H32[p]], [Hb[p]])
                            yield
                            kb.act(yraw[0:64, sl], psy[0:64, 0:64], AF.Copy, [psy], [yraw])
                            kb.act(yraw[64:128, sl], psy[64:128, 64:128], AF.Copy, [psy], [yraw])
                            yield

                    def run_il(*gens):
                        gens = list(gens)
                        while gens:
                            for g_ in list(gens):
                                try:
                                    next(g_)
                                except StopIteration:
                                    gens.remove(g_)

                    if p == 0:
                        run_il(pre_gen([0, 1, 2, 3], 0))
                    run_il(chain_gen([0, 1, 2, 3]), pre_gen([4, 5, 6, 7], p))
                    if p < 3:
                        run_il(chain_gen([4, 5, 6, 7]), seqgen(prep_gen(p + 1), pre_gen([0, 1, 2, 3], p + 1)))
                    else:
                        run_il(chain_gen([4, 5, 6, 7]))
                    ckpt('chunks')
                    kb.act(tmpb[:, :], yraw[:, :], AF.Copy, [yraw], [tmpb])
                    b6 = nbig()
                    kb.mm(b6[:, :], C(CS_BLK), tmpb[:, :], [cst, tmpb], [b6])
                    yc = wk[13]
                    kb.stt(DVE, yc[:, :], b6[:, :], -1.0 / 64, yraw[:, :], ALU.mult, ALU.add, [b6, yraw], [yc])
                    kb.act(tmpb[:, :], yc[:, :], AF.Square, [yc], [tmpb])
                    b7 = nbig()
                    kb.mm(b7[:, :], C(CS_BLK), tmpb[:, :], [cst, tmpb], [b7])
                    rs = wk[14]
                    rstd_from_ss(b7[:, :], b7, rs, 1.0 / 64, GN_EPS)
                    kb.tt(DVE, yc[:, :], yc[:, :], rs[:, :], ALU.mult, [yc, rs], [yc])
                    kb.ts(POOL, yc[:, :], yc[:, :], pc(PC_GNW + p), pc(PC_GNB + p), ALU.mult, ALU.add, [yc, pcol], [yc])
                    kb.tt(DVE, yc[:, :], yc[:, :], bonus[:, :], ALU.add, [yc, bonus], [yc])
                    b3 = nbig()
                    kb.mm(b3[:, :], wgt[:, cols], sgx[:, :], [wgt, sgx], [b3])
                    kb.tt(DVE, yT[p][:, :], yc[:, :], b3[:, :], ALU.mult, [yc, b3], [yT[p]])

                ckpt('rwkv')
                wb = wnext()
                for p in range(4):
                    bk = inproj_fm(wb, 128 * p, 512)
                    kb.act(qs[p][:, :], bk[:, :], AF.Copy, [bk], [qs[p]], scale=0.125)
                wdone()
                wb = wnext()
                for p in range(4):
                    bk = inproj_fm(wb, 128 * p, 512)
                    kb.act(kT[p][:, g * G:(g + 1) * G], bk[:, :], AF.Copy, [bk], [kT[p].atoms[g]])
                wdone()
                wb = wnext()
                for i in range(4):
                    bk = nbig()
                    for k in range(8):
                        kb.mm(bk[:, :], hT_h[:, k, i * 128:(i + 1) * 128], wb[:, k * 512:(k + 1) * 512],
                              [wb, hT.atoms[i]], [bk], start=(k == 0), stop=(k == 7), inc=(k == 7))
                    kb.cp(DVE, vtok[4 * g + i][:, :], bk[:, :], [bk], [vtok[4 * g + i]])
                wdone()

                ckpt('qkv')
                def bv(i):
                    return Tl(wkb[i][:, 0:512], wk[i].atoms)
                E32 = [[wk[0], wk[14]], [wk[1], wk[15]]]
                XC = [wk[16], wk[17]]
                SPb = [[bv(2), bv(3)], [bv(4), bv(5)]]
                ATb = [[bv(6), bv(7)], [bv(8), bv(9)]]
                Ssum = [bv(10), bv(11)]
                osq = bv(12)
                rst = wk[13]
                nkb = 4 * g + 4
                kbs = list(range(nkb - 1, -1, -1))
                for p in range(4):
                    Bo = bank[0]
                    Bz = [[bank[1], bank[6]], [bank[2], bank[7]]]
                    Bc = [bank[3], bank[4]]

                    def geom(idx):
                        kbk = kbs[idx]
                        dq = kbk - 4 * g
                        q0 = 128 * max(dq, 0)
                        return kbk, dq, q0, slice(q0, 512), slice(kbk * 128, (kbk + 1) * 128), kT[p].atoms[kbk // 4]

                    def zmm(idx, hh):
                        kbk, dq, q0, cs_, kblk, katoms = geom(idx)
                        rows = slice(64 * hh, 64 * hh + 64)
                        bz = Bz[hh][idx % 2]
                        kb.mm(bz[:, cs_], kT[p][rows, kblk], qs[p][rows, cs_], [katoms, qs[p]], [bz])

                    def expA(idx, hh):
                        kbk, dq, q0, cs_, kblk, katoms = geom(idx)
                        bz = Bz[hh][idx % 2]
                        e = E32[hh][idx % 2]
                        kb.act(e[:, cs_], bz[:, cs_], AF.Exp, [bz], [e])
                        if dq >= 0:
                            kb.tt(DVE, e[:, q0:q0 + 128], e[:, q0:q0 + 128], C(CS_MATT), ALU.mult, [e, cst], [e])

                    def lnA(idx, hh):
                        kbk, dq, q0, cs_, kblk, katoms = geom(idx)
                        e = E32[hh][idx % 2]
                        sp = SPb[hh][idx % 2]
                        kb.act(sp[:, cs_], e[:, cs_], AF.Ln, [e], [sp], bias=1.0)

                    def actA(idx):
                        expA(idx, 0)
                        expA(idx, 1)
                        lnA(idx, 0)
                        lnA(idx, 1)

                    def stB1(idx, hh):
                        kbk, dq, q0, cs_, kblk, katoms = geom(idx)
                        rows = slice(64 * hh, 64 * hh + 64)
                        sp = SPb[hh][idx % 2]
                        kb.mm(Bc[hh][:, cs_], C(CS_TRI), sp[:, cs_], [cst, sp], [Bc[hh]], start=True, stop=(idx == 0), inc=(idx == 0))
                        if idx > 0:
                            kb.mm(Bc[hh][:, cs_], C(CS_ONES), Ssum[hh][:, cs_], [cst, Ssum[hh]], [Bc[hh]], start=False, stop=True)
                        if kbk > 0:
                            kb.tt(DVE, Ssum[hh][:, cs_], Ssum[hh][:, cs_], sp[:, cs_], ALU.add, [Ssum[hh], sp], [Ssum[hh]])

                    def stB2a(idx, hh):
                        kbk, dq, q0, cs_, kblk, katoms = geom(idx)
                        at = ATb[hh][idx % 2]
                        xc = XC[hh]
                        kb.act(xc[:, cs_], Bc[hh][:, cs_], AF.Exp, [Bc[hh]], [xc], scale=-1.0)
                        e = E32[hh][idx % 2]
                        kb.tt(DVE, at[:, cs_], e[:, cs_], xc[:, cs_], ALU.mult, [e, xc], [at])

                    def stB2b(idx, hh):
                        kbk, dq, q0, cs_, kblk, katoms = geom(idx)
                        rows = slice(64 * hh, 64 * hh + 64)
                        at = ATb[hh][idx % 2]
                        kb.mm(Bo[rows, cs_], vtok[kbk][:, 128 * p + 64 * hh: 128 * p + 64 * hh + 64], at[:, cs_],
                              [vtok[kbk], at], [Bo], start=False, stop=(kbk == 0), inc=True)

                    for hh in range(2):
                        rows = slice(64 * hh, 64 * hh + 64)
                        kb.memset(POOL, Ssum[hh][:, :], 0.0, [Ssum[hh]])
                        kb.mm(Bo[rows, :], zt[:, :], qs[p][:, :], [zt, qs[p]], [Bo], start=True, stop=False, inc=False)
                    zmm(0, 0)
                    zmm(0, 1)
                    if nkb > 1:
                        zmm(1, 0)
                        zmm(1, 1)
                    actA(0)
                    for idx in range(nkb):
                        if idx + 2 < nkb:
                            zmm(idx + 2, 0)
                            zmm(idx + 2, 1)
                        if idx + 1 < nkb:
                            actA(idx + 1)
                        stB1(idx, 0)
                        stB1(idx, 1)
                        if idx > 0:
                            stB2b(idx - 1, 0)
                            stB2b(idx - 1, 1)
                        stB2a(idx, 0)
                        stB2a(idx, 1)
                    stB2b(nkb - 1, 0)
                    stB2b(nkb - 1, 1)
                    kb.act(osq[:, :], Bo[:, :], AF.Square, [Bo], [osq])
                    Bs = bank[5]
                    kb.mm(Bs[:, :], C(CS_BLK), osq[:, :], [cst, osq], [Bs])
                    rstd_from_ss(Bs[:, :], Bs, rst, 1.0 / 64, RMS_EPS)
                    kb.stt(DVE, yT[4 + p][:, :], Bo[:, :], pc(PC_SBG + p), rst[:, :], ALU.mult, ALU.mult, [Bo, rst, pcol], [yT[4 + p]])

                ckpt('attn')
                if dbg:
                    for k in range(8):
                        t = wk[14 + (k % 2)]
                        kb.cp(DVE, t[:, :], yT[k][:, :], [yT[k]], [t])
                        kb.dma(dbg_y[k * 128:(k + 1) * 128, tok0:tok0 + G], t[:, :], [t], [], t)

                ckpt('dbgy')
                for i in range(4):
                    xt = xb[i % 2]
                    kb.dma(xt[:, :], x_d[tok0 + i * 128: tok0 + (i + 1) * 128, :], [], [xt], xt)
                    for hf in range(2):
                        bk = nbig()
                        for k in range(8):
                            kb.mm(bk[:, :], yT[k][:, i * 128:(i + 1) * 128], wout[:, k, hf * 512:(hf + 1) * 512],
                                  [yT[k], wout], [bk], start=(k == 0), stop=(k == 7), inc=(k == 7))
                        kb.tt(DVE, x1[i][:, hf * 512:(hf + 1) * 512], bk[:, :], xt[:, hf * 512:(hf + 1) * 512], ALU.add,
                              [bk, xt], [x1[i]])
                    if dbg:
                        kb.dma(dbg_x1[tok0 + i * 128: tok0 + (i + 1) * 128, :], x1[i][:, :], [x1[i]], [], x1[i])

                ckpt('outproj')
                for i in range(4):
                    ln_T(x1[i], i, i)
                for c in range(8):
                    wb = wnext()
                    for fi in range(4):
                        f = 4 * c + fi
                        bk = nbig()
                        for k in range(8):
                            kb.mm(bk[:, :], wb[:, k * 512 + fi * 128: k * 512 + (fi + 1) * 128], hT_h[:, k, :],
                                  [wb, hT], [bk], start=(k == 0), stop=(k == 7), inc=(k == 7))
                        r32 = wk[16 + (f % 2)]
                        kb.act(r32[:, :], bk[:, :], AF.Relu, [bk], [r32])
                        ut = wkb[f // 2]
                        kb.tt(POOL if (f % 2 == 0) else DVE, ut[:, (f % 2) * 512:(f % 2 + 1) * 512], r32[:, :], r32[:, :], ALU.mult,
                              [r32], [ut])
                    wdone()
                for c in range(8):
                    wb = wnext()
                    for fi in range(4):
                        f = 4 * c + fi
                        ut = wkb[f // 2]
                        for i in range(4):
                            for hf in range(2):
                                bk = bank[2 * i + hf]
                                kb.mm(bk[:, :], ut[:, (f % 2) * 512 + i * 128:(f % 2) * 512 + (i + 1) * 128],
                                      wb[:, fi * 1024 + hf * 512: fi * 1024 + (hf + 1) * 512],
                                      [ut, wb], [bk], start=(f == 0), stop=(f == 31), inc=(f == 31 or (fi == 3 and i == 3 and hf == 1)))
                    wdone()
                for i in range(4):
                    for hf in range(2):
                        bk = bank[2 * i + hf]
                        kb.tt(DVE, x1[i][:, hf * 512:(hf + 1) * 512], bk[:, :], x1[i][:, hf * 512:(hf + 1) * 512], ALU.add,
                              [bk, x1[i]], [x1[i]])
                    s = stat[i % 2]
                    h = hn[i % 2]
                    kb.memset(POOL, s[:, 0:1], 0.0, [s])
                    kb.act(h[:, :], x1[i][:, :], AF.Square, [x1[i], s], [h, s], accum_out=s[:, 0:1])
                    kb.act(s[:, 1:2], s[:, 0:1], AF.Ln, [s], [s], scale=1.0 / D, bias=RMS_EPS)
                    kb.act(s[:, 2:3], s[:, 1:2], AF.Exp, [s], [s], scale=-0.5)
                    kb.stt(DVE, x1[i][:, :], x1[i][:, :], s[:, 2:3], lnf[:, :], ALU.mult, ALU.mult, [x1[i], s, lnf], [x1[i]])
                    kb.dma(out_d[tok0 + i * 128: tok0 + (i + 1) * 128, :], x1[i][:, :], [x1[i]], [], x1[i])


def _consts():
    c = np.zeros((NCS, 128, 128), np.float32)
    i = np.arange(128)
    c[CS_ID] = np.eye(128)
    c[CS_TRI] = (i[:, None] >= i[None, :])
    c[CS_ONES] = 1.0
    c[CS_MATT] = (i[:, None] < i[None, :])
    blk = (i[:, None] // 64) == (i[None, :] // 64)
    c[CS_BLK] = blk
    c[CS_UTS] = blk & (i[:, None] < i[None, :])
    c[CS_LTS] = blk & (i[:, None] > i[None, :])
    c[CS_UTI] = blk & (i[:, None] <= i[None, :])
    return np.ascontiguousarray(c.transpose(1, 0, 2).reshape(128, NCS * 128))


def _host_inputs(inp):
    f = lambda a: np.asarray(a, np.float32)
    cols = []
    cols.append(f(inp["ln1_g"])[0].reshape(8, 128))
    cols.append(f(inp["ln2_g"])[0].reshape(8, 128))
    cols.append(f(inp["tok_mu"])[0].reshape(14, 128))
    for k in ("w0", "a0", "k_k", "k_a"):
        cols.append(f(inp[k])[0].reshape(4, 128))
    cols.append(f(inp["r_k"])[0].reshape(4, 128))
    for k in ("gn_w", "gn_b", "sb_gain"):
        cols.append(f(inp[k])[0].reshape(4, 128))
    pcol = np.ascontiguousarray(np.concatenate(cols, axis=0).T)
    assert pcol.shape == (128, NPC)
    wlo = np.ascontiguousarray(np.concatenate([f(inp["w_decay_up"])[0], f(inp["w_aaa_up"])[0]], axis=0))
    shared = {
        "w_in": np.ascontiguousarray(f(inp["w_in"])[0]),
        "w_out": np.ascontiguousarray(f(inp["w_out"])[0]),
        "w_up": np.ascontiguousarray(f(inp["w_up"])[0]),
        "w_down": np.ascontiguousarray(f(inp["w_down"])[0]),
        "wlo": wlo,
        "wg": np.ascontiguousarray(f(inp["w_gate_up"])[0]),
        "pcol": pcol,
        "lnf": np.ascontiguousarray(f(inp["lnf_g"]).reshape(1, D)),
        "cst": _consts(),
    }
    return shared


def kernel(**inputs):
    x = np.asarray(inputs["x"], np.float32)
    B = x.shape[0]
    nseq = B // NCORES
    shared = _host_inputs(inputs)
    nc = build(nseq=nseq)
    in_maps = []
    for c in range(NCORES):
        m = dict(shared)
        m["x"] = np.ascontiguousarray(x[c * nseq:(c + 1) * nseq].reshape(nseq * T, D))
        in_maps.append(m)
    res = run_bass_kernel_spmd(nc, in_maps, core_ids=list(range(NCORES)))
    outs = [np.asarray(r["out"]).reshape(nseq, T, D) for r in res.results]
    return np.concatenate(outs, axis=0).astype(np.float32)
```

```python
import numpy as np
from contextlib import ExitStack
import concourse.bass as bass
import concourse.mybir as mybir
from concourse.bass_utils import run_bass_kernel_spmd

F32 = mybir.dt.float32
BF16 = mybir.dt.bfloat16
AF = mybir.ActivationFunctionType
ALU = mybir.AluOpType

NCORES = 8
D = 1024
T = 2048
G = 512
NG = T // G
DFF = 4096
C0 = 0.6065306597126334
RMS_EPS = 1e-5
GN_EPS = 64e-5

PC_LN1, PC_LN2, PC_MU, PC_W0, PC_A0, PC_KK, PC_KA, PC_RK, PC_GNW, PC_GNB, PC_SBG = 0, 8, 16, 30, 34, 38, 42, 46, 50, 54, 58
NPC = 62
CS_ID, CS_TRI, CS_ONES, CS_MATT, CS_BLK, CS_UTS, CS_LTS, CS_UTI = range(8)
NCS = 8


class Atom:
    __slots__ = ("w", "r", "x")

    def __init__(self, x=False):
        self.w = {}
        self.r = {}
        self.x = x


class Tl:
    def __init__(self, ap, atoms=None):
        self.ap = ap
        self.atoms = atoms if atoms is not None else [Atom()]

    def __getitem__(self, k):
        return self.ap[k]


class Eng:
    def __init__(self, e, sem, is_pe=False):
        self.e = e
        self.sem = sem
        self.cnt = 0
        self.waited = {}
        self.is_pe = is_pe


def _atoms(lst):
    out = []
    for t in lst:
        if isinstance(t, Tl):
            out.extend(t.atoms)
        elif isinstance(t, Atom):
            out.append(t)
        else:
            out.extend(_atoms(t))
    return out


class KB:
    def __init__(self, nc, es):
        self.nc = nc
        self.es = es
        self.PE = Eng(nc.tensor, es.enter_context(nc.semaphore("s_pe")), True)
        self.ACT = Eng(nc.scalar, es.enter_context(nc.semaphore("s_act")))
        self.DVE = Eng(nc.vector, es.enter_context(nc.semaphore("s_dve")))
        self.POOL = Eng(nc.gpsimd, es.enter_context(nc.semaphore("s_pool")))
        self.SP = Eng(nc.sync, None)
        self.dkeys = []
        self.hist = {}
        self.ninstr = 0

    def _need(self, eng, reads, writes):
        need = {}
        for a in reads:
            for s, v in a.w.items():
                if need.get(s, 0) < v:
                    need[s] = v
        for a in writes:
            for s, v in a.w.items():
                if need.get(s, 0) < v:
                    need[s] = v
            for s, v in a.r.items():
                if need.get(s, 0) < v:
                    need[s] = v
        for s, v in sorted(need.items(), key=lambda kv: -kv[1]):
            if eng.is_pe and s is eng.sem:
                continue
            if eng.waited.get(s, 0) >= v:
                continue
            eng.e.wait_ge(s, v)
            eng.waited[s] = v
            self.ninstr += 1
            snap = self.hist.get((s, v))
            if snap:
                w = eng.waited
                for s2, v2 in snap.items():
                    if w.get(s2, 0) < v2:
                        w[s2] = v2

    def op(self, eng, fn, reads, writes, inc=True):
        reads = _atoms(reads)
        writes = _atoms(writes)
        xr = [a for a in reads if a.x]
        if xr:
            writes = writes + xr
        self._need(eng, reads, writes)
        ins = fn()
        self.ninstr += 1
        if inc:
            eng.cnt += 1
            ins.then_inc(eng.sem, 1)
            val = eng.cnt
            snap = dict(eng.waited)
            if not eng.is_pe:
                snap[eng.sem] = val - 1
            self.hist[(eng.sem, val)] = snap
        else:
            val = eng.cnt + 1
        s = eng.sem
        for a in reads:
            if a.r.get(s, 0) < val:
                a.r[s] = val
        for a in writes:
            if a.w.get(s, 0) < val:
                a.w[s] = val

    def dma(self, out, in_, reads, writes, key, q=None, **kw):
        q = q or self.SP
        reads = _atoms(reads)
        writes = _atoms(writes)
        if not hasattr(key, "dsem"):
            key.dsem = self.es.enter_context(self.nc.semaphore("s_dma%d" % len(self.dkeys)))
            key.dcnt = 0
            self.dkeys.append(key)
        s = key.dsem
        self._need(q, reads, writes)
        if q.waited.get(s, 0) < key.dcnt:
            q.e.wait_ge(s, key.dcnt)
            q.waited[s] = key.dcnt
        key.dcnt += 16
        val = key.dcnt
        q.e.dma_start(out=out, in_=in_, **kw).then_inc(s, 16)
        self.ninstr += 1
        self.hist[(s, val)] = dict(q.waited)
        for a in reads:
            if a.r.get(s, 0) < val:
                a.r[s] = val
        for a in writes:
            if a.w.get(s, 0) < val:
                a.w[s] = val

    def finish(self):
        for k in self.dkeys:
            if self.SP.waited.get(k.dsem, 0) < k.dcnt:
                self.nc.sync.wait_ge(k.dsem, k.dcnt)

    def mm(self, out, lhsT, rhs, reads, writes, start=True, stop=True, inc=True):
        nc = self.nc
        self.op(self.PE, lambda: nc.tensor.matmul(out, lhsT, rhs, start=start, stop=stop), reads, writes, inc)

    def tr(self, out, in_, ident, reads, writes, inc=True):
        nc = self.nc
        self.op(self.PE, lambda: nc.tensor.transpose(out, in_, ident), reads, writes, inc)

    def act(self, out, in_, func, reads, writes, **kw):
        nc = self.nc
        self.op(self.ACT, lambda: nc.scalar.activation(out=out, in_=in_, func=func, **kw), reads, writes)

    def tt(self, eng, out, in0, in1, op, reads, writes):
        self.op(eng, lambda: eng.e.tensor_tensor(out=out, in0=in0, in1=in1, op=op), reads, writes)

    def ts(self, eng, out, in0, s1, s2, op0, op1, reads, writes):
        if s2 is None:
            self.op(eng, lambda: eng.e.tensor_scalar(out=out, in0=in0, scalar1=s1, scalar2=None, op0=op0), reads, writes)
        else:
            self.op(eng, lambda: eng.e.tensor_scalar(out=out, in0=in0, scalar1=s1, scalar2=s2, op0=op0, op1=op1), reads, writes)

    def stt(self, eng, out, in0, scalar, in1, op0, op1, reads, writes):
        self.op(eng, lambda: eng.e.scalar_tensor_tensor(out=out, in0=in0, scalar=scalar, in1=in1, op0=op0, op1=op1), reads, writes)

    def cp(self, eng, out, in_, reads, writes):
        self.op(eng, lambda: eng.e.tensor_copy(out, in_), reads, writes)

    def memset(self, eng, ap, val, writes):
        self.op(eng, lambda: eng.e.memset(ap, val), [], writes)


def _win_chunks():
    ch = [[(1536, 256)]]
    for p in range(4):
        ch.append([(128 * p, 128), (512 + 128 * p, 128), (1024 + 128 * p, 128)])
    ch.append([(1792, 512)])
    ch.append([(2304, 512)])
    ch.append([(2816, 512)])
    return ch


class StopBuild(Exception):
    pass


def build(nseq=4, dbg=False, stop_at=None):
    nc = bass.Bass("TRN2", target_bir_lowering=False)
    NT = nseq * T
    x_d = nc.dram_tensor("x", [NT, D], F32, kind="ExternalInput").ap()
    win_d = nc.dram_tensor("w_in", [D, 3328], F32, kind="ExternalInput").ap()
    wout_d = nc.dram_tensor("w_out", [D, D], F32, kind="ExternalInput").ap()
    wup_d = nc.dram_tensor("w_up", [D, DFF], F32, kind="ExternalInput").ap()
    wdn_d = nc.dram_tensor("w_down", [DFF, D], F32, kind="ExternalInput").ap()
    wlo_d = nc.dram_tensor("wlo", [128, 512], F32, kind="ExternalInput").ap()
    wg_d = nc.dram_tensor("wg", [128, 512], F32, kind="ExternalInput").ap()
    pcol_d = nc.dram_tensor("pcol", [128, NPC], F32, kind="ExternalInput").ap()
    lnf_d = nc.dram_tensor("lnf", [1, D], F32, kind="ExternalInput").ap()
    cst_d = nc.dram_tensor("cst", [128, NCS * 128], F32, kind="ExternalInput").ap()
    out_d = nc.dram_tensor("out", [NT, D], F32, kind="ExternalOutput").ap()
    wsc_d = nc.dram_tensor("wscr", [24, 128, 4096], BF16).ap()
    if dbg:
        dbg_y = nc.dram_tensor("dbg_y", [D, NT], F32, kind="ExternalOutput").ap()
        dbg_x1 = nc.dram_tensor("dbg_x1", [NT, D], F32, kind="ExternalOutput").ap()

    es = ExitStack()
    with es:
        kb = KB(nc, es)
        PE, ACT, DVE, POOL = kb.PE, kb.ACT, kb.DVE, kb.POOL

        def ckpt(name):
            if stop_at is not None and name == stop_at:
                raise StopBuild()
        try:
            _body(nc, es, kb, nseq, dbg, ckpt, locals())
        except StopBuild:
            pass
        kb.finish()
        print("instructions emitted:", kb.ninstr, "pe", PE.cnt, "act", ACT.cnt, "dve", DVE.cnt, "pool", POOL.cnt)
    return nc


def _body(nc, es, kb, nseq, dbg, ckpt, env):
    globals_ = env
    x_d, win_d, wout_d, wup_d, wdn_d, wlo_d, wg_d, pcol_d, lnf_d, cst_d, out_d, wsc_d = (env[k] for k in (
        "x_d", "win_d", "wout_d", "wup_d", "wdn_d", "wlo_d", "wg_d", "pcol_d", "lnf_d", "cst_d", "out_d", "wsc_d"))
    dbg_y = env.get("dbg_y")
    dbg_x1 = env.get("dbg_x1")
    PE, ACT, DVE, POOL = kb.PE, kb.ACT, kb.DVE, kb.POOL
    if True:

        def sb(name, shape, dt, natoms=1):
            h = es.enter_context(nc.sbuf_tensor("sb_" + name, shape, dt))
            return h

        cst = Tl(sb("cstb", [128, NCS * 128], BF16)[:])
        pcol = Tl(sb("pcol", [128, NPC], F32)[:])
        lnf = Tl(sb("lnfb", [128, D], F32)[:])
        wlo = Tl(sb("wlo", [128, 512], BF16)[:])
        wgt = Tl(sb("wgt", [128, 512], BF16)[:])
        wout = Tl(sb("wout", [128, 8, D], BF16)[:])
        NWB = 2
        wbuf = [Tl(sb("wbuf%d" % i, [128, 4096], BF16)[:]) for i in range(NWB)]
        xb = [Tl(sb("xb%d" % i, [128, D], F32)[:]) for i in range(2)]
        hn = [Tl(sb("hn%d" % i, [128, D], BF16)[:]) for i in range(1)] * 2
        stat = [Tl(sb("stat%d" % i, [128, 4], F32)[:]) for i in range(2)]
        hT_h = sb("hT", [128, 8, G], BF16)
        hT = Tl(hT_h[:], [Atom() for _ in range(4)])
        x1 = [Tl(sb("x1_%d" % i, [128, D], F32)[:]) for i in range(4)]
        yT = [Tl(sb("yT%d" % k, [128, G], BF16)[:]) for k in range(8)]
        kT = [Tl(sb("kT%d" % p, [128, T], BF16)[:], [Atom() for _ in range(NG)]) for p in range(4)]
        vtok = [Tl(sb("vtok%d" % i, [128, 512], BF16)[:]) for i in range(16)]
        qs = [Tl(sb("qs%d" % p, [128, G], BF16)[:]) for p in range(4)]
        qn = [Tl(sb("qn%d" % p, [128, G], BF16)[:]) for p in range(4)]
        NWK = 18
        wk_h = [sb("wk%d" % i, [128, 512], F32) for i in range(NWK)]
        wk = [Tl(h[:]) for h in wk_h]
        wkb = [Tl(h[:].bitcast(BF16), wk[i].atoms) for i, h in enumerate(wk_h)]
        rkv = [[Tl(sb("rkv%d_%d" % (b, j), [128, 512], F32)[:]) for j in range(3)] for b in range(1)]
        pst = [Tl(sb("pst%d" % i, [128, 513], F32)[:]) for i in range(1)] * 2
        carry = Tl(sb("carry", [128, 14], F32)[:])
        txa = Tl(sb("txa", [128, 512], BF16)[:])
        sgx = Tl(sb("sgx", [128, 512], BF16)[:])
        pb = [[Tl(sb("pb%d_%d" % (b, j), [128, 512], BF16)[:]) for j in range(5)] for b in range(1)]
        tmpb = Tl(sb("tmpb", [128, 512], BF16)[:])
        zt = Tl(sb("zt", [128, 64], BF16)[:])
        NBD = 36
        bdt = [Tl(sb("bd%d" % i, [128, 128], BF16)[:]) for i in range(NBD)]
        bdl = [[Tl(sb("bdl%d_%d" % (j, i), [128, 128], BF16)[:]) for i in range(9)] for j in range(8)]
        PCS = [Tl(sb("pcs%d" % i, [128, 8], F32)[:]) for i in range(2)]
        H32 = [Tl(sb("H32_%d" % p, [128, 128], F32)[:]) for p in range(4)]
        HP = [Tl(sb("HP_%d" % p, [128, 128], F32)[:]) for p in range(2)]
        Hb = [Tl(sb("Hb_%d" % p, [128, 128], BF16)[:]) for p in range(4)]
        bank_h = [es.enter_context(nc.psum_tensor("bank%d" % i, [128, 512], F32)) for i in range(8)]
        bank_atoms = [[Atom(True)] for _ in range(8)]
        bank = [Tl(bank_h[i][:], bank_atoms[i]) for i in range(8)]
        small = []
        for q in range(4):
            for b in range(4, 8):
                t_ = Tl(bank_h[b][:, q * 128:(q + 1) * 128], [bank_atoms[b][0]])
                t_.bfv = bank_h[b][:].bitcast(BF16)[:, q * 256:q * 256 + 128]
                small.append(t_)
        st = {"bd": 0, "small": 0, "big": 0}

        def nbd():
            t = bdt[st["bd"] % NBD]
            st["bd"] += 1
            return t

        def nsmall():
            t = small[st["small"] % len(small)]
            st["small"] += 1
            return t

        def nbig():
            t = bank[st["big"] % 4]
            st["big"] += 1
            return t

        def C(i):
            return cst[:, i * 128:(i + 1) * 128]

        def pc(col):
            return pcol[:, col:col + 1]

        kb.dma(wk[14][:, :], cst_d[:, 0:512], [], [wk[14]], wk[14])
        kb.dma(wk[15][:, :], cst_d[:, 512:1024], [], [wk[15]], wk[15])
        kb.memset(POOL, zt[:, :], 0.0, [zt])
        kb.dma(pcol[:, :], pcol_d[:, :], [], [pcol], pcol)
        kb.dma(lnf[:, :], lnf_d[0:1, :].partition_broadcast(128), [], [lnf], lnf)
        kb.cp(DVE, cst[:, 0:512], wk[14][:, :], [wk[14]], [cst])
        kb.cp(DVE, cst[:, 512:1024], wk[15][:, :], [wk[15]], [cst])
        kb.dma(wk[0][:, :], wlo_d[:, :], [], [wk[0]], wk[0])
        kb.dma(wk[1][:, :], wg_d[:, :], [], [wk[1]], wk[1])
        kb.cp(DVE, wlo[:, :], wk[0][:, :], [wk[0]], [wlo])
        kb.cp(DVE, wgt[:, :], wk[1][:, :], [wk[1]], [wgt])

        ckpt('consts')
        scr = [Tl(wsc_d[i], [Atom()]) for i in range(24)]
        cast_rr = [0]
        cast_engs = [DVE, POOL, ACT]

        def cast(out_ap, in_tl, in_ap, scale_col, out_tl):
            e = cast_engs[cast_rr[0] % 3]
            cast_rr[0] += 1
            if scale_col is None:
                if e is ACT:
                    kb.act(out_ap, in_ap, AF.Copy, [in_tl], [out_tl])
                else:
                    kb.cp(e, out_ap, in_ap, [in_tl], [out_tl])
            else:
                if e is ACT:
                    kb.act(out_ap, in_ap, AF.Copy, [in_tl, pcol], [out_tl], scale=pc(scale_col))
                else:
                    kb.ts(e, out_ap, in_ap, pc(scale_col), None, ALU.mult, None, [in_tl, pcol], [out_tl])

        stg_rr = [0]

        def stage():
            t = wk[2 + (stg_rr[0] % 12)]
            t.q = kb.SP
            stg_rr[0] += 1
            return t

        wchunks = _win_chunks()
        wb_rr = [0]
        big_stages = x1 + xb
        bst = [0]

        def stage4():
            t = big_stages[bst[0] % len(big_stages)]
            bst[0] += 1
            return t

        for ci, ch in enumerate(wchunks):
            ncols = sum(n for _, n in ch)
            wb = wbuf[wb_rr[0] % NWB]
            wb_rr[0] += 1
            for k0 in range(0, 8, 2):
                s = stage4()
                sv = s[:, 0:2 * ncols].rearrange("p (k c) -> p k c", k=2)
                off = 0
                for (c0, n) in ch:
                    kb.dma(sv[:, :, off:off + n],
                           win_d[k0 * 128:(k0 + 2) * 128, c0:c0 + n].rearrange("(k p) c -> p k c", k=2), [], [s], s)
                    off += n
                for kk in range(2):
                    cast(wb[:, (k0 + kk) * ncols:(k0 + kk + 1) * ncols], s, s[:, kk * ncols:(kk + 1) * ncols],
                         PC_LN1 + k0 + kk, wb)
            kb.dma(wsc_d[ci][:, 0:8 * ncols], wb[:, 0:8 * ncols], [wb], [scr[ci]], wb)
        for c in range(8):
            wb = wbuf[wb_rr[0] % NWB]
            wb_rr[0] += 1
            for k0 in range(0, 8, 2):
                s = stage4()
                kb.dma(s[:, :].rearrange("p (k c) -> p k c", k=2),
                       wup_d[k0 * 128:(k0 + 2) * 128, c * 512:(c + 1) * 512].rearrange("(k p) c -> p k c", k=2), [], [s], s)
                for kk in range(2):
                    cast(wb[:, (k0 + kk) * 512:(k0 + kk + 1) * 512], s, s[:, kk * 512:(kk + 1) * 512], PC_LN2 + k0 + kk, wb)
            kb.dma(wsc_d[8 + c][:, :], wb[:, :], [wb], [scr[8 + c]], wb)
        for c in range(8):
            wb = wbuf[wb_rr[0] % NWB]
            wb_rr[0] += 1
            for fi in range(4):
                s = stage4()
                r0 = c * 512 + fi * 128
                kb.dma(s[:, :], wdn_d[r0:r0 + 128, :], [], [s], s)
                cast(wb[:, fi * 1024:(fi + 1) * 1024], s, s[:, :], None, wb)
            kb.dma(wsc_d[16 + c][:, :], wb[:, :], [wb], [scr[16 + c]], wb)
        for k in range(8):
            s = stage4()
            kb.dma(s[:, :], wout_d[k * 128:(k + 1) * 128, :], [], [s], s)
            cast(wout[:, k, :], s, s[:, :], None, wout)

        ckpt('casts')
        stream = []
        for _ in range(nseq * NG):
            for ci, ch in enumerate(wchunks):
                stream.append((ci, 8 * sum(n for _, n in ch)))
            for c in range(8):
                stream.append((8 + c, 4096))
            for c in range(8):
                stream.append((16 + c, 4096))
        sst = {"issued": 0, "used": 0}

        def wprefetch():
            while sst["issued"] < len(stream) and sst["issued"] < sst["used"] + NWB:
                i = sst["issued"]
                ci, n = stream[i]
                wb = wbuf[(wb_rr[0] + i) % NWB]
                kb.dma(wb[:, 0:n], wsc_d[ci][:, 0:n], [scr[ci]], [wb], wb)
                sst["issued"] += 1

        def wnext():
            wprefetch()
            i = sst["used"]
            wb = wbuf[(wb_rr[0] + i) % NWB]
            sst["used"] += 1
            return wb

        def wdone():
            wprefetch()

        def ln_T(src, i, sidx):
            h = hn[sidx % 2]
            s = stat[sidx % 2]
            kb.memset(POOL, s[:, 0:1], 0.0, [s])
            kb.act(h[:, :], src[:, :], AF.Square, [src, s], [h, s], accum_out=s[:, 0:1])
            kb.act(s[:, 1:2], s[:, 0:1], AF.Ln, [s], [s], scale=1.0 / D, bias=RMS_EPS)
            kb.act(s[:, 2:3], s[:, 1:2], AF.Exp, [s], [s], scale=-0.5)
            kb.ts(DVE, h[:, :], src[:, :], s[:, 2:3], None, ALU.mult, None, [src, s], [h])
            bk = nbig()
            bv = bk.ap.bitcast(BF16)
            for k in range(8):
                kb.tr(bv[:, k * 128:(k + 1) * 128], h[:, k * 128:(k + 1) * 128], C(CS_ID), [h, cst], [bk], inc=(k == 7))
            kb.cp(DVE, hT_h[:, :, i * 128:(i + 1) * 128],
                  bv[:, :].rearrange("p (k c) -> p k c", k=8), [bk], [hT.atoms[i]])
            return s

        def rstd_from_ss(ps_ap, ps_tl, out_tl, scale, eps):
            kb.act(out_tl[:, :], ps_ap, AF.Ln, [ps_tl], [out_tl], scale=scale, bias=eps)
            kb.act(out_tl[:, :], out_tl[:, :], AF.Exp, [out_tl], [out_tl], scale=-0.5)

        for seq in range(nseq):
            kb.memset(POOL, carry[:, :], 0.0, [carry])
            for p in range(4):
                kb.memset(POOL, H32[p][:, :], 0.0, [H32[p]])
                kb.memset(POOL, Hb[p][:, :], 0.0, [Hb[p]])
            for g in range(NG):
                tok0 = seq * T + g * G
                for i in range(4):
                    xt = xb[i % 2]
                    kb.dma(xt[:, :], x_d[tok0 + i * 128: tok0 + (i + 1) * 128, :], [], [xt], xt)
                    ln_T(xt, i, i)

                ckpt('ln1')
                def inproj_fm(wb, col0, ncols_chunk):
                    bk = nbig()
                    for k in range(8):
                        kb.mm(bk[:, :], wb[:, k * ncols_chunk + col0: k * ncols_chunk + col0 + 128], hT_h[:, k, :],
                              [wb, hT], [bk], start=(k == 0), stop=(k == 7), inc=(k == 7))
                    return bk

                def shiftmix(bk, ct, dst_ap, dst_tl):
                    P = pst[ct % 2]
                    kb.act(P[:, 1:513], bk[:, :], AF.Copy, [bk], [P])
                    kb.cp(POOL, P[:, 0:1], carry[:, ct:ct + 1], [carry], [P])
                    d = wk[17]
                    kb.tt(DVE, d[:, :], P[:, 0:512], P[:, 1:513], ALU.subtract, [P], [d])
                    kb.stt(DVE, dst_ap, d[:, :], pc(PC_MU + ct), P[:, 1:513], ALU.mult, ALU.add, [d, P, pcol], [dst_tl])
                    kb.cp(POOL, carry[:, ct:ct + 1], P[:, 512:513], [P], [carry])

                wb = wnext()
                bk = inproj_fm(wb, 0, 256)
                t12 = wk[16]
                shiftmix(bk, 12, t12[:, :], t12)
                kb.act(txa[0:64, :], t12[0:64, :], AF.Tanh, [t12], [txa])
                kb.cp(POOL, txa[64:128, :], t12[64:128, :], [t12], [txa])
                bk = inproj_fm(wb, 128, 256)
                wdone()
                t13 = wk[16]
                shiftmix(bk, 13, t13[:, :], t13)
                kb.act(sgx[:, :], t13[:, :], AF.Sigmoid, [t13], [sgx])

                ckpt('chunk0')
                PP = {}

                def prep_gen(p):
                        wb = wnext()
                        R_, K_, V_ = rkv[0]
                        for j, dst in enumerate((R_, K_, V_)):
                            bk = inproj_fm(wb, 128 * j, 384)
                            shiftmix(bk, 4 * j + p, dst[:, :], dst)
                            yield
                        wdone()
                        Ab, Rb, Bb, Kb_, Vb = pb[0]
                        cols = slice(128 * p, 128 * (p + 1))
                        b1 = nbig()
                        kb.mm(b1[:, :], wlo[0:64, cols], txa[0:64, :], [wlo, txa], [b1])
                        sgm = wk[0]
                        kb.act(sgm[:, :], b1[:, :], AF.Sigmoid, [b1, pcol], [sgm], bias=pc(PC_W0 + p))
                        b2 = nbig()
                        kb.mm(b2[:, :], wlo[64:128, cols], txa[64:128, :], [wlo, txa], [b2])
                        lr = wk[1]
                        kb.act(lr[:, :], b2[:, :], AF.Sigmoid, [b2, pcol], [lr], bias=pc(PC_A0 + p))
                        yield
                        cs = wk[2]
                        for j in range(8):
                            sl = slice(64 * j, 64 * j + 64)
                            kb.op(DVE, lambda sl=sl: nc.vector.tensor_tensor_scan(out=cs[:, sl], data0=sgm[:, sl], data1=sgm[:, sl],
                                                                                      initial=0.0, op0=ALU.add, op1=ALU.bypass),
                                  [sgm], [cs])
                        yield
                        kkr = wk[3]
                        kb.act(kkr[:, :], K_[:, :], AF.Copy, [K_, pcol], [kkr], scale=pc(PC_KK + p))
                        kb.act(tmpb[:, :], kkr[:, :], AF.Square, [kkr], [tmpb])
                        b4 = nbig()
                        kb.mm(b4[:, :], C(CS_BLK), tmpb[:, :], [cst, tmpb], [b4])
                        rn = wk[4]
                        kb.ts(DVE, rn[:, :], b4[:, :], 1e-24, None, ALU.max, None, [b4], [rn])
                        kb.act(rn[:, :], rn[:, :], AF.Ln, [rn], [rn])
                        kb.act(rn[:, :], rn[:, :], AF.Exp, [rn], [rn], scale=-0.5)
                        kb.tt(POOL, kkr[:, :], kkr[:, :], rn[:, :], ALU.mult, [kkr, rn], [kkr])
                        yield
                        Ep, Em, Epv = wk[5], wk[6], wk[7]
                        kb.act(Ep[:, :], cs[:, :], AF.Exp, [cs], [Ep], scale=-C0)
                        kb.act(Em[:, :], cs[:, :], AF.Exp, [cs], [Em], scale=C0)
                        kb.tt(POOL, Epv[:, :], cs[:, :], sgm[:, :], ALU.subtract, [cs, sgm], [Epv])
                        kb.act(Epv[:, :], Epv[:, :], AF.Exp, [Epv], [Epv], scale=-C0)
                        yield
                        kb.stt(DVE, Ab[:, :], kkr[:, :], -1.0, Epv[:, :], ALU.mult, ALU.mult, [kkr, Epv], [Ab])
                        kb.tt(POOL, Rb[:, :], R_[:, :], Ep[:, :], ALU.mult, [R_, Ep], [Rb])
                        bv_ = wk[8]
                        kb.tt(DVE, bv_[:, :], kkr[:, :], lr[:, :], ALU.mult, [kkr, lr], [bv_])
                        kb.tt(POOL, Bb[:, :], bv_[:, :], Em[:, :], ALU.mult, [bv_, Em], [Bb])
                        kb.ts(DVE, lr[:, :], lr[:, :], -1.0, pc(PC_KA + p), ALU.add, ALU.mult, [lr, pcol], [lr])
                        yield
                        km = wk[9]
                        kb.stt(DVE, km[:, :], lr[:, :], 1.0, K_[:, :], ALU.add, ALU.mult, [lr, K_], [km])
                        kb.tt(POOL, Kb_[:, :], km[:, :], Em[:, :], ALU.mult, [km, Em], [Kb_])
                        kb.act(Vb[:, :], V_[:, :], AF.Copy, [V_], [Vb])
                        kb.stt(DVE, tmpb[:, :], R_[:, :], pc(PC_RK + p), km[:, :], ALU.mult, ALU.mult, [R_, km, pcol], [tmpb])
                        b5 = nbig()
                        kb.mm(b5[:, :], C(CS_BLK), tmpb[:, :], [cst, tmpb], [b5])
                        bonus = wk[10] if p % 2 == 0 else wk[15]
                        kb.tt(DVE, bonus[:, :], V_[:, :], b5[:, :], ALU.mult, [V_, b5], [bonus])
                        pcs = PCS[p % 2]
                        kb.act(pcs[:, :], Ep[:, 63:512:64], AF.Copy, [Ep], [pcs])
                        PP[p] = dict(bonus=bonus, pcs=pcs)
                        yield

                for _ in prep_gen(0):
                    pass
                CTX = [[dict() for _ in range(8)] for _ in range(4)]

                def seqgen(*gens):
                    for g_ in gens:
                        yield from g_

                for p in range(4):
                    bdm = C(CS_BLK).rearrange("p (a b) -> p a b", a=2)
                    pcs = PP[p]["pcs"]
                    bonus = PP[p]["bonus"]
                    Ab, Rb, Bb, Kb_, Vb = pb[0]
                    yraw = wk[12]
                    cols = slice(128 * p, 128 * (p + 1))
                    ctxs = CTX[p]

                    def pre_gen(jl, pp):
                        ctxs = CTX[pp]
                        def mkbd(src, t, sl):
                            kb.tt(POOL, t[:, :].rearrange("p (a b) -> p a b", a=2),
                                  src[:, sl].unsqueeze(1).broadcast_to([128, 2, 64]), bdm, ALU.mult, [src, cst], [t])
                            return t
                        for j in jl:
                            c = ctxs[j]
                            sl = slice(64 * j, 64 * j + 64)
                            LL = bdl[j]
                            c["AT"] = mkbd(Ab, LL[0], sl)
                            c["RT"] = mkbd(Rb, LL[1], sl)
                            c["BT"] = mkbd(Bb, nbd(), sl)
                            c["KT"] = mkbd(Kb_, nbd(), sl)
                            c["VT"] = mkbd(Vb, nbd(), sl)
                        yield
                        for j in jl:
                            c = ctxs[j]
                            LL = bdl[j]
                            for nm, src, dst in (("Btok", c["BT"], LL[2]), ("Ktok", c["KT"], LL[3]), ("Vtok", c["VT"], LL[4])):
                                ps = nsmall()
                                kb.tr(ps.bfv, src[:, :], C(CS_ID), [src, cst], [ps])
                                kb.cp(DVE, dst[:, :], ps.bfv, [ps], [dst])
                                c[nm] = dst
                        yield

                        def gram(l, r, mask, t):
                            ps = nsmall()
                            kb.mm(ps[:, :], l[:, :], r[:, :], [l, r], [ps])
                            kb.tt(DVE, t[:, :], ps[:, :], C(mask), ALU.mult, [ps, cst], [t])
                            return t
                        for j in jl:
                            c = ctxs[j]
                            LL = bdl[j]
                            c["M"] = gram(c["BT"], c["AT"], CS_UTS, nbd())
                            c["N"] = gram(c["AT"], c["BT"], CS_LTS, nbd())
                            c["Aak"] = gram(c["KT"], c["AT"], CS_UTS, LL[5])
                        yield
                        for j in jl:
                            c = ctxs[j]
                            LL = bdl[j]
                            c["Arb"] = gram(c["BT"], c["RT"], CS_UTI, LL[6])
                            c["Ark"] = gram(c["KT"], c["RT"], CS_UTI, LL[7])
                            X = nbd()
                            kb.tt(POOL, X[:, :], c["M"][:, :], C(CS_ID), ALU.add, [c["M"], cst], [X])
                            c["X"] = X
                        yield
                        for lev in range(1, 6):
                            for j in jl:
                                c = ctxs[j]
                                psn = nsmall()
                                kb.mm(psn[:, :], c["M"][:, :], c["N"][:, :], [c["M"], c["N"]], [psn])
                                Nn = nbd()
                                kb.act(Nn[:, :], psn[:, :], AF.Copy, [psn], [Nn])
                                c["Nn"] = Nn
                            yield
                            if lev < 5:
                                for j in jl:
                                    c = ctxs[j]
                                    psm = nsmall()
                                    kb.mm(psm[:, :], c["N"][:, :], c["M"][:, :], [c["M"], c["N"]], [psm])
                                    Mn = nbd()
                                    kb.act(Mn[:, :], psm[:, :], AF.Copy, [psm], [Mn])
                                    c["Mn"] = Mn
                                yield
                            for j in jl:
                                c = ctxs[j]
                                psx = nsmall()
                                kb.mm(psx[:, :], c["Nn"][:, :], c["X"][:, :], [c["Nn"], c["X"]], [psx])
                                Xn = nbd() if lev < 5 else bdl[j][8]
                                kb.tt(DVE, Xn[:, :], psx[:, :], c["X"][:, :], ALU.add, [psx, c["X"]], [Xn])
                                c["X"] = Xn
                                c["N"] = c["Nn"]
                                c["M"] = c.get("Mn") if lev < 5 else None
                            yield

                    def chain_gen(jl):
                        for j in jl:
                            c = ctxs[j]
                            sl = slice(64 * j, 64 * j + 64)
                            hp = HP[j % 2]
                            kb.act(hp[:, :], H32[p][:, :], AF.Copy, [H32[p], pcs], [hp], scale=pcs[:, j:j + 1])
                            psw = nsmall()
                            kb.mm(psw[:, :], c["AT"][:, :], Hb[p][:, :], [c["AT"], Hb[p]], [psw], start=True, stop=False, inc=False)
                            kb.mm(psw[:, :], c["Aak"][:, :], c["Vtok"][:, :], [c["Aak"], c["Vtok"]], [psw], start=False, stop=True)
                            Wb = nbd()
                            kb.act(Wb[:, :], psw[:, :], AF.Copy, [psw], [Wb])
                            yield
                            psu = nsmall()
                            kb.mm(psu[:, :], c["X"][:, :], Wb[:, :], [c["X"], Wb], [psu])
                            Ub = nbd()
                            kb.cp(DVE, Ub[:, :], psu[:, :], [psu], [Ub])
                            yield
                            psh = nsmall()
                            kb.mm(psh[:, :], c["Btok"][:, :], Ub[:, :], [c["Btok"], Ub], [psh], start=True, stop=False, inc=False)
                            kb.mm(psh[:, :], c["Ktok"][:, :], c["Vtok"][:, :], [c["Ktok"], c["Vtok"]], [psh], start=False, stop=True)
                            psy = nsmall()
                            kb.mm(psy[:, :], Hb[p][:, :], c["RT"][:, :], [Hb[p], c["RT"]], [psy], start=True, stop=False, inc=False)
                            kb.mm(psy[:, :], Ub[:, :], c["Arb"][:, :], [Ub, c["Arb"]], [psy], start=False, stop=False, inc=False)
                            kb.mm(psy[:, :], c["Vtok"][:, :], c["Ark"][:, :], [c["Vtok"], c["Ark"]], [psy], start=False, stop=True)
                            kb.stt(DVE, H32[p][:, :], psh[:, :], pcs[:, j:j + 1], hp[:, :], ALU.mult, ALU.add,
                                   [psh, pcs, hp], [H32[p]])
                            kb.act(Hb[p][:, :], H32[p][:, :], AF.Copy, [H32[p]], [Hb[p]])
                            yield
                            kb.act(yraw[0:64, sl], psy[0:64, 0:64], AF.Copy, [psy], [yraw])
                            kb.act(yraw[64:128, sl], psy[64:128, 64:128], AF.Copy, [psy], [yraw])
                            yield

                    def run_il(*gens):
                        gens = list(gens)
                        while gens:
                            for g_ in list(gens):
                                try:
                                    next(g_)
                                except StopIteration:
                                    gens.remove(g_)

                    if p == 0:
                        run_il(pre_gen([0, 1, 2, 3], 0))
                    run_il(chain_gen([0, 1, 2, 3]), pre_gen([4, 5, 6, 7], p))
                    if p < 3:
                        run_il(chain_gen([4, 5, 6, 7]), seqgen(prep_gen(p + 1), pre_gen([0, 1, 2, 3], p + 1)))
                    else:
                        run_il(chain_gen([4, 5, 6, 7]))
                    ckpt('chunks')
                    kb.act(tmpb[:, :], yraw[:, :], AF.Copy, [yraw], [tmpb])
                    b6 = nbig()
                    kb.mm(b6[:, :], C(CS_BLK), tmpb[:, :], [cst, tmpb], [b6])
                    yc = wk[13]
                    kb.stt(DVE, yc[:, :], b6[:, :], -1.0 / 64, yraw[:, :], ALU.mult, ALU.add, [b6, yraw], [yc])
                    kb.act(tmpb[:, :], yc[:, :], AF.Square, [yc], [tmpb])
                    b7 = nbig()
                    kb.mm(b7[:, :], C(CS_BLK), tmpb[:, :], [cst, tmpb], [b7])
                    rs = wk[14]
                    rstd_from_ss(b7[:, :], b7, rs, 1.0 / 64, GN_EPS)
                    kb.tt(DVE, yc[:, :], yc[:, :], rs[:, :], ALU.mult, [yc, rs], [yc])
                    kb.act(yc[:, :], yc[:, :], AF.Identity, [yc, pcol], [yc], scale=pc(PC_GNW + p), bias=pc(PC_GNB + p))
                    kb.tt(DVE, yc[:, :], yc[:, :], bonus[:, :], ALU.add, [yc, bonus], [yc])
                    b3 = nbig()
                    kb.mm(b3[:, :], wgt[:, cols], sgx[:, :], [wgt, sgx], [b3])
                    kb.tt(DVE, yT[p][:, :], yc[:, :], b3[:, :], ALU.mult, [yc, b3], [yT[p]])

                ckpt('rwkv')
                wb = wnext()
                for p in range(4):
                    bk = inproj_fm(wb, 128 * p, 512)
                    kb.act(qs[p][:, :], bk[:, :], AF.Copy, [bk], [qs[p]], scale=0.125)
                wdone()
                wb = wnext()
                for p in range(4):
                    bk = inproj_fm(wb, 128 * p, 512)
                    kb.act(kT[p][:, g * G:(g + 1) * G], bk[:, :], AF.Copy, [bk], [kT[p].atoms[g]])
                wdone()
                wb = wnext()
                for i in range(4):
                    bk = nbig()
                    for k in range(8):
                        kb.mm(bk[:, :], hT_h[:, k, i * 128:(i + 1) * 128], wb[:, k * 512:(k + 1) * 512],
                              [wb, hT.atoms[i]], [bk], start=(k == 0), stop=(k == 7), inc=(k == 7))
                    kb.cp(DVE, vtok[4 * g + i][:, :], bk[:, :], [bk], [vtok[4 * g + i]])
                wdone()

                ckpt('qkv')
                def bv(i):
                    return Tl(wkb[i][:, 0:512], wk[i].atoms)
                E32 = [[wk[0], wk[14]], [wk[1], wk[15]]]
                XC = [wk[16], wk[17]]
                SPb = [[bv(2), bv(3)], [bv(4), bv(5)]]
                ATb = [[bv(6), bv(7)], [bv(8), bv(9)]]
                Ssum = [bv(10), bv(11)]
                osq = bv(12)
                rst = wk[13]
                nkb = 4 * g + 4
                kbs = list(range(nkb - 1, -1, -1))
                for p in range(4):
                    Bo = bank[0]
                    Bz = [[bank[1], bank[6]], [bank[2], bank[7]]]
                    Bc = [bank[3], bank[4]]

                    def geom(idx):
                        kbk = kbs[idx]
                        dq = kbk - 4 * g
                        q0 = 128 * max(dq, 0)
                        return kbk, dq, q0, slice(q0, 512), slice(kbk * 128, (kbk + 1) * 128), kT[p].atoms[kbk // 4]

                    def zmm(idx, hh):
                        kbk, dq, q0, cs_, kblk, katoms = geom(idx)
                        rows = slice(64 * hh, 64 * hh + 64)
                        bz = Bz[hh][idx % 2]
                        kb.mm(bz[:, cs_], kT[p][rows, kblk], qs[p][rows, cs_], [katoms, qs[p]], [bz])

                    def expA(idx, hh):
                        kbk, dq, q0, cs_, kblk, katoms = geom(idx)
                        bz = Bz[hh][idx % 2]
                        e = E32[hh][idx % 2]
                        kb.act(e[:, cs_], bz[:, cs_], AF.Exp, [bz], [e])
                        if dq >= 0:
                            kb.tt(DVE, e[:, q0:q0 + 128], e[:, q0:q0 + 128], C(CS_MATT), ALU.mult, [e, cst], [e])

                    def lnA(idx, hh):
                        kbk, dq, q0, cs_, kblk, katoms = geom(idx)
                        e = E32[hh][idx % 2]
                        sp = SPb[hh][idx % 2]
                        kb.act(sp[:, cs_], e[:, cs_], AF.Ln, [e], [sp], bias=1.0)

                    def actA(idx):
                        expA(idx, 0)
                        expA(idx, 1)
                        lnA(idx, 0)
                        lnA(idx, 1)

                    def stB1(idx, hh):
                        kbk, dq, q0, cs_, kblk, katoms = geom(idx)
                        rows = slice(64 * hh, 64 * hh + 64)
                        sp = SPb[hh][idx % 2]
                        kb.mm(Bc[hh][:, cs_], C(CS_TRI), sp[:, cs_], [cst, sp], [Bc[hh]], start=True, stop=(idx == 0), inc=(idx == 0))
                        if idx > 0:
                            kb.mm(Bc[hh][:, cs_], C(CS_ONES), Ssum[hh][:, cs_], [cst, Ssum[hh]], [Bc[hh]], start=False, stop=True)
                        if kbk > 0:
                            kb.tt(DVE, Ssum[hh][:, cs_], Ssum[hh][:, cs_], sp[:, cs_], ALU.add, [Ssum[hh], sp], [Ssum[hh]])

                    def stB2a(idx, hh):
                        kbk, dq, q0, cs_, kblk, katoms = geom(idx)
                        at = ATb[hh][idx % 2]
                        xc = XC[hh]
                        kb.act(xc[:, cs_], Bc[hh][:, cs_], AF.Exp, [Bc[hh]], [xc], scale=-1.0)
                        e = E32[hh][idx % 2]
                        kb.tt(DVE, at[:, cs_], e[:, cs_], xc[:, cs_], ALU.mult, [e, xc], [at])

                    def stB2b(idx, hh):
                        kbk, dq, q0, cs_, kblk, katoms = geom(idx)
                        rows = slice(64 * hh, 64 * hh + 64)
                        at = ATb[hh][idx % 2]
                        kb.mm(Bo[rows, cs_], vtok[kbk][:, 128 * p + 64 * hh: 128 * p + 64 * hh + 64], at[:, cs_],
                              [vtok[kbk], at], [Bo], start=False, stop=(kbk == 0), inc=True)

                    for hh in range(2):
                        rows = slice(64 * hh, 64 * hh + 64)
                        kb.memset(POOL, Ssum[hh][:, :], 0.0, [Ssum[hh]])
                        kb.mm(Bo[rows, :], zt[:, :], qs[p][:, :], [zt, qs[p]], [Bo], start=True, stop=False, inc=False)
                    zmm(0, 0)
                    zmm(0, 1)
                    if nkb > 1:
                        zmm(1, 0)
                        zmm(1, 1)
                    actA(0)
                    for idx in range(nkb):
                        if idx + 2 < nkb:
                            zmm(idx + 2, 0)
                            zmm(idx + 2, 1)
                        if idx + 1 < nkb:
                            actA(idx + 1)
                        stB1(idx, 0)
                        stB1(idx, 1)
                        if idx > 0:
                            stB2b(idx - 1, 0)
                            stB2b(idx - 1, 1)
                        stB2a(idx, 0)
                        stB2a(idx, 1)
                    stB2b(nkb - 1, 0)
                    stB2b(nkb - 1, 1)
                    kb.act(osq[:, :], Bo[:, :], AF.Square, [Bo], [osq])
                    Bs = bank[5]
                    kb.mm(Bs[:, :], C(CS_BLK), osq[:, :], [cst, osq], [Bs])
                    rstd_from_ss(Bs[:, :], Bs, rst, 1.0 / 64, RMS_EPS)
                    kb.stt(DVE, yT[4 + p][:, :], Bo[:, :], pc(PC_SBG + p), rst[:, :], ALU.mult, ALU.mult, [Bo, rst, pcol], [yT[4 + p]])

                ckpt('attn')
                if dbg:
                    for k in range(8):
                        t = wk[14 + (k % 2)]
                        kb.cp(DVE, t[:, :], yT[k][:, :], [yT[k]], [t])
                        kb.dma(dbg_y[k * 128:(k + 1) * 128, tok0:tok0 + G], t[:, :], [t], [], t)

                ckpt('dbgy')
                for i in range(4):
                    xt = xb[i % 2]
                    kb.dma(xt[:, :], x_d[tok0 + i * 128: tok0 + (i + 1) * 128, :], [], [xt], xt)
                    for hf in range(2):
                        bk = nbig()
                        for k in range(8):
                            kb.mm(bk[:, :], yT[k][:, i * 128:(i + 1) * 128], wout[:, k, hf * 512:(hf + 1) * 512],
                                  [yT[k], wout], [bk], start=(k == 0), stop=(k == 7), inc=(k == 7))
                        kb.tt(DVE, x1[i][:, hf * 512:(hf + 1) * 512], bk[:, :], xt[:, hf * 512:(hf + 1) * 512], ALU.add,
                              [bk, xt], [x1[i]])
                    if dbg:
                        kb.dma(dbg_x1[tok0 + i * 128: tok0 + (i + 1) * 128, :], x1[i][:, :], [x1[i]], [], x1[i])

                ckpt('outproj')
                for i in range(4):
                    ln_T(x1[i], i, i)
                for c in range(8):
                    wb = wnext()
                    for fi in range(4):
                        f = 4 * c + fi
                        bk = nbig()
                        for k in range(8):
                            kb.mm(bk[:, :], wb[:, k * 512 + fi * 128: k * 512 + (fi + 1) * 128], hT_h[:, k, :],
                                  [wb, hT], [bk], start=(k == 0), stop=(k == 7), inc=(k == 7))
                        r32 = wk[16 + (f % 2)]
                        kb.act(r32[:, :], bk[:, :], AF.Relu, [bk], [r32])
                        ut = wkb[f // 2]
                        kb.tt(POOL if (f % 2 == 0) else DVE, ut[:, (f % 2) * 512:(f % 2 + 1) * 512], r32[:, :], r32[:, :], ALU.mult,
                              [r32], [ut])
                    wdone()
                for c in range(8):
                    wb = wnext()
                    for fi in range(4):
                        f = 4 * c + fi
                        ut = wkb[f // 2]
                        for i in range(4):
                            for hf in range(2):
                                bk = bank[2 * i + hf]
                                kb.mm(bk[:, :], ut[:, (f % 2) * 512 + i * 128:(f % 2) * 512 + (i + 1) * 128],
                                      wb[:, fi * 1024 + hf * 512: fi * 1024 + (hf + 1) * 512],
                                      [ut, wb], [bk], start=(f == 0), stop=(f == 31), inc=(f == 31 or (fi == 3 and i == 3 and hf == 1)))
                    wdone()
                for i in range(4):
                    for hf in range(2):
                        bk = bank[2 * i + hf]
                        kb.tt(DVE, x1[i][:, hf * 512:(hf + 1) * 512], bk[:, :], x1[i][:, hf * 512:(hf + 1) * 512], ALU.add,
                              [bk, x1[i]], [x1[i]])
                    s = stat[i % 2]
                    h = hn[i % 2]
                    kb.memset(POOL, s[:, 0:1], 0.0, [s])
                    kb.act(h[:, :], x1[i][:, :], AF.Square, [x1[i], s], [h, s], accum_out=s[:, 0:1])
                    kb.act(s[:, 1:2], s[:, 0:1], AF.Ln, [s], [s], scale=1.0 / D, bias=RMS_EPS)
                    kb.act(s[:, 2:3], s[:, 1:2], AF.Exp, [s], [s], scale=-0.5)
                    kb.stt(DVE, x1[i][:, :], x1[i][:, :], s[:, 2:3], lnf[:, :], ALU.mult, ALU.mult, [x1[i], s, lnf], [x1[i]])
                    kb.dma(out_d[tok0 + i * 128: tok0 + (i + 1) * 128, :], x1[i][:, :], [x1[i]], [], x1[i])


def _consts():
    c = np.zeros((NCS, 128, 128), np.float32)
    i = np.arange(128)
    c[CS_ID] = np.eye(128)
    c[CS_TRI] = (i[:, None] >= i[None, :])
    c[CS_ONES] = 1.0
    c[CS_MATT] = (i[:, None] < i[None, :])
    blk = (i[:, None] // 64) == (i[None, :] // 64)
    c[CS_BLK] = blk
    c[CS_UTS] = blk & (i[:, None] < i[None, :])
    c[CS_LTS] = blk & (i[:, None] > i[None, :])
    c[CS_UTI] = blk & (i[:, None] <= i[None, :])
    return np.ascontiguousarray(c.transpose(1, 0, 2).reshape(128, NCS * 128))


def _host_inputs(inp):
    f = lambda a: np.asarray(a, np.float32)
    cols = []
    cols.append(f(inp["ln1_g"])[0].reshape(8, 128))
    cols.append(f(inp["ln2_g"])[0].reshape(8, 128))
    cols.append(f(inp["tok_mu"])[0].reshape(14, 128))
    for k in ("w0", "a0", "k_k", "k_a"):
        cols.append(f(inp[k])[0].reshape(4, 128))
    cols.append(f(inp["r_k"])[0].reshape(4, 128))
    for k in ("gn_w", "gn_b", "sb_gain"):
        cols.append(f(inp[k])[0].reshape(4, 128))
    pcol = np.ascontiguousarray(np.concatenate(cols, axis=0).T)
    assert pcol.shape == (128, NPC)
    wlo = np.ascontiguousarray(np.concatenate([f(inp["w_decay_up"])[0], f(inp["w_aaa_up"])[0]], axis=0))
    shared = {
        "w_in": np.ascontiguousarray(f(inp["w_in"])[0]),
        "w_out": np.ascontiguousarray(f(inp["w_out"])[0]),
        "w_up": np.ascontiguousarray(f(inp["w_up"])[0]),
        "w_down": np.ascontiguousarray(f(inp["w_down"])[0]),
        "wlo": wlo,
        "wg": np.ascontiguousarray(f(inp["w_gate_up"])[0]),
        "pcol": pcol,
        "lnf": np.ascontiguousarray(f(inp["lnf_g"]).reshape(1, D)),
        "cst": _consts(),
    }
    return shared


def kernel(**inputs):
    x = np.asarray(inputs["x"], np.float32)
    B = x.shape[0]
    nseq = B // NCORES
    shared = _host_inputs(inputs)
    nc = build(nseq=nseq)
    in_maps = []
    for c in range(NCORES):
        m = dict(shared)
        m["x"] = np.ascontiguousarray(x[c * nseq:(c + 1) * nseq].reshape(nseq * T, D))
        in_maps.append(m)
    res = run_bass_kernel_spmd(nc, in_maps, core_ids=list(range(NCORES)))
    outs = [np.asarray(r["out"]).reshape(nseq, T, D) for r in res.results]
    return np.concatenate(outs, axis=0).astype(np.float32)
```
